# Optimizing a Trainium2 kernel written in Bass

```python
import math
import jax, jax.numpy as jnp
from jax import lax
import numpy as np


D_MODEL = 1024
BATCH = 8
SEQ = 4096
DEPTH = 4

CTX_LEN = 256
GRID_W = 64
EPS = 1e-6

DA_HEADS = 4
DA_WIDTH = D_MODEL // 2
DA_HEAD_DIM = DA_WIDTH // (2 * DA_HEADS)
ROPE_BASE = 10000.0
Q_BLOCK = 128

CM_GROUPS = 4
CM_WIDTH = D_MODEL // 4
CM_GROUP_DIM = CM_WIDTH // CM_GROUPS
CHUNK = 128

POOL_WINDOWS = (2, 4, 8, 16)
PL_WIDTH = D_MODEL // 4
PL_GROUP_DIM = PL_WIDTH // len(POOL_WINDOWS)

MIX_WIDTH = DA_WIDTH + CM_WIDTH + PL_WIDTH
IN_WIDTH = 3 * DA_WIDTH + 2 * CM_WIDTH + PL_WIDTH
D_FF = 4 * D_MODEL

kernel_name = 'hybrid_diffattn_gmlp_pool_dit_trunk'


def rmsnorm(x, g):
    xf = x.astype(jnp.float32)
    y = xf * lax.rsqrt(jnp.mean(xf * xf, axis=-1, keepdims=True) + EPS)
    return (y * g.astype(jnp.float32)).astype(x.dtype)


def modulation(cond, w_ada, b_ada):
    m = jax.nn.silu(cond) @ w_ada + b_ada
    return jnp.split(m, 6, axis=-1)


def modulate(h, shift, scale):
    return h * (1 + scale) + shift


def axial_rope_tables(n_tokens):
    n_rows = n_tokens // GRID_W
    row = jnp.repeat(jnp.arange(n_rows), GRID_W).astype(jnp.float32)
    col = jnp.tile(jnp.arange(GRID_W), n_rows).astype(jnp.float32)
    n_freq = DA_HEAD_DIM // 4
    inv = ROPE_BASE ** (-jnp.arange(n_freq, dtype=jnp.float32) / n_freq)
    ang = jnp.stack([row[:, None] * inv, col[:, None] * inv], axis=1)
    return jnp.cos(ang), jnp.sin(ang)


def apply_axial_rope(x, cos, sin):
    n_freq = x.shape[-1] // 4
    xs = x.astype(jnp.float32).reshape(x.shape[:-1] + (2, 2, n_freq))
    x1 = xs[..., 0, :]
    x2 = xs[..., 1, :]
    c = cos[None, :, None, None]
    s = sin[None, :, None, None]
    out = jnp.stack([x1 * c - x2 * s, x2 * c + x1 * s], axis=-2)
    return out.reshape(x.shape).astype(x.dtype)


def diff_attention(q, k, v, lam):
    scale = DA_HEAD_DIM ** -0.5
    s = jnp.einsum('bqhjd,bkhjd->bhjqk', q, k).astype(jnp.float32) * scale
    p = jax.nn.softmax(s, axis=-1)
    a = p[:, :, 0] - lam * p[:, :, 1]
    return jnp.einsum('bhqk,bkhe->bqhe', a.astype(v.dtype), v)


def latent_diff_attention(ql, kl, vl, kc, vc, lam):
    b, t, h, _, d = ql.shape
    k = jnp.concatenate([kc, kl], axis=1)
    v = jnp.concatenate([vc, vl], axis=1)
    nb = t // Q_BLOCK
    qb = jnp.moveaxis(ql.reshape(b, nb, Q_BLOCK, h, 2, d), 1, 0)
    out = lax.map(lambda qblk: diff_attention(qblk, k, v, lam), qb)
    return jnp.moveaxis(out, 0, 1).reshape(b, t, h, 2 * d)


def chunk_spatial_gating(u, v, g_v, w_s, b_s):
    b, t, _ = v.shape
    n = t // CHUNK
    vn = rmsnorm(v.reshape(b, t, CM_GROUPS, CM_GROUP_DIM), g_v.reshape(CM_GROUPS, CM_GROUP_DIM))
    vn = vn.reshape(b, n, CHUNK, CM_GROUPS, CM_GROUP_DIM)
    vm = jnp.einsum('gpq,bnqgc->bnpgc', w_s, vn) + b_s.T[:, :, None]
    return u * vm.reshape(b, t, CM_WIDTH)


def multiscale_pool(x, w_pool, s_pool):
    b, t, _ = x.shape
    xg = x.reshape(b, t, len(POOL_WINDOWS), PL_GROUP_DIM)
    pos = jnp.arange(t)
    outs = []
    for gi, w in enumerate(POOL_WINDOWS):
        xs = xg[:, :, gi].astype(jnp.float32)
        cs = jnp.concatenate([jnp.zeros_like(xs[:, :1]), jnp.cumsum(xs, axis=1)], axis=1)
        lo = jnp.clip(pos - w // 2, 0, t)
        hi = jnp.clip(pos + (w - w // 2), 0, t)
        mean = (cs[:, hi] - cs[:, lo]) / (hi - lo).astype(jnp.float32)[None, :, None]
        outs.append(mean - xs)
    d = jnp.stack(outs, axis=2).astype(x.dtype)
    y = jnp.einsum('btgc,gce->btge', d, w_pool).reshape(b, t, PL_WIDTH)
    return y * s_pool


def split_projection(h, w_in):
    z = h @ w_in
    b, t, _ = z.shape
    cuts = np.cumsum([DA_WIDTH, DA_WIDTH, DA_WIDTH, CM_WIDTH, CM_WIDTH])
    q, k, v, u, gv, p = jnp.split(z, [int(i) for i in cuts], axis=-1)
    q = q.reshape(b, t, DA_HEADS, 2, DA_HEAD_DIM)
    k = k.reshape(b, t, DA_HEADS, 2, DA_HEAD_DIM)
    v = v.reshape(b, t, DA_HEADS, 2 * DA_HEAD_DIM)
    return q, k, v, u, gv, p


def merge_heads(a, g_sub, lam_init, u, gv, p, g_v, w_s, b_s, w_pool, s_pool, w_out):
    b, t = a.shape[:2]
    a = (rmsnorm(a, g_sub) * (1 - lam_init)).reshape(b, t, DA_WIDTH)
    m_b = chunk_spatial_gating(u, gv, g_v, w_s, b_s)
    m_c = multiscale_pool(p, w_pool, s_pool)
    return jnp.concatenate([a, m_b, m_c], axis=-1) @ w_out


def sq_relu_mlp(h, w1, w2):
    return jnp.square(jax.nn.relu(h @ w1)) @ w2


def setup_inputs(seed: int = 0) -> dict:
    key = jax.random.key(seed)
    ks = jax.random.split(key, 24)
    f32 = jnp.float32
    nrm = lambda k, shape: jax.random.normal(k, shape, f32)
    L = DEPTH
    return {
        'x': nrm(ks[0], (BATCH, SEQ, D_MODEL)),
        'c': nrm(ks[1], (BATCH, D_MODEL)),
        'ctx': nrm(ks[2], (BATCH, CTX_LEN, D_MODEL)),
        'c_ctx': nrm(ks[3], (D_MODEL,)),
        'w_ada': nrm(ks[4], (L, D_MODEL, 6 * D_MODEL)) * (0.5 * D_MODEL ** -0.5),
        'b_ada': nrm(ks[5], (L, 6 * D_MODEL)) * 0.02,
        'g_norm_mix': 1.0 + 0.02 * nrm(ks[6], (L, D_MODEL)),
        'g_norm_mlp': 1.0 + 0.02 * nrm(ks[7], (L, D_MODEL)),
        'w_in': nrm(ks[8], (L, D_MODEL, IN_WIDTH)) * D_MODEL ** -0.5,
        'lam_q1': 0.1 * nrm(ks[9], (L, DA_HEAD_DIM)),
        'lam_k1': 0.1 * nrm(ks[10], (L, DA_HEAD_DIM)),
        'lam_q2': 0.1 * nrm(ks[11], (L, DA_HEAD_DIM)),
        'lam_k2': 0.1 * nrm(ks[12], (L, DA_HEAD_DIM)),
        'g_subln': 1.0 + 0.02 * nrm(ks[13], (L, 2 * DA_HEAD_DIM)),
        'g_vnorm': 1.0 + 0.02 * nrm(ks[14], (L, CM_WIDTH)),
        'w_spatial': nrm(ks[15], (L, CM_GROUPS, CHUNK, CHUNK)) * CHUNK ** -0.5,
        'b_spatial': 1.0 + 0.02 * nrm(ks[16], (L, CM_GROUPS, CHUNK)),
        'w_pool': nrm(ks[17], (L, len(POOL_WINDOWS), PL_GROUP_DIM, PL_GROUP_DIM)) * PL_GROUP_DIM ** -0.5,
        's_pool': 1.0 + 0.02 * nrm(ks[18], (L, PL_WIDTH)),
        'w_out': nrm(ks[19], (L, MIX_WIDTH, D_MODEL)) * MIX_WIDTH ** -0.5,
        'w1': nrm(ks[20], (L, D_MODEL, D_FF)) * D_MODEL ** -0.5,
        'w2': nrm(ks[21], (L, D_FF, D_MODEL)) * D_FF ** -0.5,
        'g_final': 1.0 + 0.02 * nrm(ks[22], (D_MODEL,)),
    }


def reference(x, c, ctx, c_ctx, w_ada, b_ada, g_norm_mix, g_norm_mlp, w_in,
              lam_q1, lam_k1, lam_q2, lam_k2, g_subln, g_vnorm, w_spatial, b_spatial,
              w_pool, s_pool, w_out, w1, w2, g_final):
    xl = x
    xc = ctx
    cos, sin = axial_rope_tables(xl.shape[1])
    for l in range(DEPTH):
        last = l == DEPTH - 1
        lam_init = 0.8 - 0.6 * math.exp(-0.3 * l)
        lam = (jnp.exp(jnp.sum(lam_q1[l].astype(jnp.float32) * lam_k1[l].astype(jnp.float32)))
               - jnp.exp(jnp.sum(lam_q2[l].astype(jnp.float32) * lam_k2[l].astype(jnp.float32)))
               + lam_init)
        sh1, sc1, gt1, sh2, sc2, gt2 = [m[:, None, :] for m in modulation(c, w_ada[l], b_ada[l])]
        csh1, csc1, cgt1, csh2, csc2, cgt2 = modulation(c_ctx, w_ada[l], b_ada[l])

        hl = modulate(rmsnorm(xl, g_norm_mix[l]), sh1, sc1)
        hc = modulate(rmsnorm(xc, g_norm_mix[l]), csh1, csc1)
        ql, kl, vl, ul, gvl, pl = split_projection(hl, w_in[l])
        qc, kc, vc, uc, gvc, pc = split_projection(hc, w_in[l])
        ql = apply_axial_rope(ql, cos, sin)
        kl = apply_axial_rope(kl, cos, sin)
        al = latent_diff_attention(ql, kl, vl, kc, vc, lam)
        mix_l = merge_heads(al, g_subln[l], lam_init, ul, gvl, pl, g_vnorm[l], w_spatial[l],
                            b_spatial[l], w_pool[l], s_pool[l], w_out[l])
        xl = xl + gt1 * mix_l
        if not last:
            ac = diff_attention(qc, kc, vc, lam)
            mix_c = merge_heads(ac, g_subln[l], lam_init, uc, gvc, pc, g_vnorm[l], w_spatial[l],
                                b_spatial[l], w_pool[l], s_pool[l], w_out[l])
            xc = xc + cgt1 * mix_c

        hl = modulate(rmsnorm(xl, g_norm_mlp[l]), sh2, sc2)
        xl = xl + gt2 * sq_relu_mlp(hl, w1[l], w2[l])
        if not last:
            hc = modulate(rmsnorm(xc, g_norm_mlp[l]), csh2, csc2)
            xc = xc + cgt2 * sq_relu_mlp(hc, w1[l], w2[l])
    return rmsnorm(xl, g_final)
```

```python
import math
from contextlib import ExitStack
import numpy as np
import concourse.bass as bass
import concourse.mybir as mybir
from concourse.bass_utils import run_bass_kernel_spmd

F32 = mybir.dt.float32
BF16 = mybir.dt.bfloat16
AF = mybir.ActivationFunctionType
ALU = mybir.AluOpType
AX = mybir.AxisListType

D = 1024
T = 4096
TC = 256
TT = T + TC
L = 4
NKT = TT // 128
EPS = 1e-6
IN_W = 2304
PADL = 4384
SEM_TICK_LIMIT = 30000


class Buf:
    __slots__ = ("name", "last_w", "readers", "excl")

    def __init__(self, name="", excl=False):
        self.name = name
        self.last_w = None
        self.readers = []
        self.excl = excl


class Op:
    __slots__ = ("eng", "fn", "deps", "needs_inc", "sem", "tick", "is_dma", "dma_key")

    def __init__(self, eng, fn, is_dma=False, dma_key=None):
        self.eng = eng
        self.fn = fn
        self.deps = []
        self.needs_inc = False
        self.sem = None
        self.tick = 0
        self.is_dma = is_dma
        self.dma_key = dma_key


class Prog:
    ENGS = ("pe", "act", "dve", "pool", "sp")

    def __init__(self, nc):
        self.nc = nc
        self.q = {e: [] for e in self.ENGS}
        self.out_dmas = []
        self.pending_dmas = []
        self.dma_slots = []
        self.dma_active = {}
        self.dma_free = []

    muted = False
    limit = 10 ** 12
    count = 0

    def _add(self, op, reads, writes):
        Prog.count += 1
        if Prog.muted or Prog.count > Prog.limit:
            return op
        ex = [b for b in reads if b.excl]
        if ex:
            reads = [b for b in reads if not b.excl]
            writes = list(writes) + ex
        deps = set()
        for b in reads:
            if b.last_w is not None:
                deps.add(b.last_w)
        for b in writes:
            if b.last_w is not None:
                deps.add(b.last_w)
            for r in b.readers:
                deps.add(r)
        deps.discard(op)
        for d in deps:
            if d.eng == "pe" and op.eng == "pe" and not d.is_dma and not op.is_dma:
                continue
            d.needs_inc = True
            op.deps.append(d)
        for b in reads:
            b.readers.append(op)
        for b in writes:
            b.last_w = op
            b.readers = []
        self.q[op.eng].append(op)
        return op

    def op(self, eng, fn, reads=(), writes=()):
        return self._add(Op(eng, fn), reads, writes)

    def dma(self, out_ap, in_ap, reads=(), writes=(), key=None, eng="sp", is_output=False):
        def fn(e):
            return e.dma_start(out=out_ap, in_=in_ap)
        if key is None:
            key = writes[0] if writes else reads[0]
        o = Op(eng, fn, is_dma=True, dma_key=key)
        self._add(o, reads, writes)
        o.needs_inc = True
        if Prog.muted or Prog.count > Prog.limit:
            return o
        k = id(key)
        if k not in self.dma_active:
            if self.dma_free:
                self.dma_active[k] = self.dma_free.pop()
            else:
                self.dma_slots.append(0)
                self.dma_active[k] = len(self.dma_slots) - 1
        slot = self.dma_active[k]
        self.dma_slots[slot] += 16
        o.sem, o.tick = slot, self.dma_slots[slot]
        self.pending_dmas.append(o)
        if is_output:
            self.out_dmas.append(o)
        return o

    def barrier(self):
        if Prog.muted:
            return
        marks = []
        for e in self.ENGS:
            last = None
            for o in reversed(self.q[e]):
                if not o.is_dma and o.fn is not None:
                    last = o
                    break
            if last is not None:
                last.needs_inc = True
                marks.append(last)
        marks += self.pending_dmas
        self.pending_dmas = []
        for k, slot in self.dma_active.items():
            if self.dma_slots[slot] < 40000:
                self.dma_free.append(slot)
        self.dma_active = {}
        for e in self.ENGS:
            o = Op(e, None)
            o.deps = list(marks)
            self.q[e].append(o)

    def emit(self, stack):
        nc = self.nc

        def new_sem(name):
            return stack.enter_context(nc.semaphore(name))

        for e in self.ENGS:
            cur, cnt, n = None, 0, 0
            for o in self.q[e]:
                if o.is_dma or not o.needs_inc:
                    continue
                if cur is None or cnt >= SEM_TICK_LIMIT:
                    cur = new_sem(f"s_{e}_{n}")
                    n += 1
                    cnt = 0
                cnt += 1
                o.sem, o.tick = cur, cnt
        slot_sems = [new_sem(f"s_dma_{i}") for i in range(len(self.dma_slots))]
        for e in self.ENGS:
            for o in self.q[e]:
                if o.is_dma:
                    o.sem = slot_sems[o.sem]
        engmap = {"pe": "tensor", "act": "scalar", "dve": "vector", "pool": "gpsimd", "sp": "sync"}
        block = stack.enter_context(nc.Block())
        for e in self.ENGS:
            def body(eng, ops=self.q[e], e=e):
                waited = {}

                def wait(d):
                    k = id(d.sem)
                    if waited.get(k, 0) >= d.tick:
                        return
                    waited[k] = d.tick
                    eng.wait_ge(d.sem, d.tick)
                for o in ops:
                    for d in o.deps:
                        wait(d)
                    if o.fn is None:
                        continue
                    ins = o.fn(eng)
                    if o.needs_inc:
                        ins.then_inc(o.sem, 16 if o.is_dma else 1)
                if e == "sp":
                    for d in self.out_dmas:
                        wait(d)
            getattr(block, engmap[e])(body)


class _Skip:
    cur = None

    def __enter__(self):
        self.st = ExitStack()
        Prog.muted = True
        return self.st

    def __exit__(self, *a):
        Prog.muted = False
        self.st.close()
        return False


class Rot:
    def __init__(self, items):
        self.items = items
        self.i = 0

    def next(self):
        it = self.items[self.i % len(self.items)]
        self.i += 1
        return it


def lam_init(l):
    return 0.8 - 0.6 * math.exp(-0.3 * l)


def build_program(n_layers=L, dbg=False, stop_step=10 ** 9):
    nc = bass.Bass("TRN2", target_bir_lowering=False)
    P = Prog(nc)
    top = ExitStack()

    def din(name, shape, dt=F32):
        return nc.dram_tensor(name, list(shape), dt, kind="ExternalInput").ap()

    def dscr(name, shape, dt):
        kind = "ExternalOutput" if dbg else "Internal"
        return nc.dram_tensor(name, list(shape), dt, kind=kind).ap()

    xT_in = din("xT", [D, T])
    ctxT_in = din("ctxT", [D, TC])
    cb_in = din("cb", [128, 2, D])
    wadaT_in = din("wadaT", [L, 128, 48, D])
    bada_in = din("bada", [128, L, 48])
    gm_in = din("gm", [128, L, 8])
    gl_in = din("gl", [128, L, 8])
    gf_in = din("gf", [128, 8])
    win_in = din("w_in", [L, D, IN_W])
    lamv_in = din("lamv", [128, L, 4, 64])
    gsub_in = din("gsub", [128, L, 128])
    gvn_in = din("gvn", [128, L, 2])
    wsT_in = din("wsT", [L, 128, 4, 128])
    bsT_in = din("bsT", [L, 128, 2, 512])
    wpool_in = din("w_pool", [L, 4, 64, 64])
    spool_in = din("spool", [128, L, 2])
    wout_in = din("w_out", [L, D, D])
    w1_in = din("w1", [L, D, 4 * D])
    w2_in = din("w2", [L, 4 * D, D])
    cosT_in = din("cosT", [128, T])
    sinT_in = din("sinT", [128, T])
    rsw_in = din("rsw", [128, 128])
    ident_in = din("ident", [128, 128])
    ic_in = din("icT", [128, 2, PADL])
    yT_out = nc.dram_tensor("yT", [D, T], F32, kind="ExternalOutput").ap()

    XT = dscr("XT", [8, 128, TT], F32)
    QT = dscr("QT", [4, 128, TT], BF16)
    KT = dscr("KT", [4, 128, TT], BF16)
    VV = dscr("VV", [TT, 4, 130], BF16)
    PT = dscr("PT", [2, 128, TT], F32)
    MIX = dscr("MIX", [8, 128, TT], BF16)
    SCB = dscr("SCB", [128, 2, D], F32)

    uid = [0]

    def sbt(st, name, shape, dt=F32):
        uid[0] += 1
        t = st.enter_context(nc.sbuf_tensor(f"sb{uid[0]}_{name}", list(shape), dt))
        return t, Buf(name)

    def pst(st, name, shape, dt=F32):
        uid[0] += 1
        t = st.enter_context(nc.psum_tensor(f"ps{uid[0]}_{name}", list(shape), dt))
        return t, Buf(name, excl=True)

    def act(out, in_, func, reads, writes, **kw):
        P.op("act", lambda e: e.activation(out=out, in_=in_, func=func, **kw), reads, writes)

    def tt(out, in0, in1, op, reads, writes, eng="dve"):
        P.op(eng, lambda e: e.tensor_tensor(out=out, in0=in0, in1=in1, op=op), reads, writes)

    def ts(out, in0, s1, s2, op0, op1, reads, writes, eng="dve"):
        if s2 is None:
            P.op(eng, lambda e: e.tensor_scalar(out=out, in0=in0, scalar1=s1, scalar2=None, op0=op0), reads, writes)
        else:
            P.op(eng, lambda e: e.tensor_scalar(out=out, in0=in0, scalar1=s1, scalar2=s2, op0=op0, op1=op1), reads, writes)

    def stt(out, in0, scalar, in1, op0, op1, reads, writes, eng="dve"):
        P.op(eng, lambda e: e.scalar_tensor_tensor(out=out, in0=in0, scalar=scalar, in1=in1, op0=op0, op1=op1), reads, writes)

    def red(out, in_, reads, writes, eng="dve"):
        P.op(eng, lambda e: e.tensor_reduce(out=out, in_=in_, axis=AX.X, op=ALU.add), reads, writes)

    def cpy(out, in_, reads, writes, eng="dve"):
        if eng == "act":
            act(out, in_, AF.Copy, reads, writes)
        else:
            P.op(eng, lambda e: e.tensor_copy(out=out, in_=in_), reads, writes)

    def memset(ap, val, writes, eng="dve"):
        P.op(eng, lambda e: e.memset(ap, val), (), writes)

    def mm(out, lhsT, rhs, start, stop, reads, writes, skip=False):
        if skip:
            P.op("pe", lambda e: e.matmul(out, lhsT=lhsT, rhs=rhs, start=start, stop=stop, skip_group_check=True), reads, writes)
        else:
            P.op("pe", lambda e: e.matmul(out, lhsT=lhsT, rhs=rhs, start=start, stop=stop), reads, writes)

    def transpose(out, in_, ident, reads, writes):
        P.op("pe", lambda e: e.transpose(out=out, in_=in_, identity=ident), reads, writes)

    def rsqrt_chain(out, in_, scale, lnbuf, lnB, reads, outB):
        act(lnbuf, in_, AF.Ln, list(reads) + [B_eps], [lnB], scale=scale, bias=epsb[0:lnbuf.shape[0], 0:1])
        act(out, lnbuf, AF.Exp, [lnB], [outB], scale=-0.5)

    cvt_i = [0]

    def convert(out, in_, reads, writes, engs=("dve", "act")):
        e = engs[cvt_i[0] % len(engs)]
        cvt_i[0] += 1
        cpy(out, in_, reads, writes, eng=e)

    ident_bf, B_ident = sbt(top, "ident_bf", [128, 128], BF16)
    ones_bf, B_ones = sbt(top, "ones_bf", [128, 128], BF16)
    rsw_bf, B_rsw = sbt(top, "rsw_bf", [128, 128], BF16)
    gm, B_gm = sbt(top, "gm", [128, L, 8])
    gl, B_gl = sbt(top, "gl", [128, L, 8])
    gf, B_gf = sbt(top, "gf", [128, 8])
    spool, B_spool = sbt(top, "spool", [128, L, 2])
    gvn, B_gvn = sbt(top, "gvn", [128, L, 2])
    GS, B_GS = sbt(top, "GS", [128, L, 128])
    lam, B_lam = sbt(top, "lam", [128, L])
    nlam, B_nlam = sbt(top, "nlam", [128, L])
    mod, B_mod = sbt(top, "mod", [128, L, 48, 2])
    gs1, B_gs1 = sbt(top, "gs1", [128, L, 8, 2])
    gs2, B_gs2 = sbt(top, "gs2", [128, L, 8, 2])
    epsb, B_eps = sbt(top, "epsb", [128, 1])

    with ExitStack() as st:
        stg, B_stg = sbt(st, "pro_stg", [128, 2, 128])
        P.dma(stg[:, 0, :], ident_in, writes=[B_stg])
        P.dma(stg[:, 1, :], rsw_in, writes=[B_stg])
        cpy(ident_bf[:], stg[:, 0, :], [B_stg], [B_ident])
        cpy(rsw_bf[:], stg[:, 1, :], [B_stg], [B_rsw])
        memset(ones_bf[:], 1.0, [B_ones])
        memset(epsb[:], EPS, [B_eps])
        P.dma(gm[:], gm_in, writes=[B_gm])
        P.dma(gl[:], gl_in, writes=[B_gl])
        P.dma(gf[:], gf_in, writes=[B_gf])
        P.dma(spool[:], spool_in, writes=[B_spool])
        P.dma(gvn[:], gvn_in, writes=[B_gvn])
        bada, B_bada = sbt(st, "bada", [128, L, 48])
        P.dma(bada[:], bada_in, writes=[B_bada])
        gsub, B_gsub = sbt(st, "gsub", [128, L, 128])
        P.dma(gsub[:], gsub_in, writes=[B_gsub])
        lamv, B_lamv = sbt(st, "lamv", [128, L, 4, 64])
        P.dma(lamv[:], lamv_in, writes=[B_lamv])
        cbt, B_cb = sbt(st, "cbt", [128, 2, D])
        P.dma(cbt[:], cb_in, writes=[B_cb])
        scb, B_scb = sbt(st, "scb", [128, 2, D])
        act(scb[:], cbt[:], AF.Silu, [B_cb], [B_scb])
        lprod, B_lprod = sbt(st, "lprod", [128, L, 2, 64])
        lsum, B_lsum = sbt(st, "lsum", [128, L, 2])
        lexp, B_lexp = sbt(st, "lexp", [128, L, 2])
        for l in range(L):
            for j in range(2):
                tt(lprod[:, l, j, :], lamv[:, l, 2 * j, :], lamv[:, l, 2 * j + 1, :], ALU.mult, [B_lamv], [B_lprod])
        red(lsum[:], lprod[:], [B_lprod], [B_lsum])
        act(lexp[:], lsum[:], AF.Exp, [B_lsum], [B_lexp])
        for l in range(L):
            tt(lam[:, l:l + 1], lexp[:, l, 0:1], lexp[:, l, 1:2], ALU.subtract, [B_lexp], [B_lam])
            ts(lam[:, l:l + 1], lam[:, l:l + 1], float(lam_init(l)), None, ALU.add, None, [B_lam], [B_lam])
            ts(GS[:, l, :], gsub[:, l, :], float(1.0 - lam_init(l)), None, ALU.mult, None, [B_gsub], [B_GS])
        ts(nlam[:], lam[:], -1.0, None, ALU.mult, None, [B_lam], [B_nlam])
        wst = [sbt(st, f"wst{i}", [128, D]) for i in range(4)]
        wrot = Rot(wst)
        prod, B_prod = sbt(st, "prod", [128, 2, D])
        P.dma(SCB, scb[:], reads=[B_scb])

        def mod_chunk(l, j, scb_, B_scb_, wrot_, prod_, B_prod_, redt_, B_redt_):
            w_t, w_b = wrot_.next()
            P.dma(w_t[:], wadaT_in[l, :, j, :], writes=[w_b], eng="pool")
            if prod_.shape[1] == 2:
                tt(prod_[:], scb_[:], w_t[:].unsqueeze(1).to_broadcast([128, 2, D]), ALU.mult, [B_scb_, w_b], [B_prod_])
                red(redt_[:, j, :], prod_[:], [B_prod_], [B_redt_])
            else:
                for c_ in range(2):
                    tt(prod_[:, 0, :], scb_[:, c_, :], w_t[:], ALU.mult, [B_scb_, w_b], [B_prod_])
                    red(redt_[:, j, c_:c_ + 1], prod_[:, 0, :], [B_prod_], [B_redt_])

        def mod_finish(l, redt_, B_redt_, bada_, B_bada_):
            tt(mod[:, l], redt_[:], bada_[:, l, :].unsqueeze(2).to_broadcast([128, 48, 2]), ALU.add,
               [B_redt_, B_bada_], [B_mod])
            stt(gs1[:, l], mod[:, l, 8:16, :], 1.0, gm[:, l, :].unsqueeze(2).to_broadcast([128, 8, 2]), ALU.add, ALU.mult,
                [B_mod, B_gm], [B_gs1])
            stt(gs2[:, l], mod[:, l, 32:40, :], 1.0, gl[:, l, :].unsqueeze(2).to_broadcast([128, 8, 2]), ALU.add, ALU.mult,
                [B_mod, B_gl], [B_gs2])

        redt0, B_redt0 = sbt(st, "redt0", [128, 48, 2])
        memset(redt0[:], 0.0, [B_redt0])
        prodB = sbt(st, "prodB", [128, 2, D])
        prods = Rot([(prod, B_prod), prodB])
        junk, B_junk = sbt(st, "junk", [128, D])
        for j in range(48):
            w_t, w_b = wrot.next()
            P.dma(w_t[:], wadaT_in[0, :, j, :], writes=[w_b], eng="pool")
            pr_, B_pr_ = prods.next()
            tt(pr_[:], scb[:], w_t[:].unsqueeze(1).to_broadcast([128, 2, D]), ALU.mult, [B_scb, w_b], [B_pr_])
            for c_ in range(2):
                act(junk[:], pr_[:, c_, :], AF.Copy, [B_pr_], [B_junk, B_redt0], accum_out=redt0[:, j, c_:c_ + 1])
        mod_finish(0, redt0, B_redt0, bada, B_bada)
        P.barrier()
        if dbg:
            print('ops@pro', Prog.count)

    def x_src(l, tok0, nt):
        if l == 0:
            if tok0 < T:
                return xT_in.rearrange("(k p) t -> p k t", p=128)[:, :, tok0:tok0 + nt]
            return ctxT_in.rearrange("(k p) t -> p k t", p=128)[:, :, tok0 - T:tok0 - T + nt]
        return XT.rearrange("k p t -> p k t")[:, :, tok0:tok0 + nt]

    def x_dst(tok0, nt):
        return XT.rearrange("k p t -> p k t")[:, :, tok0:tok0 + nt]

    def norm_mod(st_bufs, xs, B_xs, nt, gsv, shv, hb, B_hb, pss, B_pss, do_square=True):
        sq, B_sq, lnb, B_ln, rstd, B_rstd, tmps = st_bufs
        if do_square:
            act(sq[:, :, :nt], xs[:, :, :nt], AF.Square, [B_xs], [B_sq])
        for k in range(8):
            mm(pss[:, :nt], ones_bf[:], sq[:, k, :nt], k == 0, k == 7, [B_ones, B_sq], [B_pss])
        rsqrt_chain(rstd[:, :nt], pss[:, :nt], 1.0 / D, lnb[:, :nt], B_ln, [B_pss], B_rstd)
        for k in range(8):
            tm, B_tm = tmps.next()
            stt(tm[:, :nt], xs[:, k, :nt], gsv(k), rstd[:, :nt], ALU.mult, ALU.mult, [B_xs, B_rstd, B_gs1, B_gs2], [B_tm])
            act(hb[:, k, :nt], tm[:, :nt], AF.Identity, [B_tm, B_mod], [B_hb], bias=shv(k))

    tiles512 = [(i * 512, 512, 0) for i in range(8)] + [(T, TC, 1)]
    for l in range(n_layers):
        last = (l == L - 1)
        with (ExitStack() if l * 5 + 0 <= stop_step else _Skip()) as st:
            win, _ = sbt(st, "win", [128, 8, IN_W], BF16)
            Bwin = [Buf(f"win{i}") for i in range(9)]
            wstg = [sbt(st, f"wstg{i}", [128, 8, 256]) for i in range(3)]
            wsr = Rot(wstg)

            def bw(col0, width):
                return Bwin[col0 // 256:(col0 + width - 1) // 256 + 1]

            def load_win():
                for pc in range(9):
                    s_t, s_b = wsr.next()
                    P.dma(s_t[:], win_in[l].rearrange("(k p) n -> p k n", p=128)[:, :, pc * 256:(pc + 1) * 256], writes=[s_b],
                          eng="pool")
                    convert(win[:, :, pc * 256:(pc + 1) * 256], s_t[:], [s_b], [Bwin[pc]])
            wsf, B_wsf = sbt(st, "wsf", [128, 4, 128])
            wsb, B_wsb = sbt(st, "wsb", [128, 4, 128], BF16)
            P.dma(wsf[:], wsT_in[l], writes=[B_wsf])
            cpy(wsb[:], wsf[:], [B_wsf], [B_wsb])
            bst, B_bst = sbt(st, "bst", [128, 2, 512])
            P.dma(bst[:], bsT_in[l], writes=[B_bst])
            xs2 = [sbt(st, f"xs{i}", [128, 8, 512]) for i in range(2)]
            cs2 = [sbt(st, f"cos{i}", [128, 2, 512]) for i in range(2)]
            sq, B_sq = sbt(st, "sq", [128, 8, 512], BF16)
            hb, B_hb = sbt(st, "hb", [128, 8, 512], BF16)
            lnb, B_ln = sbt(st, "lnb", [128, 512])
            rstd, B_rstd = sbt(st, "rstd", [128, 512])
            tmps = Rot([sbt(st, f"tmp{i}", [128, 512]) for i in range(4)])
            qbs = Rot([sbt(st, f"qb{i}", [128, 512], BF16) for i in range(2)])
            r1s = Rot([sbt(st, f"r1{i}", [128, 512]) for i in range(2)])
            r2s = Rot([sbt(st, f"r2{i}", [128, 512]) for i in range(2)])
            qrs = Rot([sbt(st, f"qr{i}", [128, 512], BF16) for i in range(3)])
            vb, B_vb = sbt(st, "vb", [128, 4, 4, 130], BF16)
            memset(vb[:], 1.0, [B_vb])
            pf, B_pf = sbt(st, "pf", [128, 2, 512])
            gvfs = Rot([sbt(st, f"gvf{i}", [128, 256]) for i in range(2)])
            sqgs = Rot([sbt(st, f"sqg{i}", [128, 256]) for i in range(2)])
            ssgs = Rot([sbt(st, f"ssg{i}", [128, 4]) for i in range(2)])
            lngs = Rot([sbt(st, f"lng{i}", [128, 4]) for i in range(2)])
            rsgs = Rot([sbt(st, f"rsg{i}", [128, 4]) for i in range(2)])
            vns = Rot([sbt(st, f"vn{i}", [128, 4, 64], BF16) for i in range(4)])
            mbt, B_mbt = sbt(st, "mbt", [128, 512])
            mbb, B_mbb = sbt(st, "mbb", [128, 2, 512], BF16)
            pA = Rot([pst(st, f"pA{i}", [128, 512]) for i in range(4)])
            pR = Rot([pst(st, f"pR{i}", [128, 512]) for i in range(2)])
            usb, B_usb = sbt(st, "usb", [128, 2, 512])
            pVM, B_pVM = pst(st, "pVM", [128, 2, 512])
            evi = [0]
            if dbg:
                print('  mark tiles', Prog.count)
            hbB = sbt(st, "hbB", [128, 8, 512], BF16)
            hbs = [(hb, B_hb), hbB]

            def p1_load(ti):
                tok0, nt, cond = tiles512[ti]
                xs, B_xs = xs2[ti % 2]
                cs, B_cs = cs2[ti % 2]
                P.dma(xs[:, :, :nt], x_src(l, tok0, nt), writes=[B_xs])
                if cond == 0:
                    P.dma(cs[:, 0, :], cosT_in[:, tok0:tok0 + nt], writes=[B_cs])
                    P.dma(cs[:, 1, :], sinT_in[:, tok0:tok0 + nt], writes=[B_cs])

            def p1_square(ti, part=None):
                tok0, nt, cond = tiles512[ti]
                xs, B_xs = xs2[ti % 2]
                ks = slice(0, 8) if part is None else slice(2 * part, 2 * part + 2)
                act(sq[:, ks, :nt], xs[:, ks, :nt], AF.Square, [B_xs], [B_sq])

            def p1_norm(ti):
                tok0, nt, cond = tiles512[ti]
                xs, B_xs = xs2[ti % 2]
                hb_, B_hb_ = hbs[ti % 2]
                pss, B_pss = pA.next()
                norm_mod((sq, B_sq, lnb, B_ln, rstd, B_rstd, tmps), xs, B_xs, nt,
                         lambda k: gs1[:, l, k, cond:cond + 1], lambda k: mod[:, l, k, cond:cond + 1],
                         hb_, B_hb_, pss, B_pss, do_square=False)

            p1_load(0)
            p1_square(0)
            p1_norm(0)
            load_win()
            for ti, (tok0, nt, cond) in enumerate(tiles512):
                xs, B_xs = xs2[ti % 2]
                cs, B_cs = cs2[ti % 2]
                hb, B_hb = hbs[ti % 2]
                if ti + 1 < len(tiles512):
                    p1_load(ti + 1)
                if dbg and ti == 0:
                    print('  mark qk', Prog.count)
                full = not (last and cond == 1)
                ns = nt // 128
                prev = [None]

                def rope_tail():
                    if prev[0] is None:
                        return
                    (dst_ap, qb, B_qb, r1, B_r1) = prev[0]
                    prev[0] = None
                    r2, B_r2 = r2s.next()
                    pr, B_pr = pR.next()
                    qr, B_qr = qrs.next()
                    mm(pr[:, :nt], rsw_bf[:], qb[:, :nt], True, True, [B_rsw, B_qb], [B_pr])
                    tt(r2[:, :nt], pr[:, :nt], cs[:, 1, :nt], ALU.mult, [B_pr, B_cs], [B_r2])
                    tt(qr[:, :nt], r1[:, :nt], r2[:, :nt], ALU.add, [B_r1, B_r2], [B_qr])
                    P.dma(dst_ap, qr[:, :nt], reads=[B_qr])

                for qk in range(2):
                    if last and cond == 1 and qk == 0:
                        continue
                    dst = QT if qk == 0 else KT
                    for c in range(4):
                        col0 = qk * 512 + c * 128
                        ps, B_ps = pA.next()
                        for k in range(8):
                            mm(ps[:, :nt], win[:, k, col0:col0 + 128], hb[:, k, :nt], k == 0, k == 7, bw(col0, 128) + [B_hb], [B_ps])
                        if cond == 0:
                            qb, B_qb = qbs.next()
                            r1, B_r1 = r1s.next()
                            act(qb[:, :nt], ps[:, :nt], AF.Copy, [B_ps], [B_qb])
                            tt(r1[:, :nt], ps[:, :nt], cs[:, 0, :nt], ALU.mult, [B_ps, B_cs], [B_r1])
                            rope_tail()
                            prev[0] = (dst[c, :, tok0:tok0 + nt], qb, B_qb, r1, B_r1)
                        else:
                            qr, B_qr = qrs.next()
                            act(qr[:, :nt], ps[:, :nt], AF.Copy, [B_ps], [B_qr])
                            P.dma(dst[c, :, tok0:tok0 + nt], qr[:, :nt], reads=[B_qr])
                        if qk == 1 and ti + 1 < len(tiles512):
                            p1_square(ti + 1, part=c)
                if ti + 1 < len(tiles512):
                    p1_norm(ti + 1)
                rope_tail()
                for s in range(ns):
                    ps, B_ps = pA.next()
                    for k in range(8):
                        mm(ps[:, :], hb[:, k, s * 128:(s + 1) * 128], win[:, k, 1024:1536], k == 0, k == 7, bw(1024, 512) + [B_hb], [B_ps])
                    evi[0] += 1
                    cpy(vb[:, s, :, 0:128], ps[:, :].rearrange("p (h e) -> p h e", h=4), [B_ps], [B_vb],
                        eng=("act" if evi[0] % 2 else "dve"))
                P.dma(VV[tok0:tok0 + nt].rearrange("(s p) h e -> p s h e", p=128), vb[:, :ns], reads=[B_vb])
                vm_todo = []
                if full:
                    for s in range(ns):
                        ps, B_ps = pA.next()
                        for k in range(8):
                            mm(ps[:, 0:256], hb[:, k, s * 128:(s + 1) * 128], win[:, k, 1792:2048], k == 0, k == 7, bw(1792, 256) + [B_hb], [B_ps])
                        gvf, B_gvf = gvfs.next()
                        sqg, B_sqg = sqgs.next()
                        ssg, B_ssg = ssgs.next()
                        lng, B_lng = lngs.next()
                        rsg, B_rsg = rsgs.next()
                        act(gvf[:], ps[:, 0:256], AF.Copy, [B_ps], [B_gvf])
                        tt(sqg[:], gvf[:], gvf[:], ALU.mult, [B_gvf], [B_sqg])
                        red(ssg[:], sqg[:].rearrange("p (g c) -> p g c", g=4), [B_sqg], [B_ssg])
                        rsqrt_chain(rsg[:], ssg[:], 1.0 / 64, lng[:], B_lng, [B_ssg], B_rsg)
                        vn, B_vn = vns.next()
                        tt(vn[:], gvf[:].rearrange("p (g c) -> p g c", g=4), rsg[:].unsqueeze(2).to_broadcast([128, 4, 64]),
                           ALU.mult, [B_gvf, B_rsg], [B_vn])
                        vm_todo.append((s, vn, B_vn))
                if not full:
                    continue
                for c2 in range(2):
                    col0 = 1536 + c2 * 128
                    ps, B_ps = pA.next()
                    for k in range(8):
                        mm(ps[:, :nt], win[:, k, col0:col0 + 128], hb[:, k, :nt], k == 0, k == 7, bw(col0, 128) + [B_hb], [B_ps])
                    act(usb[:, c2, :nt], ps[:, :nt], AF.Copy, [B_ps], [B_usb])
                for c2 in range(2):
                    col0 = 2048 + c2 * 128
                    ps, B_ps = pA.next()
                    for k in range(8):
                        mm(ps[:, :nt], win[:, k, col0:col0 + 128], hb[:, k, :nt], k == 0, k == 7, bw(col0, 128) + [B_hb], [B_ps])
                    act(pf[:, c2, :nt], ps[:, :nt], AF.Copy, [B_ps], [B_pf])
                P.dma(PT.rearrange("c p t -> p c t")[:, :, tok0:tok0 + nt], pf[:, :, :nt], reads=[B_pf])
                for (s, vn, B_vn) in vm_todo:
                    for g in range(4):
                        gl_, c2 = g % 2, g // 2
                        mm(pVM[gl_ * 64:(gl_ + 1) * 64, c2, s * 128:(s + 1) * 128], vn[:, g, :], wsb[:, g, :], True, True,
                           [B_vn, B_wsb], [B_pVM])
                for c2 in range(2):
                    stt(mbt[:, :nt], pVM[:, c2, :nt], gvn[:, l, c2:c2 + 1], bst[:, c2, :nt], ALU.mult, ALU.add,
                        [B_pVM, B_gvn, B_bst], [B_mbt])
                    tt(mbb[:, c2, :nt], mbt[:, :nt], usb[:, c2, :nt], ALU.mult, [B_mbt, B_usb], [B_mbb])
                P.dma(MIX[4:6].rearrange("c p t -> p c t")[:, :, tok0:tok0 + nt], mbb[:, :, :nt], reads=[B_mbb])
            P.barrier()
            if dbg:
                print('ops@', l, Prog.count)

        with (ExitStack() if l * 5 + 1 <= stop_step else _Skip()) as st:
            xpA = sbt(st, "xp", [128, PADL])
            xpB = sbt(st, "xpB", [128, PADL])
            icB = sbt(st, "icB", [128, PADL])
            a2, B_a2 = sbt(st, "a2", [128, PADL])
            s4, B_s4 = sbt(st, "s4", [128, PADL])
            s8, B_s8 = sbt(st, "s8", [128, PADL])
            s16, B_s16 = sbt(st, "s16", [128, PADL])
            ic, B_ic = sbt(st, "ic", [128, PADL])
            dt_, B_dt = sbt(st, "dt", [128, PADL])
            dbf, B_dbf = sbt(st, "dbf", [128, PADL], BF16)
            wpf, B_wpf = sbt(st, "wpf", [128, 2, 128])
            wpb, B_wpb = sbt(st, "wpb", [128, 2, 128], BF16)
            mcs = Rot([sbt(st, f"mc{i}", [128, 512], BF16) for i in range(2)])
            pA = Rot([pst(st, f"pbA{i}", [128, 512]) for i in range(2)])
            memset(wpf[:], 0.0, [B_wpf])
            for g in range(4):
                gl_, c2 = g % 2, g // 2
                P.dma(wpf[gl_ * 64:(gl_ + 1) * 64, c2, gl_ * 64:(gl_ + 1) * 64], wpool_in[l, g], writes=[B_wpf])
            cpy(wpb[:], wpf[:], [B_wpf], [B_wpb])
            Lp = PADL
            icA = (ic, B_ic)
            for c2 in range(2):
                xp, B_xp = (xpA, xpB)[c2]
                ic_, B_ic_ = (icA, icB)[c2]
                memset(xp[:, 0:8], 0.0, [B_xp])
                memset(xp[:, 4104:4120], 0.0, [B_xp])
                memset(xp[:, 4376:4384], 0.0, [B_xp])
                P.dma(xp[:, 8:8 + T], PT[c2, :, 0:T], writes=[B_xp], eng=("sp" if c2 == 0 else "pool"))
                P.dma(xp[:, 4120:4120 + TC], PT[c2, :, T:TT], writes=[B_xp], eng=("sp" if c2 == 0 else "pool"))
                P.dma(ic_[:], ic_in[:, c2, :], writes=[B_ic_], eng=("sp" if c2 == 0 else "pool"))
            for c2 in range(2):
                xp, B_xp = (xpA, xpB)[c2]
                ic, B_ic = (icA, icB)[c2]
                tt(a2[:, 1:Lp], xp[:, 0:Lp - 1], xp[:, 1:Lp], ALU.add, [B_xp], [B_a2])
                if c2 == 0:
                    tt(s4[64:128, 2:Lp - 1], a2[64:128, 1:Lp - 2], a2[64:128, 3:Lp], ALU.add, [B_a2], [B_s4])
                    srcs = [(a2, B_a2), (s4, B_s4)]
                else:
                    tt(s4[:, 2:Lp - 1], a2[:, 1:Lp - 2], a2[:, 3:Lp], ALU.add, [B_a2], [B_s4])
                    tt(s8[:, 4:Lp - 3], s4[:, 2:Lp - 5], s4[:, 6:Lp - 1], ALU.add, [B_s4], [B_s8])
                    tt(s16[64:128, 8:Lp - 7], s8[64:128, 4:Lp - 11], s8[64:128, 12:Lp - 3], ALU.add, [B_s8], [B_s16])
                    srcs = [(s8, B_s8), (s16, B_s16)]
                for gl_ in range(2):
                    s_t, s_b = srcs[gl_]
                    pr_ = slice(gl_ * 64, (gl_ + 1) * 64)
                    tt(dt_[pr_, 8:Lp - 8], s_t[pr_, 8:Lp - 8], ic[pr_, 8:Lp - 8], ALU.mult, [s_b, B_ic], [B_dt])
                tt(dbf[:, 8:Lp - 8], dt_[:, 8:Lp - 8], xp[:, 8:Lp - 8], ALU.subtract, [B_dt, B_xp], [B_dbf])
                for ti, (tok0, nt, cond) in enumerate(tiles512):
                    if last and cond == 1:
                        continue
                    col = 8 + tok0 if cond == 0 else 4120 + (tok0 - T)
                    ps, B_ps = pA.next()
                    mm(ps[:, :nt], wpb[:, c2, :], dbf[:, col:col + nt], True, True, [B_wpb, B_dbf], [B_ps])
                    mc, B_mc = mcs.next()
                    act(mc[:, :nt], ps[:, :nt], AF.Identity, [B_ps, B_spool], [B_mc], scale=spool[:, l, c2:c2 + 1])
                    P.dma(MIX[6 + c2, :, tok0:tok0 + nt], mc[:, :nt], reads=[B_mc])
            P.barrier()
            if dbg:
                print('ops@', l, Prog.count)

        sawm = ExitStack()
        w1b, B_w1 = sbt(sawm, "w1b", [128, 8, 4 * D], BF16)
        w1_pieces = [(k, nq) for k in range(8) for nq in range(4)]

        wout, B_wout = sbt(sawm, "wout", [128, 8, D], BF16)
        wout_pieces = list(range(8))

        def wout_piece(stg_rot):
            pc = wout_pieces.pop(0)
            s_t, s_b = stg_rot.next()
            P.dma(s_t[:].rearrange("p (k n) -> p k n", k=8),
                  wout_in[l].rearrange("(k p) n -> p k n", p=128)[:, :, pc * 128:(pc + 1) * 128], writes=[s_b], eng="pool")
            cpy(wout[:, :, pc * 128:(pc + 1) * 128], s_t[:].rearrange("p (k n) -> p k n", k=8), [s_b], [B_wout], eng="dve")

        def w1_piece(stg_rot):
            if wout_pieces:
                wout_piece(stg_rot)
                return
            k, nq = w1_pieces.pop(0)
            s_t, s_b = stg_rot.next()
            P.dma(s_t[:], w1_in[l, k * 128:(k + 1) * 128, nq * 1024:(nq + 1) * 1024], writes=[s_b], eng="pool")
            cpy(w1b[:, k, nq * 1024:(nq + 1) * 1024], s_t[:], [s_b], [B_w1], eng="dve")

        with (ExitStack() if l * 5 + 2 <= stop_step else _Skip()) as st:
            stgA = Rot([sbt(st, f"stgA{i}", [128, 1024]) for i in range(2)])
            kt_sb, B_kt = sbt(st, "kt_sb", [128, 4, TT], BF16)
            v_sb, B_v = sbt(st, "v_sb", [128, NKT, 4, 130], BF16)
            for h in range(4):
                P.dma(kt_sb[:, h, :], KT[h], writes=[Buf()], key=B_kt, eng=("sp" if h % 2 == 0 else "pool"))
            for g4 in range(0, NKT, 6):
                n4 = min(6, NKT - g4)
                P.dma(v_sb[:, g4:g4 + n4], VV[g4 * 128:(g4 + n4) * 128].rearrange("(k p) h e -> p k h e", p=128),
                      writes=[Buf()], key=B_v, eng=("pool" if (g4 // 6) % 2 == 0 else "sp"))
            P.barrier()
            qbl = [sbt(st, f"qbl{i}", [128, 512], BF16) for i in range(2)]
            Es = [sbt(st, f"E{i}", [128, 2, 512], BF16) for i in range(3)]
            sps = [pst(st, f"sps{i}", [128, 2, 512]) for i in range(2)]
            accb = [pst(st, "accA", [128, 512]), pst(st, "accB", [128, 512]), pst(st, "accC", [128, 512])]
            pT, B_pT = pst(st, "pT", [128, 4, 128], BF16)
            accS, B_accS = sbt(st, "accS", [128, 8, 130])
            rz8, B_rz = sbt(st, "rz8", [128, 8, 1])
            nrz4, B_nrz = sbt(st, "nrz4", [128, 4, 1])
            o04, B_o0 = sbt(st, "o04", [128, 4, 128])
            t14, B_t1 = sbt(st, "t14", [128, 4, 128])
            oo4, B_oo = sbt(st, "oo4", [128, 4, 128])
            osq4, B_osq = t14, B_t1
            oss4, B_oss = sbt(st, "oss4", [128, 4])
            oln4, B_oln = sbt(st, "oln4", [128, 4])
            ors4, B_ors = sbt(st, "ors4", [128, 4, 1])
            abt, B_abt = o04, B_o0
            ab4s = [sbt(st, f"ab4{i}", [128, 4, 128], BF16) for i in range(2)]
            aTs = Rot([sbt(st, f"aT{i}", [128, 512], BF16) for i in range(2)])
            do_mod = (l + 1 < n_layers)
            if do_mod:
                scb2, B_scb2 = sbt(st, "scb2", [128, 2, D])
                bada2, B_bada2 = sbt(st, "bada2", [128, L, 48])
                P.dma(scb2[:], SCB, writes=[B_scb2])
                P.dma(bada2[:], bada_in, writes=[B_bada2])
                wrot2 = Rot([sbt(st, f"wst2{i}", [128, D]) for i in range(2)])
                prod2, B_prod2 = sbt(st, "prod2", [128, 1, D])
                redt2, B_redt2 = sbt(st, "redt2", [128, 48, 2])
            mod_j = [0]
            pending = []
            blocks = []
            for h in range(4):
                for qb_i in range(8):
                    blocks.append((h, qb_i * 512, 512, list(range(NKT))))
                if not last:
                    blocks.append((h, T, TC, [32, 33]))
            for bi, (h, q0, qn, kts) in enumerate(blocks):
                qt_, B_q = qbl[bi % 2]
                P.dma(qt_[:, :qn], QT[h, :, q0:q0 + qn], writes=[B_q])
                nqs = qn // 128

                def qk(i, kt):
                    sp, B_sp = sps[i % 2]
                    for j in range(2):
                        pr_ = slice(j * 64, (j + 1) * 64)
                        mm(sp[:, j, :qn], kt_sb[pr_, h, kt * 128:(kt + 1) * 128], qt_[pr_, :qn], True, True, [B_kt, B_q], [B_sp])
                    E, B_E = Es[i % 3]
                    act(E[:, :, :qn], sp[:, :, :qn], AF.Exp, [B_sp], [B_E], scale=0.125)

                def pv(i, kt):
                    E, B_E = Es[i % 3]
                    banks_started = set()
                    for j in range(2):
                        for qs in range(nqs):
                            idx = j * 4 + qs
                            bk, slot = idx // 3, idx % 3
                            a_t, a_b = accb[bk]
                            first_in_bank = (i == 0 and bk not in banks_started)
                            banks_started.add(bk)
                            mm(a_t[:, slot * 130:slot * 130 + 129], E[:, j, qs * 128:(qs + 1) * 128], v_sb[:, kt, h, 0:129],
                               first_in_bank, i == len(kts) - 1, [B_E, B_v], [a_b], skip=True)

                qk(0, kts[0])
                if len(kts) > 1:
                    qk(1, kts[1])
                for i, kt in enumerate(kts):
                    if i + 2 < len(kts):
                        qk(i + 2, kts[i + 2])
                    pv(i, kt)
                    while pending and pending[0][0] <= i:
                        pending.pop(0)[1]()
                while pending:
                    pending.pop(0)[1]()
                used = sorted(set((j * 4 + qs) // 3 for j in range(2) for qs in range(nqs)))
                for bk in used:
                    a_t, a_b = accb[bk]
                    nsl = 3 if bk < 2 else 2
                    cpy(accS[:, bk * 3:bk * 3 + nsl, :], a_t[:, 0:nsl * 130].rearrange("p (s c) -> p s c", c=130), [a_b], [B_accS],
                        eng="dve")
                P.op("dve", lambda e: e.reciprocal(out=rz8[:], in_=accS[:, :, 128:129]), [B_accS], [B_rz])
                ts(nrz4[:, :nqs], rz8[:, 4:4 + nqs], nlam[:, l:l + 1], None, ALU.mult, None, [B_rz, B_nlam], [B_nrz])
                tt(o04[:, :nqs], accS[:, 0:nqs, 0:128], rz8[:, 0:nqs].to_broadcast([128, nqs, 128]), ALU.mult, [B_accS, B_rz], [B_o0])
                tt(t14[:, :nqs], accS[:, 4:4 + nqs, 0:128], nrz4[:, :nqs].to_broadcast([128, nqs, 128]), ALU.mult,
                   [B_accS, B_nrz], [B_t1])
                tt(oo4[:, :nqs], o04[:, :nqs], t14[:, :nqs], ALU.add, [B_o0, B_t1], [B_oo])
                tt(osq4[:, :nqs], oo4[:, :nqs], oo4[:, :nqs], ALU.mult, [B_oo], [B_osq])
                red(oss4[:, :nqs], osq4[:, :nqs], [B_osq], [B_oss])
                ab4, B_ab = ab4s[bi % 2]
                aT, B_aT = aTs.next()

                def stage2(nqs=nqs):
                    rsqrt_chain(ors4[:, :nqs, 0], oss4[:, :nqs], 1.0 / 128, oln4[:, :nqs], B_oln, [B_oss], B_ors)

                def stage3(nqs=nqs, ab4=ab4, B_ab=B_ab):
                    tt(abt[:, :nqs], oo4[:, :nqs], ors4[:, :nqs].to_broadcast([128, nqs, 128]), ALU.mult, [B_oo, B_ors], [B_abt])
                    tt(ab4[:, :nqs], abt[:, :nqs], GS[:, l, :].unsqueeze(1).to_broadcast([128, nqs, 128]), ALU.mult,
                       [B_abt, B_GS], [B_ab])

                def stage4(nqs=nqs, ab4=ab4, B_ab=B_ab):
                    for qs in range(nqs):
                        transpose(pT[:, qs, :], ab4[:, qs, :], ident_bf[:], [B_ab, B_ident], [B_pT])

                def stage5(nqs=nqs, qn=qn, h=h, q0=q0, aT=aT, B_aT=B_aT):
                    cpy(aT[:, :qn], pT[:, 0:nqs, :].rearrange("p a b -> p (a b)"), [B_pT], [B_aT], eng="dve")
                    P.dma(MIX[h, :, q0:q0 + qn], aT[:, :qn], reads=[B_aT])

                pending = [(5, stage2), (10, stage3), (15, stage4), (19, stage5)]
                if w1_pieces:
                    w1_piece(stgA)
                    if bi % 4 == 0 and w1_pieces:
                        w1_piece(stgA)
                if do_mod:
                    for _ in range(2):
                        if mod_j[0] < 48:
                            mod_chunk(l + 1, mod_j[0], scb2, B_scb2, wrot2, prod2, B_prod2, redt2, B_redt2)
                            mod_j[0] += 1
            while pending:
                pending.pop(0)[1]()
            while w1_pieces or wout_pieces:
                w1_piece(stgA)
            if do_mod:
                while mod_j[0] < 48:
                    mod_chunk(l + 1, mod_j[0], scb2, B_scb2, wrot2, prod2, B_prod2, redt2, B_redt2)
                    mod_j[0] += 1
                mod_finish(l + 1, redt2, B_redt2, bada2, B_bada2)
            P.barrier()
            if dbg:
                print('ops@', l, Prog.count)

        tiles256 = [(i * 256, 256, 0) for i in range(16)] + ([] if last else [(T, TC, 1)])
        with (ExitStack() if l * 5 + 3 <= stop_step else _Skip()) as st:
            w2b, B_w2 = sbt(st, "w2b", [128, 32, D], BF16)
            wstg = Rot([sbt(st, f"wstg{i}", [128, 1024]) for i in range(2)])
            xs2 = [sbt(st, f"wxs{i}", [128, 8, 256]) for i in range(2)]
            bfA = sbt(st, "bfA", [128, 8, 256], BF16)
            bfB = sbt(st, "bfB", [128, 8, 256], BF16)
            mx2 = [bfA, bfB]
            lnb, B_ln = sbt(st, "mlnb", [128, 256])
            rstd, B_rstd = sbt(st, "mrstd", [128, 256])
            tmps = Rot([sbt(st, f"mtmp{i}", [128, 256]) for i in range(2)])
            hid, B_hid = sbt(st, "hid", [128, 32, 256], BF16)
            r32s = Rot([sbt(st, f"r32{i}", [128, 256]) for i in range(2)])
            pA = Rot([pst(st, f"pmA{i}", [128, 512]) for i in range(3)])
            pB = Rot([pst(st, f"pmB{i}", [128, 512]) for i in range(2)])
            XTb = [Buf(f"XTb{i}") for i in range(len(tiles256))]
            w2_f = [0]

            def w2_piece():
                f = w2_f[0]
                w2_f[0] += 1
                s_t, s_b = wstg.next()
                P.dma(s_t[:], w2_in[l, f * 128:(f + 1) * 128, :], writes=[s_b], eng="pool")
                convert(w2b[:, f, :], s_t[:], [s_b], [B_w2], engs=("act", "dve"))

            sq, B_sq = bfA
            bfC = sbt(st, "bfC", [128, 8, 256], BF16)
            h2s = [bfB, bfC]

            def m_norm(ti):
                tok0, nt, cond = tiles256[ti]
                xn, B_xn = xs2[ti % 2]
                h2, B_h2 = h2s[ti % 2]
                mx, B_mx = bfA
                for co in range(8):
                    ps, B_ps = pA.next()
                    for k in range(8):
                        mm(ps[:, :nt], wout[:, k, co * 128:(co + 1) * 128], mx[:, k, :], k == 0, k == 7, [B_wout, B_mx], [B_ps])
                    stt(xn[:, co, :], ps[:, :nt], mod[:, l, 16 + co, cond:cond + 1], xn[:, co, :], ALU.mult, ALU.add,
                        [B_ps, B_mod, B_xn], [B_xn])
                pss, B_pss = pA.next()
                norm_mod((sq, B_sq, lnb, B_ln, rstd, B_rstd, tmps), xn, B_xn, nt,
                         lambda k: gs2[:, l, k, cond:cond + 1], lambda k: mod[:, l, 24 + k, cond:cond + 1],
                         h2, B_h2, pss, B_pss)

            def m_load(ti):
                tok0, nt, cond = tiles256[ti]
                xn, B_xn = xs2[ti % 2]
                mx, B_mx = bfA
                P.dma(xn[:], x_src(l, tok0, nt), writes=[B_xn])
                P.dma(mx[:], MIX.rearrange("c p t -> p c t")[:, :, tok0:tok0 + nt], writes=[B_mx])

            m_load(0)
            m_norm(0)
            while w2_f[0] < 32:
                w2_piece()
            for ti, (tok0, nt, cond) in enumerate(tiles256):
                xn, B_xn = xs2[ti % 2]
                h2, B_h2 = h2s[ti % 2]
                if ti + 1 < len(tiles256):
                    m_load(ti + 1)
                for f in range(32):
                    ps, B_ps = pA.next()
                    for k in range(8):
                        mm(ps[:, :nt], w1b[:, k, f * 128:(f + 1) * 128], h2[:, k, :], k == 0, k == 7, [B_w1, B_h2], [B_ps])
                    r32, B_r32 = r32s.next()
                    act(r32[:], ps[:, :nt], AF.Relu, [B_ps], [B_r32])
                    tt(hid[:, f, :], r32[:], r32[:], ALU.mult, [B_r32], [B_hid])
                if ti + 1 < len(tiles256):
                    m_norm(ti + 1)
                for co in range(8):
                    ps, B_ps = pB.next()
                    for f in range(32):
                        mm(ps[:, :nt], w2b[:, f, co * 128:(co + 1) * 128], hid[:, f, :], f == 0, f == 31, [B_w2, B_hid], [B_ps])
                    stt(xn[:, co, :], ps[:, :nt], mod[:, l, 40 + co, cond:cond + 1], xn[:, co, :], ALU.mult, ALU.add,
                        [B_ps, B_mod, B_xn], [B_xn])
                if not last:
                    P.dma(x_dst(tok0, nt), xn[:], reads=[B_xn])
                else:
                    act(sq[:], xn[:], AF.Square, [B_xn], [B_sq])
                    pss, B_pss = pA.next()
                    for k in range(8):
                        mm(pss[:, :nt], ones_bf[:], sq[:, k, :], k == 0, k == 7, [B_ones, B_sq], [B_pss])
                    rsqrt_chain(rstd[:], pss[:, :nt], 1.0 / D, lnb[:], B_ln, [B_pss], B_rstd)
                    for k in range(8):
                        stt(xn[:, k, :], xn[:, k, :], gf[:, k:k + 1], rstd[:], ALU.mult, ALU.mult, [B_xn, B_gf, B_rstd], [B_xn])
                    P.dma(yT_out.rearrange("(k p) t -> p k t", p=128)[:, :, tok0:tok0 + nt], xn[:], reads=[B_xn], is_output=True)
            P.barrier()
            if dbg:
                print('ops@', l, Prog.count)
        sawm.close()

    P.emit(top)
    top.close()
    return nc


def _const_tables():
    grid_w = 64
    n_freq = 16
    t = np.arange(T)
    row = (t // grid_w).astype(np.float32)
    col = (t % grid_w).astype(np.float32)
    inv = (np.float32(10000.0) ** (-np.arange(n_freq, dtype=np.float32) / np.float32(n_freq))).astype(np.float32)
    cosT = np.zeros((128, T), np.float32)
    sinT = np.zeros((128, T), np.float32)
    rsw = np.zeros((128, 128), np.float32)
    for p in range(128):
        axis = (p % 64) // 32
        half = (p % 32) // 16
        f = p % 16
        ang = ((row if axis == 0 else col) * inv[f]).astype(np.float32)
        cosT[p] = np.cos(ang)
        sinT[p] = np.sin(ang) * (-1.0 if half == 0 else 1.0)
        partner = p + 16 if half == 0 else p - 16
        rsw[partner, p] = 1.0
    ic = np.zeros((128, 2, PADL), np.float32)
    wins = (2, 4, 8, 16)
    for g, w in enumerate(wins):
        gl_, c2 = g % 2, g // 2
        for (tseg, base) in ((T, 8), (TC, 4120)):
            pos = np.arange(tseg)
            lo = np.clip(pos - w // 2, 0, tseg)
            hi = np.clip(pos + (w - w // 2), 0, tseg)
            ic[gl_ * 64:(gl_ + 1) * 64, c2, base:base + tseg] = (1.0 / (hi - lo).astype(np.float32))[None, :]
    return cosT, sinT, rsw, np.eye(128, dtype=np.float32), ic


_NC_CACHE = {}


def _pvec(v, nch):
    v = np.asarray(v, np.float32)
    lead = v.shape[:-1]
    return np.ascontiguousarray(np.moveaxis(v.reshape(lead + (nch, 128)), -1, 0))


def kernel(x, c, ctx, c_ctx, w_ada, b_ada, g_norm_mix, g_norm_mlp, w_in, lam_q1, lam_k1, lam_q2, lam_k2,
           g_subln, g_vnorm, w_spatial, b_spatial, w_pool, s_pool, w_out, w1, w2, g_final):
    f = lambda a: np.ascontiguousarray(np.asarray(a, np.float32))
    x, c, ctx, c_ctx = f(x), f(c), f(ctx), f(c_ctx)
    n = 8
    if "nc" not in _NC_CACHE:
        _NC_CACHE["nc"] = build_program()
    nc = _NC_CACHE["nc"]
    cosT, sinT, rsw, ident, ic = _const_tables()
    w_ada = f(w_ada)
    wadaT = np.ascontiguousarray(w_ada.reshape(L, D, 48, 128).transpose(0, 3, 2, 1))
    b_sp = f(b_spatial)
    bsT = np.zeros((L, 128, 2, 512), np.float32)
    for g in range(4):
        gl_, c2 = g % 2, g // 2
        bsT[:, gl_ * 64:(gl_ + 1) * 64, c2, :] = np.tile(b_sp[:, g, :], (1, 4))[:, None, :]
    lamv = np.stack([f(lam_q1), f(lam_k1), f(lam_q2), f(lam_k2)], axis=1)
    shared = {
        "wadaT": wadaT,
        "bada": _pvec(f(b_ada), 48),
        "gm": _pvec(f(g_norm_mix), 8),
        "gl": _pvec(f(g_norm_mlp), 8),
        "gf": _pvec(f(g_final), 8),
        "w_in": f(w_in),
        "lamv": np.ascontiguousarray(np.broadcast_to(lamv[None], (128, L, 4, 64))),
        "gsub": np.ascontiguousarray(np.broadcast_to(f(g_subln)[None], (128, L, 128))),
        "gvn": _pvec(f(g_vnorm), 2),
        "wsT": np.ascontiguousarray(f(w_spatial).transpose(0, 3, 1, 2)),
        "bsT": bsT,
        "w_pool": f(w_pool),
        "spool": _pvec(f(s_pool), 2),
        "w_out": f(w_out),
        "w1": f(w1),
        "w2": f(w2),
        "cosT": cosT, "sinT": sinT, "rsw": rsw, "ident": ident, "icT": ic,
    }
    in_maps = []
    for b in range(n):
        m = dict(shared)
        m["xT"] = np.ascontiguousarray(x[b].T)
        m["ctxT"] = np.ascontiguousarray(ctx[b].T)
        cbv = np.stack([c[b], c_ctx], axis=0)
        m["cb"] = np.ascontiguousarray(np.broadcast_to(cbv[None], (128, 2, D)))
        in_maps.append(m)
    if _NC_CACHE.get("return_maps"):
        return in_maps
    res = run_bass_kernel_spmd(nc, in_maps, core_ids=list(range(n)))
    out = np.stack([np.ascontiguousarray(res.results[b]["yT"].T) for b in range(n)], axis=0)
    return out.astype(np.float32)
```

```python
import math
from contextlib import ExitStack
import numpy as np
import concourse.bass as bass
import concourse.mybir as mybir
from concourse.bass_utils import run_bass_kernel_spmd

F32 = mybir.dt.float32
BF16 = mybir.dt.bfloat16
AF = mybir.ActivationFunctionType
ALU = mybir.AluOpType
AX = mybir.AxisListType

D = 1024
T = 4096
TC = 256
TT = T + TC
L = 4
NKT = TT // 128
EPS = 1e-6
IN_W = 2304
PADL = 4384
SEM_TICK_LIMIT = 30000


class Buf:
    __slots__ = ("name", "last_w", "readers", "excl")

    def __init__(self, name="", excl=False):
        self.name = name
        self.last_w = None
        self.readers = []
        self.excl = excl


class Op:
    __slots__ = ("eng", "fn", "deps", "needs_inc", "sem", "tick", "is_dma", "dma_key")

    def __init__(self, eng, fn, is_dma=False, dma_key=None):
        self.eng = eng
        self.fn = fn
        self.deps = []
        self.needs_inc = False
        self.sem = None
        self.tick = 0
        self.is_dma = is_dma
        self.dma_key = dma_key


class Prog:
    ENGS = ("pe", "act", "dve", "pool", "sp")

    def __init__(self, nc):
        self.nc = nc
        self.q = {e: [] for e in self.ENGS}
        self.out_dmas = []
        self.pending_dmas = []
        self.dma_slots = []
        self.dma_active = {}
        self.dma_free = []

    muted = False
    limit = 10 ** 12
    count = 0

    def _add(self, op, reads, writes):
        Prog.count += 1
        if Prog.muted or Prog.count > Prog.limit:
            return op
        ex = [b for b in reads if b.excl]
        if ex:
            reads = [b for b in reads if not b.excl]
            writes = list(writes) + ex
        deps = set()
        for b in reads:
            if b.last_w is not None:
                deps.add(b.last_w)
        for b in writes:
            if b.last_w is not None:
                deps.add(b.last_w)
            for r in b.readers:
                deps.add(r)
        deps.discard(op)
        for d in deps:
            if d.eng == "pe" and op.eng == "pe" and not d.is_dma and not op.is_dma:
                continue
            d.needs_inc = True
            op.deps.append(d)
        for b in reads:
            b.readers.append(op)
        for b in writes:
            b.last_w = op
            b.readers = []
        self.q[op.eng].append(op)
        return op

    def op(self, eng, fn, reads=(), writes=()):
        return self._add(Op(eng, fn), reads, writes)

    def dma(self, out_ap, in_ap, reads=(), writes=(), key=None, eng="sp", is_output=False):
        def fn(e):
            return e.dma_start(out=out_ap, in_=in_ap)
        if key is None:
            key = writes[0] if writes else reads[0]
        o = Op(eng, fn, is_dma=True, dma_key=key)
        self._add(o, reads, writes)
        o.needs_inc = True
        if Prog.muted or Prog.count > Prog.limit:
            return o
        k = id(key)
        if k not in self.dma_active:
            if self.dma_free:
                self.dma_active[k] = self.dma_free.pop()
            else:
                self.dma_slots.append(0)
                self.dma_active[k] = len(self.dma_slots) - 1
        slot = self.dma_active[k]
        self.dma_slots[slot] += 16
        o.sem, o.tick = slot, self.dma_slots[slot]
        self.pending_dmas.append(o)
        if is_output:
            self.out_dmas.append(o)
        return o

    def barrier(self):
        if Prog.muted:
            return
        marks = []
        for e in self.ENGS:
            last = None
            for o in reversed(self.q[e]):
                if not o.is_dma and o.fn is not None:
                    last = o
                    break
            if last is not None:
                last.needs_inc = True
                marks.append(last)
        marks += self.pending_dmas
        self.pending_dmas = []
        for k, slot in self.dma_active.items():
            if self.dma_slots[slot] < 40000:
                self.dma_free.append(slot)
        self.dma_active = {}
        for e in self.ENGS:
            o = Op(e, None)
            o.deps = list(marks)
            self.q[e].append(o)

    def emit(self, stack):
        nc = self.nc

        def new_sem(name):
            return stack.enter_context(nc.semaphore(name))

        for e in self.ENGS:
            cur, cnt, n = None, 0, 0
            for o in self.q[e]:
                if o.is_dma or not o.needs_inc:
                    continue
                if cur is None or cnt >= SEM_TICK_LIMIT:
                    cur = new_sem(f"s_{e}_{n}")
                    n += 1
                    cnt = 0
                cnt += 1
                o.sem, o.tick = cur, cnt
        slot_sems = [new_sem(f"s_dma_{i}") for i in range(len(self.dma_slots))]
        for e in self.ENGS:
            for o in self.q[e]:
                if o.is_dma:
                    o.sem = slot_sems[o.sem]
        engmap = {"pe": "tensor", "act": "scalar", "dve": "vector", "pool": "gpsimd", "sp": "sync"}
        block = stack.enter_context(nc.Block())
        for e in self.ENGS:
            def body(eng, ops=self.q[e], e=e):
                waited = {}

                def wait(d):
                    k = id(d.sem)
                    if waited.get(k, 0) >= d.tick:
                        return
                    waited[k] = d.tick
                    eng.wait_ge(d.sem, d.tick)
                for o in ops:
                    for d in o.deps:
                        wait(d)
                    if o.fn is None:
                        continue
                    ins = o.fn(eng)
                    if o.needs_inc:
                        ins.then_inc(o.sem, 16 if o.is_dma else 1)
                if e == "sp":
                    for d in self.out_dmas:
                        wait(d)
            getattr(block, engmap[e])(body)


class _Skip:
    cur = None

    def __enter__(self):
        self.st = ExitStack()
        Prog.muted = True
        return self.st

    def __exit__(self, *a):
        Prog.muted = False
        self.st.close()
        return False


class Rot:
    def __init__(self, items):
        self.items = items
        self.i = 0

    def next(self):
        it = self.items[self.i % len(self.items)]
        self.i += 1
        return it


def lam_init(l):
    return 0.8 - 0.6 * math.exp(-0.3 * l)


def build_program(n_layers=L, dbg=False, stop_step=10 ** 9):
    nc = bass.Bass("TRN2", target_bir_lowering=False)
    P = Prog(nc)
    top = ExitStack()

    def din(name, shape, dt=F32):
        return nc.dram_tensor(name, list(shape), dt, kind="ExternalInput").ap()

    def dscr(name, shape, dt):
        kind = "ExternalOutput" if dbg else "Internal"
        return nc.dram_tensor(name, list(shape), dt, kind=kind).ap()

    xT_in = din("xT", [D, T])
    ctxT_in = din("ctxT", [D, TC])
    cb_in = din("cb", [128, 2, D])
    wadaT_in = din("wadaT", [L, 128, 48, D])
    bada_in = din("bada", [128, L, 48])
    gm_in = din("gm", [128, L, 8])
    gl_in = din("gl", [128, L, 8])
    gf_in = din("gf", [128, 8])
    win_in = din("w_in", [L, D, IN_W])
    lamv_in = din("lamv", [128, L, 4, 64])
    gsub_in = din("gsub", [128, L, 128])
    gvn_in = din("gvn", [128, L, 2])
    wsT_in = din("wsT", [L, 128, 4, 128])
    bsT_in = din("bsT", [L, 128, 2, 512])
    wpool_in = din("w_pool", [L, 4, 64, 64])
    spool_in = din("spool", [128, L, 2])
    wout_in = din("w_out", [L, D, D])
    w1_in = din("w1", [L, D, 4 * D])
    w2_in = din("w2", [L, 4 * D, D])
    cosT_in = din("cosT", [128, T])
    sinT_in = din("sinT", [128, T])
    rsw_in = din("rsw", [128, 128])
    ident_in = din("ident", [128, 128])
    ic_in = din("icT", [128, 2, PADL])
    yT_out = nc.dram_tensor("yT", [D, T], F32, kind="ExternalOutput").ap()

    XT = dscr("XT", [8, 128, TT], F32)
    QT = dscr("QT", [4, 128, TT], BF16)
    KT = dscr("KT", [4, 128, TT], BF16)
    VV = dscr("VV", [TT, 4, 130], BF16)
    PT = dscr("PT", [2, 128, TT], F32)
    MIX = dscr("MIX", [8, 128, TT], BF16)
    SCB = dscr("SCB", [128, 2, D], F32)

    uid = [0]

    def sbt(st, name, shape, dt=F32):
        uid[0] += 1
        t = st.enter_context(nc.sbuf_tensor(f"sb{uid[0]}_{name}", list(shape), dt))
        return t, Buf(name)

    def pst(st, name, shape, dt=F32):
        uid[0] += 1
        t = st.enter_context(nc.psum_tensor(f"ps{uid[0]}_{name}", list(shape), dt))
        return t, Buf(name, excl=True)

    def act(out, in_, func, reads, writes, **kw):
        P.op("act", lambda e: e.activation(out=out, in_=in_, func=func, **kw), reads, writes)

    def tt(out, in0, in1, op, reads, writes, eng="dve"):
        P.op(eng, lambda e: e.tensor_tensor(out=out, in0=in0, in1=in1, op=op), reads, writes)

    def ts(out, in0, s1, s2, op0, op1, reads, writes, eng="dve"):
        if s2 is None:
            P.op(eng, lambda e: e.tensor_scalar(out=out, in0=in0, scalar1=s1, scalar2=None, op0=op0), reads, writes)
        else:
            P.op(eng, lambda e: e.tensor_scalar(out=out, in0=in0, scalar1=s1, scalar2=s2, op0=op0, op1=op1), reads, writes)

    def stt(out, in0, scalar, in1, op0, op1, reads, writes, eng="dve"):
        P.op(eng, lambda e: e.scalar_tensor_tensor(out=out, in0=in0, scalar=scalar, in1=in1, op0=op0, op1=op1), reads, writes)

    def red(out, in_, reads, writes, eng="dve"):
        P.op(eng, lambda e: e.tensor_reduce(out=out, in_=in_, axis=AX.X, op=ALU.add), reads, writes)

    def cpy(out, in_, reads, writes, eng="dve"):
        if eng == "act":
            act(out, in_, AF.Copy, reads, writes)
        else:
            P.op(eng, lambda e: e.tensor_copy(out=out, in_=in_), reads, writes)

    def memset(ap, val, writes, eng="dve"):
        P.op(eng, lambda e: e.memset(ap, val), (), writes)

    def mm(out, lhsT, rhs, start, stop, reads, writes, skip=False):
        if skip:
            P.op("pe", lambda e: e.matmul(out, lhsT=lhsT, rhs=rhs, start=start, stop=stop, skip_group_check=True), reads, writes)
        else:
            P.op("pe", lambda e: e.matmul(out, lhsT=lhsT, rhs=rhs, start=start, stop=stop), reads, writes)

    def transpose(out, in_, ident, reads, writes):
        P.op("pe", lambda e: e.transpose(out=out, in_=in_, identity=ident), reads, writes)

    def rsqrt_chain(out, in_, scale, lnbuf, lnB, reads, outB):
        act(lnbuf, in_, AF.Ln, list(reads) + [B_eps], [lnB], scale=scale, bias=epsb[0:lnbuf.shape[0], 0:1])
        act(out, lnbuf, AF.Exp, [lnB], [outB], scale=-0.5)

    cvt_i = [0]

    def convert(out, in_, reads, writes, engs=("dve", "act")):
        e = engs[cvt_i[0] % len(engs)]
        cvt_i[0] += 1
        cpy(out, in_, reads, writes, eng=e)

    ident_bf, B_ident = sbt(top, "ident_bf", [128, 128], BF16)
    ones_bf, B_ones = sbt(top, "ones_bf", [128, 128], BF16)
    rsw_bf, B_rsw = sbt(top, "rsw_bf", [128, 128], BF16)
    gm, B_gm = sbt(top, "gm", [128, L, 8])
    gl, B_gl = sbt(top, "gl", [128, L, 8])
    gf, B_gf = sbt(top, "gf", [128, 8])
    spool, B_spool = sbt(top, "spool", [128, L, 2])
    gvn, B_gvn = sbt(top, "gvn", [128, L, 2])
    GS, B_GS = sbt(top, "GS", [128, L, 128])
    lam, B_lam = sbt(top, "lam", [128, L])
    nlam, B_nlam = sbt(top, "nlam", [128, L])
    mod, B_mod = sbt(top, "mod", [128, L, 48, 2])
    gs1, B_gs1 = sbt(top, "gs1", [128, L, 8, 2])
    gs2, B_gs2 = sbt(top, "gs2", [128, L, 8, 2])
    epsb, B_eps = sbt(top, "epsb", [128, 1])

    with ExitStack() as st:
        stg, B_stg = sbt(st, "pro_stg", [128, 2, 128])
        P.dma(stg[:, 0, :], ident_in, writes=[B_stg])
        P.dma(stg[:, 1, :], rsw_in, writes=[B_stg])
        cpy(ident_bf[:], stg[:, 0, :], [B_stg], [B_ident])
        cpy(rsw_bf[:], stg[:, 1, :], [B_stg], [B_rsw])
        memset(ones_bf[:], 1.0, [B_ones])
        memset(epsb[:], EPS, [B_eps])
        P.dma(gm[:], gm_in, writes=[B_gm])
        P.dma(gl[:], gl_in, writes=[B_gl])
        P.dma(gf[:], gf_in, writes=[B_gf])
        P.dma(spool[:], spool_in, writes=[B_spool])
        P.dma(gvn[:], gvn_in, writes=[B_gvn])
        bada, B_bada = sbt(st, "bada", [128, L, 48])
        P.dma(bada[:], bada_in, writes=[B_bada])
        gsub, B_gsub = sbt(st, "gsub", [128, L, 128])
        P.dma(gsub[:], gsub_in, writes=[B_gsub])
        lamv, B_lamv = sbt(st, "lamv", [128, L, 4, 64])
        P.dma(lamv[:], lamv_in, writes=[B_lamv])
        cbt, B_cb = sbt(st, "cbt", [128, 2, D])
        P.dma(cbt[:], cb_in, writes=[B_cb])
        scb, B_scb = sbt(st, "scb", [128, 2, D])
        act(scb[:], cbt[:], AF.Silu, [B_cb], [B_scb])
        lprod, B_lprod = sbt(st, "lprod", [128, L, 2, 64])
        lsum, B_lsum = sbt(st, "lsum", [128, L, 2])
        lexp, B_lexp = sbt(st, "lexp", [128, L, 2])
        for l in range(L):
            for j in range(2):
                tt(lprod[:, l, j, :], lamv[:, l, 2 * j, :], lamv[:, l, 2 * j + 1, :], ALU.mult, [B_lamv], [B_lprod])
        red(lsum[:], lprod[:], [B_lprod], [B_lsum])
        act(lexp[:], lsum[:], AF.Exp, [B_lsum], [B_lexp])
        for l in range(L):
            tt(lam[:, l:l + 1], lexp[:, l, 0:1], lexp[:, l, 1:2], ALU.subtract, [B_lexp], [B_lam])
            ts(lam[:, l:l + 1], lam[:, l:l + 1], float(lam_init(l)), None, ALU.add, None, [B_lam], [B_lam])
            ts(GS[:, l, :], gsub[:, l, :], float(1.0 - lam_init(l)), None, ALU.mult, None, [B_gsub], [B_GS])
        ts(nlam[:], lam[:], -1.0, None, ALU.mult, None, [B_lam], [B_nlam])
        wst = [sbt(st, f"wst{i}", [128, D]) for i in range(4)]
        wrot = Rot(wst)
        prod, B_prod = sbt(st, "prod", [128, 2, D])
        P.dma(SCB, scb[:], reads=[B_scb])

        def mod_chunk(l, j, scb_, B_scb_, wrot_, prod_, B_prod_, redt_, B_redt_):
            w_t, w_b = wrot_.next()
            P.dma(w_t[:], wadaT_in[l, :, j, :], writes=[w_b], eng="pool")
            if prod_.shape[1] == 2:
                tt(prod_[:], scb_[:], w_t[:].unsqueeze(1).to_broadcast([128, 2, D]), ALU.mult, [B_scb_, w_b], [B_prod_])
                red(redt_[:, j, :], prod_[:], [B_prod_], [B_redt_])
            else:
                for c_ in range(2):
                    tt(prod_[:, 0, :], scb_[:, c_, :], w_t[:], ALU.mult, [B_scb_, w_b], [B_prod_])
                    red(redt_[:, j, c_:c_ + 1], prod_[:, 0, :], [B_prod_], [B_redt_])

        def mod_finish(l, redt_, B_redt_, bada_, B_bada_):
            tt(mod[:, l], redt_[:], bada_[:, l, :].unsqueeze(2).to_broadcast([128, 48, 2]), ALU.add,
               [B_redt_, B_bada_], [B_mod])
            stt(gs1[:, l], mod[:, l, 8:16, :], 1.0, gm[:, l, :].unsqueeze(2).to_broadcast([128, 8, 2]), ALU.add, ALU.mult,
                [B_mod, B_gm], [B_gs1])
            stt(gs2[:, l], mod[:, l, 32:40, :], 1.0, gl[:, l, :].unsqueeze(2).to_broadcast([128, 8, 2]), ALU.add, ALU.mult,
                [B_mod, B_gl], [B_gs2])

        redt0, B_redt0 = sbt(st, "redt0", [128, 48, 2])
        memset(redt0[:], 0.0, [B_redt0])
        prodB = sbt(st, "prodB", [128, 2, D])
        prods = Rot([(prod, B_prod), prodB])
        junk, B_junk = sbt(st, "junk", [128, D])
        for j in range(48):
            w_t, w_b = wrot.next()
            P.dma(w_t[:], wadaT_in[0, :, j, :], writes=[w_b], eng="pool")
            pr_, B_pr_ = prods.next()
            tt(pr_[:], scb[:], w_t[:].unsqueeze(1).to_broadcast([128, 2, D]), ALU.mult, [B_scb, w_b], [B_pr_])
            for c_ in range(2):
                act(junk[:], pr_[:, c_, :], AF.Copy, [B_pr_], [B_junk, B_redt0], accum_out=redt0[:, j, c_:c_ + 1])
        mod_finish(0, redt0, B_redt0, bada, B_bada)
        P.barrier()
        if dbg:
            print('ops@pro', Prog.count)

    def x_src(l, tok0, nt):
        if l == 0:
            if tok0 < T:
                return xT_in.rearrange("(k p) t -> p k t", p=128)[:, :, tok0:tok0 + nt]
            return ctxT_in.rearrange("(k p) t -> p k t", p=128)[:, :, tok0 - T:tok0 - T + nt]
        return XT.rearrange("k p t -> p k t")[:, :, tok0:tok0 + nt]

    def x_dst(tok0, nt):
        return XT.rearrange("k p t -> p k t")[:, :, tok0:tok0 + nt]

    def norm_mod(st_bufs, xs, B_xs, nt, gsv, shv, hb, B_hb, pss, B_pss, do_square=True):
        sq, B_sq, lnb, B_ln, rstd, B_rstd, tmps = st_bufs
        if do_square:
            act(sq[:, :, :nt], xs[:, :, :nt], AF.Square, [B_xs], [B_sq])
        for k in range(8):
            mm(pss[:, :nt], ones_bf[:], sq[:, k, :nt], k == 0, k == 7, [B_ones, B_sq], [B_pss])
        rsqrt_chain(rstd[:, :nt], pss[:, :nt], 1.0 / D, lnb[:, :nt], B_ln, [B_pss], B_rstd)
        for k in range(8):
            tm, B_tm = tmps.next()
            stt(tm[:, :nt], xs[:, k, :nt], gsv(k), rstd[:, :nt], ALU.mult, ALU.mult, [B_xs, B_rstd, B_gs1, B_gs2], [B_tm])
            act(hb[:, k, :nt], tm[:, :nt], AF.Identity, [B_tm, B_mod], [B_hb], bias=shv(k))

    tiles512 = [(i * 512, 512, 0) for i in range(8)] + [(T, TC, 1)]
    for l in range(n_layers):
        last = (l == L - 1)
        with (ExitStack() if l * 5 + 0 <= stop_step else _Skip()) as st:
            win, _ = sbt(st, "win", [128, 8, IN_W], BF16)
            Bwin = [Buf(f"win{i}") for i in range(9)]
            wstg = [sbt(st, f"wstg{i}", [128, 8, 256]) for i in range(3)]
            wsr = Rot(wstg)

            def bw(col0, width):
                return Bwin[col0 // 256:(col0 + width - 1) // 256 + 1]

            def load_win():
                for pc in range(9):
                    s_t, s_b = wsr.next()
                    P.dma(s_t[:], win_in[l].rearrange("(k p) n -> p k n", p=128)[:, :, pc * 256:(pc + 1) * 256], writes=[s_b],
                          eng="pool")
                    convert(win[:, :, pc * 256:(pc + 1) * 256], s_t[:], [s_b], [Bwin[pc]])
            wsf, B_wsf = sbt(st, "wsf", [128, 4, 128])
            wsb, B_wsb = sbt(st, "wsb", [128, 4, 128], BF16)
            P.dma(wsf[:], wsT_in[l], writes=[B_wsf])
            cpy(wsb[:], wsf[:], [B_wsf], [B_wsb])
            bst, B_bst = sbt(st, "bst", [128, 2, 512])
            P.dma(bst[:], bsT_in[l], writes=[B_bst])
            xs2 = [sbt(st, f"xs{i}", [128, 8, 512]) for i in range(2)]
            cs2 = [sbt(st, f"cos{i}", [128, 2, 512]) for i in range(2)]
            sq, B_sq = sbt(st, "sq", [128, 8, 512], BF16)
            hb, B_hb = sbt(st, "hb", [128, 8, 512], BF16)
            lnb, B_ln = sbt(st, "lnb", [128, 512])
            rstd, B_rstd = sbt(st, "rstd", [128, 512])
            tmps = Rot([sbt(st, f"tmp{i}", [128, 512]) for i in range(4)])
            qbs = Rot([sbt(st, f"qb{i}", [128, 512], BF16) for i in range(2)])
            r1s = Rot([sbt(st, f"r1{i}", [128, 512]) for i in range(2)])
            r2s = Rot([sbt(st, f"r2{i}", [128, 512]) for i in range(2)])
            qrs = Rot([sbt(st, f"qr{i}", [128, 512], BF16) for i in range(3)])
            vb, B_vb = sbt(st, "vb", [128, 4, 4, 130], BF16)
            memset(vb[:], 1.0, [B_vb])
            pf, B_pf = sbt(st, "pf", [128, 2, 512])
            gvfs = Rot([sbt(st, f"gvf{i}", [128, 256]) for i in range(2)])
            sqgs = Rot([sbt(st, f"sqg{i}", [128, 256]) for i in range(2)])
            ssgs = Rot([sbt(st, f"ssg{i}", [128, 4]) for i in range(2)])
            lngs = Rot([sbt(st, f"lng{i}", [128, 4]) for i in range(2)])
            rsgs = Rot([sbt(st, f"rsg{i}", [128, 4]) for i in range(2)])
            vns = Rot([sbt(st, f"vn{i}", [128, 4, 64], BF16) for i in range(4)])
            mbt, B_mbt = sbt(st, "mbt", [128, 512])
            mbb, B_mbb = sbt(st, "mbb", [128, 2, 512], BF16)
            pA = Rot([pst(st, f"pA{i}", [128, 512]) for i in range(4)])
            pR = Rot([pst(st, f"pR{i}", [128, 512]) for i in range(2)])
            usb, B_usb = sbt(st, "usb", [128, 2, 512])
            pVM, B_pVM = pst(st, "pVM", [128, 2, 512])
            evi = [0]
            if dbg:
                print('  mark tiles', Prog.count)
            hbB = sbt(st, "hbB", [128, 8, 512], BF16)
            hbs = [(hb, B_hb), hbB]

            def p1_load(ti):
                tok0, nt, cond = tiles512[ti]
                xs, B_xs = xs2[ti % 2]
                cs, B_cs = cs2[ti % 2]
                P.dma(xs[:, :, :nt], x_src(l, tok0, nt), writes=[B_xs])
                if cond == 0:
                    P.dma(cs[:, 0, :], cosT_in[:, tok0:tok0 + nt], writes=[B_cs])
                    P.dma(cs[:, 1, :], sinT_in[:, tok0:tok0 + nt], writes=[B_cs])

            def p1_square(ti, part=None):
                tok0, nt, cond = tiles512[ti]
                xs, B_xs = xs2[ti % 2]
                ks = slice(0, 8) if part is None else slice(2 * part, 2 * part + 2)
                act(sq[:, ks, :nt], xs[:, ks, :nt], AF.Square, [B_xs], [B_sq])

            def p1_norm(ti):
                tok0, nt, cond = tiles512[ti]
                xs, B_xs = xs2[ti % 2]
                hb_, B_hb_ = hbs[ti % 2]
                pss, B_pss = pA.next()
                norm_mod((sq, B_sq, lnb, B_ln, rstd, B_rstd, tmps), xs, B_xs, nt,
                         lambda k: gs1[:, l, k, cond:cond + 1], lambda k: mod[:, l, k, cond:cond + 1],
                         hb_, B_hb_, pss, B_pss, do_square=False)

            p1_load(0)
            p1_square(0)
            p1_norm(0)
            load_win()
            for ti, (tok0, nt, cond) in enumerate(tiles512):
                xs, B_xs = xs2[ti % 2]
                cs, B_cs = cs2[ti % 2]
                hb, B_hb = hbs[ti % 2]
                if ti + 1 < len(tiles512):
                    p1_load(ti + 1)
                if dbg and ti == 0:
                    print('  mark qk', Prog.count)
                full = not (last and cond == 1)
                ns = nt // 128
                prev = [None]

                def rope_tail():
                    if prev[0] is None:
                        return
                    (dst_ap, qb, B_qb, r1, B_r1) = prev[0]
                    prev[0] = None
                    r2, B_r2 = r2s.next()
                    pr, B_pr = pR.next()
                    qr, B_qr = qrs.next()
                    mm(pr[:, :nt], rsw_bf[:], qb[:, :nt], True, True, [B_rsw, B_qb], [B_pr])
                    tt(r2[:, :nt], pr[:, :nt], cs[:, 1, :nt], ALU.mult, [B_pr, B_cs], [B_r2])
                    tt(qr[:, :nt], r1[:, :nt], r2[:, :nt], ALU.add, [B_r1, B_r2], [B_qr])
                    P.dma(dst_ap, qr[:, :nt], reads=[B_qr])

                for qk in range(2):
                    if last and cond == 1 and qk == 0:
                        continue
                    dst = QT if qk == 0 else KT
                    for c in range(4):
                        col0 = qk * 512 + c * 128
                        ps, B_ps = pA.next()
                        for k in range(8):
                            mm(ps[:, :nt], win[:, k, col0:col0 + 128], hb[:, k, :nt], k == 0, k == 7, bw(col0, 128) + [B_hb], [B_ps])
                        if cond == 0:
                            qb, B_qb = qbs.next()
                            r1, B_r1 = r1s.next()
                            act(qb[:, :nt], ps[:, :nt], AF.Copy, [B_ps], [B_qb])
                            tt(r1[:, :nt], ps[:, :nt], cs[:, 0, :nt], ALU.mult, [B_ps, B_cs], [B_r1])
                            rope_tail()
                            prev[0] = (dst[c, :, tok0:tok0 + nt], qb, B_qb, r1, B_r1)
                        else:
                            qr, B_qr = qrs.next()
                            act(qr[:, :nt], ps[:, :nt], AF.Copy, [B_ps], [B_qr])
                            P.dma(dst[c, :, tok0:tok0 + nt], qr[:, :nt], reads=[B_qr])
                        if qk == 1 and ti + 1 < len(tiles512):
                            p1_square(ti + 1, part=c)
                if ti + 1 < len(tiles512):
                    p1_norm(ti + 1)
                rope_tail()
                for s in range(ns):
                    ps, B_ps = pA.next()
                    for k in range(8):
                        mm(ps[:, :], hb[:, k, s * 128:(s + 1) * 128], win[:, k, 1024:1536], k == 0, k == 7, bw(1024, 512) + [B_hb], [B_ps])
                    evi[0] += 1
                    cpy(vb[:, s, :, 0:128], ps[:, :].rearrange("p (h e) -> p h e", h=4), [B_ps], [B_vb],
                        eng=("act" if evi[0] % 2 else "dve"))
                P.dma(VV[tok0:tok0 + nt].rearrange("(s p) h e -> p s h e", p=128), vb[:, :ns], reads=[B_vb])
                vm_todo = []
                if full:
                    for s in range(ns):
                        ps, B_ps = pA.next()
                        for k in range(8):
                            mm(ps[:, 0:256], hb[:, k, s * 128:(s + 1) * 128], win[:, k, 1792:2048], k == 0, k == 7, bw(1792, 256) + [B_hb], [B_ps])
                        gvf, B_gvf = gvfs.next()
                        sqg, B_sqg = sqgs.next()
                        ssg, B_ssg = ssgs.next()
                        lng, B_lng = lngs.next()
                        rsg, B_rsg = rsgs.next()
                        act(gvf[:], ps[:, 0:256], AF.Copy, [B_ps], [B_gvf])
                        tt(sqg[:], gvf[:], gvf[:], ALU.mult, [B_gvf], [B_sqg])
                        red(ssg[:], sqg[:].rearrange("p (g c) -> p g c", g=4), [B_sqg], [B_ssg])
                        rsqrt_chain(rsg[:], ssg[:], 1.0 / 64, lng[:], B_lng, [B_ssg], B_rsg)
                        vn, B_vn = vns.next()
                        tt(vn[:], gvf[:].rearrange("p (g c) -> p g c", g=4), rsg[:].unsqueeze(2).to_broadcast([128, 4, 64]),
                           ALU.mult, [B_gvf, B_rsg], [B_vn])
                        vm_todo.append((s, vn, B_vn))
                if not full:
                    continue
                for c2 in range(2):
                    col0 = 1536 + c2 * 128
                    ps, B_ps = pA.next()
                    for k in range(8):
                        mm(ps[:, :nt], win[:, k, col0:col0 + 128], hb[:, k, :nt], k == 0, k == 7, bw(col0, 128) + [B_hb], [B_ps])
                    act(usb[:, c2, :nt], ps[:, :nt], AF.Copy, [B_ps], [B_usb])
                for c2 in range(2):
                    col0 = 2048 + c2 * 128
                    ps, B_ps = pA.next()
                    for k in range(8):
                        mm(ps[:, :nt], win[:, k, col0:col0 + 128], hb[:, k, :nt], k == 0, k == 7, bw(col0, 128) + [B_hb], [B_ps])
                    act(pf[:, c2, :nt], ps[:, :nt], AF.Copy, [B_ps], [B_pf])
                P.dma(PT.rearrange("c p t -> p c t")[:, :, tok0:tok0 + nt], pf[:, :, :nt], reads=[B_pf])
                for (s, vn, B_vn) in vm_todo:
                    for g in range(4):
                        gl_, c2 = g % 2, g // 2
                        mm(pVM[gl_ * 64:(gl_ + 1) * 64, c2, s * 128:(s + 1) * 128], vn[:, g, :], wsb[:, g, :], True, True,
                           [B_vn, B_wsb], [B_pVM])
                for c2 in range(2):
                    stt(mbt[:, :nt], pVM[:, c2, :nt], gvn[:, l, c2:c2 + 1], bst[:, c2, :nt], ALU.mult, ALU.add,
                        [B_pVM, B_gvn, B_bst], [B_mbt])
                    tt(mbb[:, c2, :nt], mbt[:, :nt], usb[:, c2, :nt], ALU.mult, [B_mbt, B_usb], [B_mbb])
                P.dma(MIX[4:6].rearrange("c p t -> p c t")[:, :, tok0:tok0 + nt], mbb[:, :, :nt], reads=[B_mbb])
            P.barrier()
            if dbg:
                print('ops@', l, Prog.count)

        with (ExitStack() if l * 5 + 1 <= stop_step else _Skip()) as st:
            xpA = sbt(st, "xp", [128, PADL])
            xpB = sbt(st, "xpB", [128, PADL])
            icB = sbt(st, "icB", [128, PADL])
            a2, B_a2 = sbt(st, "a2", [128, PADL])
            s4, B_s4 = sbt(st, "s4", [128, PADL])
            s8, B_s8 = sbt(st, "s8", [128, PADL])
            s16, B_s16 = sbt(st, "s16", [128, PADL])
            ic, B_ic = sbt(st, "ic", [128, PADL])
            dt_, B_dt = sbt(st, "dt", [128, PADL])
            dbf, B_dbf = sbt(st, "dbf", [128, PADL], BF16)
            wpf, B_wpf = sbt(st, "wpf", [128, 2, 128])
            wpb, B_wpb = sbt(st, "wpb", [128, 2, 128], BF16)
            mcs = Rot([sbt(st, f"mc{i}", [128, 512], BF16) for i in range(2)])
            pA = Rot([pst(st, f"pbA{i}", [128, 512]) for i in range(2)])
            memset(wpf[:], 0.0, [B_wpf])
            for g in range(4):
                gl_, c2 = g % 2, g // 2
                P.dma(wpf[gl_ * 64:(gl_ + 1) * 64, c2, gl_ * 64:(gl_ + 1) * 64], wpool_in[l, g], writes=[B_wpf])
            cpy(wpb[:], wpf[:], [B_wpf], [B_wpb])
            Lp = PADL
            icA = (ic, B_ic)
            for c2 in range(2):
                xp, B_xp = (xpA, xpB)[c2]
                ic_, B_ic_ = (icA, icB)[c2]
                memset(xp[:, 0:8], 0.0, [B_xp])
                memset(xp[:, 4104:4120], 0.0, [B_xp])
                memset(xp[:, 4376:4384], 0.0, [B_xp])
                P.dma(xp[:, 8:8 + T], PT[c2, :, 0:T], writes=[B_xp], eng=("sp" if c2 == 0 else "pool"))
                P.dma(xp[:, 4120:4120 + TC], PT[c2, :, T:TT], writes=[B_xp], eng=("sp" if c2 == 0 else "pool"))
                P.dma(ic_[:], ic_in[:, c2, :], writes=[B_ic_], eng=("sp" if c2 == 0 else "pool"))
            for c2 in range(2):
                xp, B_xp = (xpA, xpB)[c2]
                ic, B_ic = (icA, icB)[c2]
                tt(a2[:, 1:Lp], xp[:, 0:Lp - 1], xp[:, 1:Lp], ALU.add, [B_xp], [B_a2])
                if c2 == 0:
                    tt(s4[64:128, 2:Lp - 1], a2[64:128, 1:Lp - 2], a2[64:128, 3:Lp], ALU.add, [B_a2], [B_s4])
                    srcs = [(a2, B_a2), (s4, B_s4)]
                else:
                    tt(s4[:, 2:Lp - 1], a2[:, 1:Lp - 2], a2[:, 3:Lp], ALU.add, [B_a2], [B_s4])
                    tt(s8[:, 4:Lp - 3], s4[:, 2:Lp - 5], s4[:, 6:Lp - 1], ALU.add, [B_s4], [B_s8])
                    tt(s16[64:128, 8:Lp - 7], s8[64:128, 4:Lp - 11], s8[64:128, 12:Lp - 3], ALU.add, [B_s8], [B_s16])
                    srcs = [(s8, B_s8), (s16, B_s16)]
                for gl_ in range(2):
                    s_t, s_b = srcs[gl_]
                    pr_ = slice(gl_ * 64, (gl_ + 1) * 64)
                    tt(dt_[pr_, 8:Lp - 8], s_t[pr_, 8:Lp - 8], ic[pr_, 8:Lp - 8], ALU.mult, [s_b, B_ic], [B_dt])
                tt(dbf[:, 8:Lp - 8], dt_[:, 8:Lp - 8], xp[:, 8:Lp - 8], ALU.subtract, [B_dt, B_xp], [B_dbf])
                for ti, (tok0, nt, cond) in enumerate(tiles512):
                    if last and cond == 1:
                        continue
                    col = 8 + tok0 if cond == 0 else 4120 + (tok0 - T)
                    ps, B_ps = pA.next()
                    mm(ps[:, :nt], wpb[:, c2, :], dbf[:, col:col + nt], True, True, [B_wpb, B_dbf], [B_ps])
                    mc, B_mc = mcs.next()
                    act(mc[:, :nt], ps[:, :nt], AF.Identity, [B_ps, B_spool], [B_mc], scale=spool[:, l, c2:c2 + 1])
                    P.dma(MIX[6 + c2, :, tok0:tok0 + nt], mc[:, :nt], reads=[B_mc])
            P.barrier()
            if dbg:
                print('ops@', l, Prog.count)

        sawm = ExitStack()
        w1b, B_w1 = sbt(sawm, "w1b", [128, 8, 4 * D], BF16)
        w1_pieces = [(k, nq) for k in range(8) for nq in range(4)]

        wout, B_wout = sbt(sawm, "wout", [128, 8, D], BF16)
        wout_pieces = list(range(8))

        def wout_piece(stg_rot):
            pc = wout_pieces.pop(0)
            s_t, s_b = stg_rot.next()
            P.dma(s_t[:].rearrange("p (k n) -> p k n", k=8),
                  wout_in[l].rearrange("(k p) n -> p k n", p=128)[:, :, pc * 128:(pc + 1) * 128], writes=[s_b], eng="pool")
            cpy(wout[:, :, pc * 128:(pc + 1) * 128], s_t[:].rearrange("p (k n) -> p k n", k=8), [s_b], [B_wout], eng="dve")

        def w1_piece(stg_rot):
            if wout_pieces:
                wout_piece(stg_rot)
                return
            k, nq = w1_pieces.pop(0)
            s_t, s_b = stg_rot.next()
            P.dma(s_t[:], w1_in[l, k * 128:(k + 1) * 128, nq * 1024:(nq + 1) * 1024], writes=[s_b], eng="pool")
            cpy(w1b[:, k, nq * 1024:(nq + 1) * 1024], s_t[:], [s_b], [B_w1], eng="dve")

        with (ExitStack() if l * 5 + 2 <= stop_step else _Skip()) as st:
            stgA = Rot([sbt(st, f"stgA{i}", [128, 1024]) for i in range(2)])
            kt_sb, B_kt = sbt(st, "kt_sb", [128, 4, TT], BF16)
            v_sb, B_v = sbt(st, "v_sb", [128, NKT, 4, 130], BF16)
            for h in range(4):
                P.dma(kt_sb[:, h, :], KT[h], writes=[Buf()], key=B_kt, eng=("sp" if h % 2 == 0 else "pool"))
            for g4 in range(0, NKT, 6):
                n4 = min(6, NKT - g4)
                P.dma(v_sb[:, g4:g4 + n4], VV[g4 * 128:(g4 + n4) * 128].rearrange("(k p) h e -> p k h e", p=128),
                      writes=[Buf()], key=B_v, eng=("pool" if (g4 // 6) % 2 == 0 else "sp"))
            P.barrier()
            qbl = [sbt(st, f"qbl{i}", [128, 512], BF16) for i in range(2)]
            Es = [sbt(st, f"E{i}", [128, 2, 512], BF16) for i in range(3)]
            sps = [pst(st, f"sps{i}", [128, 2, 512]) for i in range(2)]
            accb = [pst(st, "accA", [128, 512]), pst(st, "accB", [128, 512]), pst(st, "accC", [128, 512])]
            pT, B_pT = pst(st, "pT", [128, 4, 128], BF16)
            accS, B_accS = sbt(st, "accS", [128, 8, 130])
            rz8, B_rz = sbt(st, "rz8", [128, 8, 1])
            nrz4, B_nrz = sbt(st, "nrz4", [128, 4, 1])
            o04, B_o0 = sbt(st, "o04", [128, 4, 128])
            t14, B_t1 = sbt(st, "t14", [128, 4, 128])
            oo4, B_oo = sbt(st, "oo4", [128, 4, 128])
            osq4, B_osq = t14, B_t1
            oss4, B_oss = sbt(st, "oss4", [128, 4])
            oln4, B_oln = sbt(st, "oln4", [128, 4])
            ors4, B_ors = sbt(st, "ors4", [128, 4, 1])
            abt, B_abt = o04, B_o0
            ab4s = [sbt(st, f"ab4{i}", [128, 4, 128], BF16) for i in range(2)]
            aTs = Rot([sbt(st, f"aT{i}", [128, 512], BF16) for i in range(2)])
            do_mod = (l + 1 < n_layers)
            if do_mod:
                scb2, B_scb2 = sbt(st, "scb2", [128, 2, D])
                bada2, B_bada2 = sbt(st, "bada2", [128, L, 48])
                P.dma(scb2[:], SCB, writes=[B_scb2])
                P.dma(bada2[:], bada_in, writes=[B_bada2])
                wrot2 = Rot([sbt(st, f"wst2{i}", [128, D]) for i in range(2)])
                prod2, B_prod2 = sbt(st, "prod2", [128, 1, D])
                redt2, B_redt2 = sbt(st, "redt2", [128, 48, 2])
            mod_j = [0]
            pending = []
            blocks = []
            for h in range(4):
                for qb_i in range(8):
                    blocks.append((h, qb_i * 512, 512, list(range(NKT))))
                if not last:
                    blocks.append((h, T, TC, [32, 33]))
            for bi, (h, q0, qn, kts) in enumerate(blocks):
                qt_, B_q = qbl[bi % 2]
                P.dma(qt_[:, :qn], QT[h, :, q0:q0 + qn], writes=[B_q])
                nqs = qn // 128

                def qk(i, kt):
                    sp, B_sp = sps[i % 2]
                    for j in range(2):
                        pr_ = slice(j * 64, (j + 1) * 64)
                        mm(sp[:, j, :qn], kt_sb[pr_, h, kt * 128:(kt + 1) * 128], qt_[pr_, :qn], True, True, [B_kt, B_q], [B_sp])
                    E, B_E = Es[i % 3]
                    act(E[:, :, :qn], sp[:, :, :qn], AF.Exp, [B_sp], [B_E], scale=0.125)

                def pv(i, kt):
                    E, B_E = Es[i % 3]
                    banks_started = set()
                    for j in range(2):
                        for qs in range(nqs):
                            idx = j * 4 + qs
                            bk, slot = idx // 3, idx % 3
                            a_t, a_b = accb[bk]
                            first_in_bank = (i == 0 and bk not in banks_started)
                            banks_started.add(bk)
                            mm(a_t[:, slot * 130:slot * 130 + 129], E[:, j, qs * 128:(qs + 1) * 128], v_sb[:, kt, h, 0:129],
                               first_in_bank, i == len(kts) - 1, [B_E, B_v], [a_b], skip=True)

                qk(0, kts[0])
                if len(kts) > 1:
                    qk(1, kts[1])
                for i, kt in enumerate(kts):
                    if i + 2 < len(kts):
                        qk(i + 2, kts[i + 2])
                    pv(i, kt)
                    while pending and pending[0][0] <= i:
                        pending.pop(0)[1]()
                while pending:
                    pending.pop(0)[1]()
                used = sorted(set((j * 4 + qs) // 3 for j in range(2) for qs in range(nqs)))
                for bk in used:
                    a_t, a_b = accb[bk]
                    nsl = 3 if bk < 2 else 2
                    cpy(accS[:, bk * 3:bk * 3 + nsl, :], a_t[:, 0:nsl * 130].rearrange("p (s c) -> p s c", c=130), [a_b], [B_accS],
                        eng="dve")
                P.op("dve", lambda e: e.reciprocal(out=rz8[:], in_=accS[:, :, 128:129]), [B_accS], [B_rz])
                ts(nrz4[:, :nqs], rz8[:, 4:4 + nqs], nlam[:, l:l + 1], None, ALU.mult, None, [B_rz, B_nlam], [B_nrz])
                tt(o04[:, :nqs], accS[:, 0:nqs, 0:128], rz8[:, 0:nqs].to_broadcast([128, nqs, 128]), ALU.mult, [B_accS, B_rz], [B_o0])
                tt(t14[:, :nqs], accS[:, 4:4 + nqs, 0:128], nrz4[:, :nqs].to_broadcast([128, nqs, 128]), ALU.mult,
                   [B_accS, B_nrz], [B_t1])
                tt(oo4[:, :nqs], o04[:, :nqs], t14[:, :nqs], ALU.add, [B_o0, B_t1], [B_oo])
                tt(osq4[:, :nqs], oo4[:, :nqs], oo4[:, :nqs], ALU.mult, [B_oo], [B_osq])
                red(oss4[:, :nqs], osq4[:, :nqs], [B_osq], [B_oss])
                ab4, B_ab = ab4s[bi % 2]
                aT, B_aT = aTs.next()

                def stage2(nqs=nqs):
                    rsqrt_chain(ors4[:, :nqs, 0], oss4[:, :nqs], 1.0 / 128, oln4[:, :nqs], B_oln, [B_oss], B_ors)

                def stage3(nqs=nqs, ab4=ab4, B_ab=B_ab):
                    tt(abt[:, :nqs], oo4[:, :nqs], ors4[:, :nqs].to_broadcast([128, nqs, 128]), ALU.mult, [B_oo, B_ors], [B_abt])
                    tt(ab4[:, :nqs], abt[:, :nqs], GS[:, l, :].unsqueeze(1).to_broadcast([128, nqs, 128]), ALU.mult,
                       [B_abt, B_GS], [B_ab])

                def stage4(nqs=nqs, ab4=ab4, B_ab=B_ab):
                    for qs in range(nqs):
                        transpose(pT[:, qs, :], ab4[:, qs, :], ident_bf[:], [B_ab, B_ident], [B_pT])

                def stage5(nqs=nqs, qn=qn, h=h, q0=q0, aT=aT, B_aT=B_aT):
                    cpy(aT[:, :qn], pT[:, 0:nqs, :].rearrange("p a b -> p (a b)"), [B_pT], [B_aT], eng="dve")
                    P.dma(MIX[h, :, q0:q0 + qn], aT[:, :qn], reads=[B_aT])

                pending = [(5, stage2), (10, stage3), (24, stage4), (29, stage5)]
                near_ctx = (qn != 512) or (bi + 1 < len(blocks) and blocks[bi + 1][2] != 512)
                if near_ctx:
                    continue
                if w1_pieces:
                    w1_piece(stgA)
                    if bi % 4 == 0 and w1_pieces:
                        w1_piece(stgA)
                if do_mod:
                    for _ in range(2 if mod_j[0] >= 2 * bi else 3):
                        if mod_j[0] < 48:
                            mod_chunk(l + 1, mod_j[0], scb2, B_scb2, wrot2, prod2, B_prod2, redt2, B_redt2)
                            mod_j[0] += 1
            while pending:
                pending.pop(0)[1]()
            while w1_pieces or wout_pieces:
                w1_piece(stgA)
            if do_mod:
                while mod_j[0] < 48:
                    mod_chunk(l + 1, mod_j[0], scb2, B_scb2, wrot2, prod2, B_prod2, redt2, B_redt2)
                    mod_j[0] += 1
                mod_finish(l + 1, redt2, B_redt2, bada2, B_bada2)
            P.barrier()
            if dbg:
                print('ops@', l, Prog.count)

        tiles256 = [(i * 256, 256, 0) for i in range(16)] + ([] if last else [(T, TC, 1)])
        with (ExitStack() if l * 5 + 3 <= stop_step else _Skip()) as st:
            w2b, B_w2 = sbt(st, "w2b", [128, 32, D], BF16)
            wstg = Rot([sbt(st, f"wstg{i}", [128, 1024]) for i in range(2)])
            xs2 = [sbt(st, f"wxs{i}", [128, 8, 256]) for i in range(2)]
            bfA = sbt(st, "bfA", [128, 8, 256], BF16)
            bfB = sbt(st, "bfB", [128, 8, 256], BF16)
            mx2 = [bfA, bfB]
            lnb, B_ln = sbt(st, "mlnb", [128, 256])
            rstd, B_rstd = sbt(st, "mrstd", [128, 256])
            tmps = Rot([sbt(st, f"mtmp{i}", [128, 256]) for i in range(2)])
            hid, B_hid = sbt(st, "hid", [128, 32, 256], BF16)
            r32s = Rot([sbt(st, f"r32{i}", [128, 256]) for i in range(2)])
            pA = Rot([pst(st, f"pmA{i}", [128, 512]) for i in range(3)])
            pB = Rot([pst(st, f"pmB{i}", [128, 512]) for i in range(2)])
            XTb = [Buf(f"XTb{i}") for i in range(len(tiles256))]
            w2_f = [0]

            def w2_piece():
                f = w2_f[0]
                w2_f[0] += 1
                s_t, s_b = wstg.next()
                P.dma(s_t[:], w2_in[l, f * 128:(f + 1) * 128, :], writes=[s_b], eng="pool")
                convert(w2b[:, f, :], s_t[:], [s_b], [B_w2], engs=("act", "dve"))

            sq, B_sq = bfA
            bfC = sbt(st, "bfC", [128, 8, 256], BF16)
            h2s = [bfB, bfC]

            def m_norm(ti):
                tok0, nt, cond = tiles256[ti]
                xn, B_xn = xs2[ti % 2]
                h2, B_h2 = h2s[ti % 2]
                mx, B_mx = bfA
                for co in range(8):
                    ps, B_ps = pA.next()
                    for k in range(8):
                        mm(ps[:, :nt], wout[:, k, co * 128:(co + 1) * 128], mx[:, k, :], k == 0, k == 7, [B_wout, B_mx], [B_ps])
                    stt(xn[:, co, :], ps[:, :nt], mod[:, l, 16 + co, cond:cond + 1], xn[:, co, :], ALU.mult, ALU.add,
                        [B_ps, B_mod, B_xn], [B_xn])
                pss, B_pss = pA.next()
                norm_mod((sq, B_sq, lnb, B_ln, rstd, B_rstd, tmps), xn, B_xn, nt,
                         lambda k: gs2[:, l, k, cond:cond + 1], lambda k: mod[:, l, 24 + k, cond:cond + 1],
                         h2, B_h2, pss, B_pss)

            def m_load(ti):
                tok0, nt, cond = tiles256[ti]
                xn, B_xn = xs2[ti % 2]
                mx, B_mx = bfA
                P.dma(xn[:], x_src(l, tok0, nt), writes=[B_xn])
                P.dma(mx[:], MIX.rearrange("c p t -> p c t")[:, :, tok0:tok0 + nt], writes=[B_mx])

            m_load(0)
            m_norm(0)
            while w2_f[0] < 32:
                w2_piece()
            for ti, (tok0, nt, cond) in enumerate(tiles256):
                xn, B_xn = xs2[ti % 2]
                h2, B_h2 = h2s[ti % 2]
                if ti + 1 < len(tiles256):
                    m_load(ti + 1)
                for f in range(32):
                    ps, B_ps = pA.next()
                    for k in range(8):
                        mm(ps[:, :nt], w1b[:, k, f * 128:(f + 1) * 128], h2[:, k, :], k == 0, k == 7, [B_w1, B_h2], [B_ps])
                    r32, B_r32 = r32s.next()
                    act(r32[:], ps[:, :nt], AF.Relu, [B_ps], [B_r32])
                    tt(hid[:, f, :], r32[:], r32[:], ALU.mult, [B_r32], [B_hid])
                if ti + 1 < len(tiles256):
                    m_norm(ti + 1)
                for co in range(8):
                    ps, B_ps = pB.next()
                    for f in range(32):
                        mm(ps[:, :nt], w2b[:, f, co * 128:(co + 1) * 128], hid[:, f, :], f == 0, f == 31, [B_w2, B_hid], [B_ps])
                    stt(xn[:, co, :], ps[:, :nt], mod[:, l, 40 + co, cond:cond + 1], xn[:, co, :], ALU.mult, ALU.add,
                        [B_ps, B_mod, B_xn], [B_xn])
                if not last:
                    P.dma(x_dst(tok0, nt), xn[:], reads=[B_xn])
                else:
                    act(sq[:], xn[:], AF.Square, [B_xn], [B_sq])
                    pss, B_pss = pA.next()
                    for k in range(8):
                        mm(pss[:, :nt], ones_bf[:], sq[:, k, :], k == 0, k == 7, [B_ones, B_sq], [B_pss])
                    rsqrt_chain(rstd[:], pss[:, :nt], 1.0 / D, lnb[:], B_ln, [B_pss], B_rstd)
                    for k in range(8):
                        stt(xn[:, k, :], xn[:, k, :], gf[:, k:k + 1], rstd[:], ALU.mult, ALU.mult, [B_xn, B_gf, B_rstd], [B_xn])
                    P.dma(yT_out.rearrange("(k p) t -> p k t", p=128)[:, :, tok0:tok0 + nt], xn[:], reads=[B_xn], is_output=True)
            P.barrier()
            if dbg:
                print('ops@', l, Prog.count)
        sawm.close()

    P.emit(top)
    top.close()
    return nc


def _const_tables():
    grid_w = 64
    n_freq = 16
    t = np.arange(T)
    row = (t // grid_w).astype(np.float32)
    col = (t % grid_w).astype(np.float32)
    inv = (np.float32(10000.0) ** (-np.arange(n_freq, dtype=np.float32) / np.float32(n_freq))).astype(np.float32)
    cosT = np.zeros((128, T), np.float32)
    sinT = np.zeros((128, T), np.float32)
    rsw = np.zeros((128, 128), np.float32)
    for p in range(128):
        axis = (p % 64) // 32
        half = (p % 32) // 16
        f = p % 16
        ang = ((row if axis == 0 else col) * inv[f]).astype(np.float32)
        cosT[p] = np.cos(ang)
        sinT[p] = np.sin(ang) * (-1.0 if half == 0 else 1.0)
        partner = p + 16 if half == 0 else p - 16
        rsw[partner, p] = 1.0
    ic = np.zeros((128, 2, PADL), np.float32)
    wins = (2, 4, 8, 16)
    for g, w in enumerate(wins):
        gl_, c2 = g % 2, g // 2
        for (tseg, base) in ((T, 8), (TC, 4120)):
            pos = np.arange(tseg)
            lo = np.clip(pos - w // 2, 0, tseg)
            hi = np.clip(pos + (w - w // 2), 0, tseg)
            ic[gl_ * 64:(gl_ + 1) * 64, c2, base:base + tseg] = (1.0 / (hi - lo).astype(np.float32))[None, :]
    return cosT, sinT, rsw, np.eye(128, dtype=np.float32), ic


_NC_CACHE = {}


def _pvec(v, nch):
    v = np.asarray(v, np.float32)
    lead = v.shape[:-1]
    return np.ascontiguousarray(np.moveaxis(v.reshape(lead + (nch, 128)), -1, 0))


def kernel(x, c, ctx, c_ctx, w_ada, b_ada, g_norm_mix, g_norm_mlp, w_in, lam_q1, lam_k1, lam_q2, lam_k2,
           g_subln, g_vnorm, w_spatial, b_spatial, w_pool, s_pool, w_out, w1, w2, g_final):
    f = lambda a: np.ascontiguousarray(np.asarray(a, np.float32))
    x, c, ctx, c_ctx = f(x), f(c), f(ctx), f(c_ctx)
    n = 8
    if "nc" not in _NC_CACHE:
        _NC_CACHE["nc"] = build_program()
    nc = _NC_CACHE["nc"]
    cosT, sinT, rsw, ident, ic = _const_tables()
    w_ada = f(w_ada)
    wadaT = np.ascontiguousarray(w_ada.reshape(L, D, 48, 128).transpose(0, 3, 2, 1))
    b_sp = f(b_spatial)
    bsT = np.zeros((L, 128, 2, 512), np.float32)
    for g in range(4):
        gl_, c2 = g % 2, g // 2
        bsT[:, gl_ * 64:(gl_ + 1) * 64, c2, :] = np.tile(b_sp[:, g, :], (1, 4))[:, None, :]
    lamv = np.stack([f(lam_q1), f(lam_k1), f(lam_q2), f(lam_k2)], axis=1)
    shared = {
        "wadaT": wadaT,
        "bada": _pvec(f(b_ada), 48),
        "gm": _pvec(f(g_norm_mix), 8),
        "gl": _pvec(f(g_norm_mlp), 8),
        "gf": _pvec(f(g_final), 8),
        "w_in": f(w_in),
        "lamv": np.ascontiguousarray(np.broadcast_to(lamv[None], (128, L, 4, 64))),
        "gsub": np.ascontiguousarray(np.broadcast_to(f(g_subln)[None], (128, L, 128))),
        "gvn": _pvec(f(g_vnorm), 2),
        "wsT": np.ascontiguousarray(f(w_spatial).transpose(0, 3, 1, 2)),
        "bsT": bsT,
        "w_pool": f(w_pool),
        "spool": _pvec(f(s_pool), 2),
        "w_out": f(w_out),
        "w1": f(w1),
        "w2": f(w2),
        "cosT": cosT, "sinT": sinT, "rsw": rsw, "ident": ident, "icT": ic,
    }
    in_maps = []
    for b in range(n):
        m = dict(shared)
        m["xT"] = np.ascontiguousarray(x[b].T)
        m["ctxT"] = np.ascontiguousarray(ctx[b].T)
        cbv = np.stack([c[b], c_ctx], axis=0)
        m["cb"] = np.ascontiguousarray(np.broadcast_to(cbv[None], (128, 2, D)))
        in_maps.append(m)
    if _NC_CACHE.get("return_maps"):
        return in_maps
    res = run_bass_kernel_spmd(nc, in_maps, core_ids=list(range(n)))
    out = np.stack([np.ascontiguousarray(res.results[b]["yT"].T) for b in range(n)], axis=0)
    return out.astype(np.float32)
```

```python
import math
from contextlib import ExitStack
import numpy as np
import concourse.bass as bass
import concourse.mybir as mybir
from concourse.bass_utils import run_bass_kernel_spmd

F32 = mybir.dt.float32
BF16 = mybir.dt.bfloat16
AF = mybir.ActivationFunctionType
ALU = mybir.AluOpType
AX = mybir.AxisListType

D = 1024
T = 4096
TC = 256
TT = T + TC
L = 4
NKT = TT // 128
EPS = 1e-6
IN_W = 2304
PADL = 4384
SEM_TICK_LIMIT = 30000


class Buf:
    __slots__ = ("name", "last_w", "readers", "excl")

    def __init__(self, name="", excl=False):
        self.name = name
        self.last_w = None
        self.readers = []
        self.excl = excl


class Op:
    __slots__ = ("eng", "fn", "deps", "needs_inc", "sem", "tick", "is_dma", "dma_key")

    def __init__(self, eng, fn, is_dma=False, dma_key=None):
        self.eng = eng
        self.fn = fn
        self.deps = []
        self.needs_inc = False
        self.sem = None
        self.tick = 0
        self.is_dma = is_dma
        self.dma_key = dma_key


class Prog:
    ENGS = ("pe", "act", "dve", "pool", "sp")

    def __init__(self, nc):
        self.nc = nc
        self.q = {e: [] for e in self.ENGS}
        self.out_dmas = []
        self.pending_dmas = []
        self.dma_slots = []
        self.dma_active = {}
        self.dma_free = []

    muted = False
    limit = 10 ** 12
    count = 0

    def _add(self, op, reads, writes):
        Prog.count += 1
        if Prog.muted or Prog.count > Prog.limit:
            return op
        ex = [b for b in reads if b.excl]
        if ex:
            reads = [b for b in reads if not b.excl]
            writes = list(writes) + ex
        deps = set()
        for b in reads:
            if b.last_w is not None:
                deps.add(b.last_w)
        for b in writes:
            if b.last_w is not None:
                deps.add(b.last_w)
            for r in b.readers:
                deps.add(r)
        deps.discard(op)
        for d in deps:
            if d.eng == "pe" and op.eng == "pe" and not d.is_dma and not op.is_dma:
                continue
            d.needs_inc = True
            op.deps.append(d)
        for b in reads:
            b.readers.append(op)
        for b in writes:
            b.last_w = op
            b.readers = []
        self.q[op.eng].append(op)
        return op

    def op(self, eng, fn, reads=(), writes=()):
        return self._add(Op(eng, fn), reads, writes)

    def dma(self, out_ap, in_ap, reads=(), writes=(), key=None, eng="sp", is_output=False):
        def fn(e):
            return e.dma_start(out=out_ap, in_=in_ap)
        if key is None:
            key = writes[0] if writes else reads[0]
        o = Op(eng, fn, is_dma=True, dma_key=key)
        self._add(o, reads, writes)
        o.needs_inc = True
        if Prog.muted or Prog.count > Prog.limit:
            return o
        k = id(key)
        if k not in self.dma_active:
            if self.dma_free:
                self.dma_active[k] = self.dma_free.pop()
            else:
                self.dma_slots.append(0)
                self.dma_active[k] = len(self.dma_slots) - 1
        slot = self.dma_active[k]
        self.dma_slots[slot] += 16
        o.sem, o.tick = slot, self.dma_slots[slot]
        self.pending_dmas.append(o)
        if is_output:
            self.out_dmas.append(o)
        return o

    def barrier(self):
        if Prog.muted:
            return
        marks = []
        for e in self.ENGS:
            last = None
            for o in reversed(self.q[e]):
                if not o.is_dma and o.fn is not None:
                    last = o
                    break
            if last is not None:
                last.needs_inc = True
                marks.append(last)
        marks += self.pending_dmas
        self.pending_dmas = []
        for k, slot in self.dma_active.items():
            if self.dma_slots[slot] < 40000:
                self.dma_free.append(slot)
        self.dma_active = {}
        for e in self.ENGS:
            o = Op(e, None)
            o.deps = list(marks)
            self.q[e].append(o)

    def emit(self, stack):
        nc = self.nc

        def new_sem(name):
            return stack.enter_context(nc.semaphore(name))

        for e in self.ENGS:
            cur, cnt, n = None, 0, 0
            for o in self.q[e]:
                if o.is_dma or not o.needs_inc:
                    continue
                if cur is None or cnt >= SEM_TICK_LIMIT:
                    cur = new_sem(f"s_{e}_{n}")
                    n += 1
                    cnt = 0
                cnt += 1
                o.sem, o.tick = cur, cnt
        slot_sems = [new_sem(f"s_dma_{i}") for i in range(len(self.dma_slots))]
        for e in self.ENGS:
            for o in self.q[e]:
                if o.is_dma:
                    o.sem = slot_sems[o.sem]
        engmap = {"pe": "tensor", "act": "scalar", "dve": "vector", "pool": "gpsimd", "sp": "sync"}
        block = stack.enter_context(nc.Block())
        for e in self.ENGS:
            def body(eng, ops=self.q[e], e=e):
                waited = {}

                def wait(d):
                    k = id(d.sem)
                    if waited.get(k, 0) >= d.tick:
                        return
                    waited[k] = d.tick
                    eng.wait_ge(d.sem, d.tick)
                for o in ops:
                    for d in o.deps:
                        wait(d)
                    if o.fn is None:
                        continue
                    ins = o.fn(eng)
                    if o.needs_inc:
                        ins.then_inc(o.sem, 16 if o.is_dma else 1)
                if e == "sp":
                    for d in self.out_dmas:
                        wait(d)
            getattr(block, engmap[e])(body)


class _Skip:
    cur = None

    def __enter__(self):
        self.st = ExitStack()
        Prog.muted = True
        return self.st

    def __exit__(self, *a):
        Prog.muted = False
        self.st.close()
        return False


class Rot:
    def __init__(self, items):
        self.items = items
        self.i = 0

    def next(self):
        it = self.items[self.i % len(self.items)]
        self.i += 1
        return it


def lam_init(l):
    return 0.8 - 0.6 * math.exp(-0.3 * l)


def build_program(n_layers=L, dbg=False, stop_step=10 ** 9):
    nc = bass.Bass("TRN2", target_bir_lowering=False)
    P = Prog(nc)
    top = ExitStack()

    def din(name, shape, dt=F32):
        return nc.dram_tensor(name, list(shape), dt, kind="ExternalInput").ap()

    def dscr(name, shape, dt):
        kind = "ExternalOutput" if dbg else "Internal"
        return nc.dram_tensor(name, list(shape), dt, kind=kind).ap()

    xT_in = din("xT", [D, T])
    ctxT_in = din("ctxT", [D, TC])
    cb_in = din("cb", [128, 2, D])
    wadaT_in = din("wadaT", [L, 128, 48, D])
    bada_in = din("bada", [128, L, 48])
    gm_in = din("gm", [128, L, 8])
    gl_in = din("gl", [128, L, 8])
    gf_in = din("gf", [128, 8])
    win_in = din("w_in", [L, D, IN_W])
    lamv_in = din("lamv", [128, L, 4, 64])
    gsub_in = din("gsub", [128, L, 128])
    gvn_in = din("gvn", [128, L, 2])
    wsT_in = din("wsT", [L, 128, 4, 128])
    bsT_in = din("bsT", [L, 128, 2, 512])
    wpool_in = din("w_pool", [L, 4, 64, 64])
    spool_in = din("spool", [128, L, 2])
    wout_in = din("w_out", [L, D, D])
    w1_in = din("w1", [L, D, 4 * D])
    w2_in = din("w2", [L, 4 * D, D])
    cosT_in = din("cosT", [128, T])
    sinT_in = din("sinT", [128, T])
    rsw_in = din("rsw", [128, 128])
    ident_in = din("ident", [128, 128])
    ic_in = din("icT", [128, 2, PADL])
    yT_out = nc.dram_tensor("yT", [D, T], F32, kind="ExternalOutput").ap()

    XT = dscr("XT", [8, 128, TT], F32)
    QT = dscr("QT", [4, 128, TT], BF16)
    KT = dscr("KT", [4, 128, TT], BF16)
    VV = dscr("VV", [TT, 4, 130], BF16)
    PT = dscr("PT", [2, 128, TT], F32)
    MIX = dscr("MIX", [8, 128, TT], BF16)
    SCB = dscr("SCB", [128, 2, D], F32)

    uid = [0]

    def sbt(st, name, shape, dt=F32):
        uid[0] += 1
        t = st.enter_context(nc.sbuf_tensor(f"sb{uid[0]}_{name}", list(shape), dt))
        return t, Buf(name)

    def pst(st, name, shape, dt=F32):
        uid[0] += 1
        t = st.enter_context(nc.psum_tensor(f"ps{uid[0]}_{name}", list(shape), dt))
        return t, Buf(name, excl=True)

    def act(out, in_, func, reads, writes, **kw):
        P.op("act", lambda e: e.activation(out=out, in_=in_, func=func, **kw), reads, writes)

    def tt(out, in0, in1, op, reads, writes, eng="dve"):
        P.op(eng, lambda e: e.tensor_tensor(out=out, in0=in0, in1=in1, op=op), reads, writes)

    def ts(out, in0, s1, s2, op0, op1, reads, writes, eng="dve"):
        if s2 is None:
            P.op(eng, lambda e: e.tensor_scalar(out=out, in0=in0, scalar1=s1, scalar2=None, op0=op0), reads, writes)
        else:
            P.op(eng, lambda e: e.tensor_scalar(out=out, in0=in0, scalar1=s1, scalar2=s2, op0=op0, op1=op1), reads, writes)

    def stt(out, in0, scalar, in1, op0, op1, reads, writes, eng="dve"):
        P.op(eng, lambda e: e.scalar_tensor_tensor(out=out, in0=in0, scalar=scalar, in1=in1, op0=op0, op1=op1), reads, writes)

    def red(out, in_, reads, writes, eng="dve"):
        P.op(eng, lambda e: e.tensor_reduce(out=out, in_=in_, axis=AX.X, op=ALU.add), reads, writes)

    def cpy(out, in_, reads, writes, eng="dve"):
        if eng == "act":
            act(out, in_, AF.Copy, reads, writes)
        else:
            P.op(eng, lambda e: e.tensor_copy(out=out, in_=in_), reads, writes)

    def memset(ap, val, writes, eng="dve"):
        P.op(eng, lambda e: e.memset(ap, val), (), writes)

    def mm(out, lhsT, rhs, start, stop, reads, writes, skip=False):
        if skip:
            P.op("pe", lambda e: e.matmul(out, lhsT=lhsT, rhs=rhs, start=start, stop=stop, skip_group_check=True), reads, writes)
        else:
            P.op("pe", lambda e: e.matmul(out, lhsT=lhsT, rhs=rhs, start=start, stop=stop), reads, writes)

    def transpose(out, in_, ident, reads, writes):
        P.op("pe", lambda e: e.transpose(out=out, in_=in_, identity=ident), reads, writes)

    def rsqrt_chain(out, in_, scale, lnbuf, lnB, reads, outB):
        act(lnbuf, in_, AF.Ln, list(reads) + [B_eps], [lnB], scale=scale, bias=epsb[0:lnbuf.shape[0], 0:1])
        act(out, lnbuf, AF.Exp, [lnB], [outB], scale=-0.5)

    cvt_i = [0]

    def convert(out, in_, reads, writes, engs=("dve", "act")):
        e = engs[cvt_i[0] % len(engs)]
        cvt_i[0] += 1
        cpy(out, in_, reads, writes, eng=e)

    ident_bf, B_ident = sbt(top, "ident_bf", [128, 128], BF16)
    ones_bf, B_ones = sbt(top, "ones_bf", [128, 128], BF16)
    rsw_bf, B_rsw = sbt(top, "rsw_bf", [128, 128], BF16)
    gm, B_gm = sbt(top, "gm", [128, L, 8])
    gl, B_gl = sbt(top, "gl", [128, L, 8])
    gf, B_gf = sbt(top, "gf", [128, 8])
    spool, B_spool = sbt(top, "spool", [128, L, 2])
    gvn, B_gvn = sbt(top, "gvn", [128, L, 2])
    GS, B_GS = sbt(top, "GS", [128, L, 128])
    lam, B_lam = sbt(top, "lam", [128, L])
    nlam, B_nlam = sbt(top, "nlam", [128, L])
    mod, B_mod = sbt(top, "mod", [128, L, 48, 2])
    gs1, B_gs1 = sbt(top, "gs1", [128, L, 8, 2])
    gs2, B_gs2 = sbt(top, "gs2", [128, L, 8, 2])
    epsb, B_eps = sbt(top, "epsb", [128, 1])

    with ExitStack() as st:
        stg, B_stg = sbt(st, "pro_stg", [128, 2, 128])
        P.dma(stg[:, 0, :], ident_in, writes=[B_stg])
        P.dma(stg[:, 1, :], rsw_in, writes=[B_stg])
        cpy(ident_bf[:], stg[:, 0, :], [B_stg], [B_ident])
        cpy(rsw_bf[:], stg[:, 1, :], [B_stg], [B_rsw])
        memset(ones_bf[:], 1.0, [B_ones])
        memset(epsb[:], EPS, [B_eps])
        P.dma(gm[:], gm_in, writes=[B_gm])
        P.dma(gl[:], gl_in, writes=[B_gl])
        P.dma(gf[:], gf_in, writes=[B_gf])
        P.dma(spool[:], spool_in, writes=[B_spool])
        P.dma(gvn[:], gvn_in, writes=[B_gvn])
        bada, B_bada = sbt(st, "bada", [128, L, 48])
        P.dma(bada[:], bada_in, writes=[B_bada])
        gsub, B_gsub = sbt(st, "gsub", [128, L, 128])
        P.dma(gsub[:], gsub_in, writes=[B_gsub])
        lamv, B_lamv = sbt(st, "lamv", [128, L, 4, 64])
        P.dma(lamv[:], lamv_in, writes=[B_lamv])
        cbt, B_cb = sbt(st, "cbt", [128, 2, D])
        P.dma(cbt[:], cb_in, writes=[B_cb])
        scb, B_scb = sbt(st, "scb", [128, 2, D])
        act(scb[:], cbt[:], AF.Silu, [B_cb], [B_scb])
        lprod, B_lprod = sbt(st, "lprod", [128, L, 2, 64])
        lsum, B_lsum = sbt(st, "lsum", [128, L, 2])
        lexp, B_lexp = sbt(st, "lexp", [128, L, 2])
        for l in range(L):
            for j in range(2):
                tt(lprod[:, l, j, :], lamv[:, l, 2 * j, :], lamv[:, l, 2 * j + 1, :], ALU.mult, [B_lamv], [B_lprod])
        red(lsum[:], lprod[:], [B_lprod], [B_lsum])
        act(lexp[:], lsum[:], AF.Exp, [B_lsum], [B_lexp])
        for l in range(L):
            tt(lam[:, l:l + 1], lexp[:, l, 0:1], lexp[:, l, 1:2], ALU.subtract, [B_lexp], [B_lam])
            ts(lam[:, l:l + 1], lam[:, l:l + 1], float(lam_init(l)), None, ALU.add, None, [B_lam], [B_lam])
            ts(GS[:, l, :], gsub[:, l, :], float(1.0 - lam_init(l)), None, ALU.mult, None, [B_gsub], [B_GS])
        ts(nlam[:], lam[:], -1.0, None, ALU.mult, None, [B_lam], [B_nlam])
        wst = [sbt(st, f"wst{i}", [128, D]) for i in range(4)]
        wrot = Rot(wst)
        prod, B_prod = sbt(st, "prod", [128, 2, D])
        P.dma(SCB, scb[:], reads=[B_scb])

        def mod_chunk(l, j, scb_, B_scb_, wrot_, prod_, B_prod_, redt_, B_redt_):
            w_t, w_b = wrot_.next()
            P.dma(w_t[:], wadaT_in[l, :, j, :], writes=[w_b], eng="pool")
            if prod_.shape[1] == 2:
                tt(prod_[:], scb_[:], w_t[:].unsqueeze(1).to_broadcast([128, 2, D]), ALU.mult, [B_scb_, w_b], [B_prod_])
                red(redt_[:, j, :], prod_[:], [B_prod_], [B_redt_])
            else:
                for c_ in range(2):
                    tt(prod_[:, 0, :], scb_[:, c_, :], w_t[:], ALU.mult, [B_scb_, w_b], [B_prod_])
                    red(redt_[:, j, c_:c_ + 1], prod_[:, 0, :], [B_prod_], [B_redt_])

        def mod_finish(l, redt_, B_redt_, bada_, B_bada_):
            tt(mod[:, l], redt_[:], bada_[:, l, :].unsqueeze(2).to_broadcast([128, 48, 2]), ALU.add,
               [B_redt_, B_bada_], [B_mod])
            stt(gs1[:, l], mod[:, l, 8:16, :], 1.0, gm[:, l, :].unsqueeze(2).to_broadcast([128, 8, 2]), ALU.add, ALU.mult,
                [B_mod, B_gm], [B_gs1])
            stt(gs2[:, l], mod[:, l, 32:40, :], 1.0, gl[:, l, :].unsqueeze(2).to_broadcast([128, 8, 2]), ALU.add, ALU.mult,
                [B_mod, B_gl], [B_gs2])

        redt0, B_redt0 = sbt(st, "redt0", [128, 48, 2])
        memset(redt0[:], 0.0, [B_redt0])
        prodB = sbt(st, "prodB", [128, 2, D])
        prods = Rot([(prod, B_prod), prodB])
        junk, B_junk = sbt(st, "junk", [128, D])
        for j in range(48):
            w_t, w_b = wrot.next()
            P.dma(w_t[:], wadaT_in[0, :, j, :], writes=[w_b], eng="pool")
            pr_, B_pr_ = prods.next()
            tt(pr_[:], scb[:], w_t[:].unsqueeze(1).to_broadcast([128, 2, D]), ALU.mult, [B_scb, w_b], [B_pr_])
            for c_ in range(2):
                act(junk[:], pr_[:, c_, :], AF.Copy, [B_pr_], [B_junk, B_redt0], accum_out=redt0[:, j, c_:c_ + 1])
        mod_finish(0, redt0, B_redt0, bada, B_bada)
        P.barrier()
        if dbg:
            print('ops@pro', Prog.count)

    def x_src(l, tok0, nt):
        if l == 0:
            if tok0 < T:
                return xT_in.rearrange("(k p) t -> p k t", p=128)[:, :, tok0:tok0 + nt]
            return ctxT_in.rearrange("(k p) t -> p k t", p=128)[:, :, tok0 - T:tok0 - T + nt]
        return XT.rearrange("k p t -> p k t")[:, :, tok0:tok0 + nt]

    def x_dst(tok0, nt):
        return XT.rearrange("k p t -> p k t")[:, :, tok0:tok0 + nt]

    def norm_mod(st_bufs, xs, B_xs, nt, gsv, shv, hb, B_hb, pss, B_pss, do_square=True):
        sq, B_sq, lnb, B_ln, rstd, B_rstd, tmps = st_bufs
        if do_square:
            act(sq[:, :, :nt], xs[:, :, :nt], AF.Square, [B_xs], [B_sq])
        for k in range(8):
            mm(pss[:, :nt], ones_bf[:], sq[:, k, :nt], k == 0, k == 7, [B_ones, B_sq], [B_pss])
        rsqrt_chain(rstd[:, :nt], pss[:, :nt], 1.0 / D, lnb[:, :nt], B_ln, [B_pss], B_rstd)
        for k in range(8):
            tm, B_tm = tmps.next()
            stt(tm[:, :nt], xs[:, k, :nt], gsv(k), rstd[:, :nt], ALU.mult, ALU.mult, [B_xs, B_rstd, B_gs1, B_gs2], [B_tm])
            act(hb[:, k, :nt], tm[:, :nt], AF.Identity, [B_tm, B_mod], [B_hb], bias=shv(k))

    tiles512 = [(i * 512, 512, 0) for i in range(8)] + [(T, TC, 1)]
    for l in range(n_layers):
        last = (l == L - 1)
        with (ExitStack() if l * 5 + 0 <= stop_step else _Skip()) as st:
            win, _ = sbt(st, "win", [128, 8, IN_W], BF16)
            Bwin = [Buf(f"win{i}") for i in range(9)]
            wstg = [sbt(st, f"wstg{i}", [128, 8, 256]) for i in range(3)]
            wsr = Rot(wstg)

            def bw(col0, width):
                return Bwin[col0 // 256:(col0 + width - 1) // 256 + 1]

            def load_win():
                for pc in range(9):
                    s_t, s_b = wsr.next()
                    P.dma(s_t[:], win_in[l].rearrange("(k p) n -> p k n", p=128)[:, :, pc * 256:(pc + 1) * 256], writes=[s_b],
                          eng="pool")
                    convert(win[:, :, pc * 256:(pc + 1) * 256], s_t[:], [s_b], [Bwin[pc]])
            wsf, B_wsf = sbt(st, "wsf", [128, 4, 128])
            wsb, B_wsb = sbt(st, "wsb", [128, 4, 128], BF16)
            P.dma(wsf[:], wsT_in[l], writes=[B_wsf])
            cpy(wsb[:], wsf[:], [B_wsf], [B_wsb])
            bst, B_bst = sbt(st, "bst", [128, 2, 512])
            P.dma(bst[:], bsT_in[l], writes=[B_bst])
            xs2 = [sbt(st, f"xs{i}", [128, 8, 512]) for i in range(2)]
            cs2 = [sbt(st, f"cos{i}", [128, 2, 512]) for i in range(2)]
            sq, B_sq = sbt(st, "sq", [128, 8, 512], BF16)
            hb, B_hb = sbt(st, "hb", [128, 8, 512], BF16)
            lnb, B_ln = sbt(st, "lnb", [128, 512])
            rstd, B_rstd = sbt(st, "rstd", [128, 512])
            tmps = Rot([sbt(st, f"tmp{i}", [128, 512]) for i in range(4)])
            qbs = Rot([sbt(st, f"qb{i}", [128, 512], BF16) for i in range(2)])
            r1s = Rot([sbt(st, f"r1{i}", [128, 512]) for i in range(2)])
            r2s = Rot([sbt(st, f"r2{i}", [128, 512]) for i in range(2)])
            qrs = Rot([sbt(st, f"qr{i}", [128, 512], BF16) for i in range(3)])
            vb, B_vb = sbt(st, "vb", [128, 4, 4, 130], BF16)
            memset(vb[:], 1.0, [B_vb])
            pf, B_pf = sbt(st, "pf", [128, 2, 512])
            gvfs = Rot([sbt(st, f"gvf{i}", [128, 256]) for i in range(2)])
            sqgs = Rot([sbt(st, f"sqg{i}", [128, 256]) for i in range(2)])
            ssgs = Rot([sbt(st, f"ssg{i}", [128, 4]) for i in range(2)])
            lngs = Rot([sbt(st, f"lng{i}", [128, 4]) for i in range(2)])
            rsgs = Rot([sbt(st, f"rsg{i}", [128, 4]) for i in range(2)])
            vns = Rot([sbt(st, f"vn{i}", [128, 4, 64], BF16) for i in range(4)])
            mbt, B_mbt = sbt(st, "mbt", [128, 512])
            mbb, B_mbb = sbt(st, "mbb", [128, 2, 512], BF16)
            pA = Rot([pst(st, f"pA{i}", [128, 512]) for i in range(4)])
            pR = Rot([pst(st, f"pR{i}", [128, 512]) for i in range(2)])
            usb, B_usb = sbt(st, "usb", [128, 2, 512])
            pVM, B_pVM = pst(st, "pVM", [128, 2, 512])
            evi = [0]
            if dbg:
                print('  mark tiles', Prog.count)
            hbB = sbt(st, "hbB", [128, 8, 512], BF16)
            hbs = [(hb, B_hb), hbB]

            def p1_load(ti):
                tok0, nt, cond = tiles512[ti]
                xs, B_xs = xs2[ti % 2]
                cs, B_cs = cs2[ti % 2]
                P.dma(xs[:, :, :nt], x_src(l, tok0, nt), writes=[B_xs])
                if cond == 0:
                    P.dma(cs[:, 0, :], cosT_in[:, tok0:tok0 + nt], writes=[B_cs])
                    P.dma(cs[:, 1, :], sinT_in[:, tok0:tok0 + nt], writes=[B_cs])

            def p1_square(ti, part=None):
                tok0, nt, cond = tiles512[ti]
                xs, B_xs = xs2[ti % 2]
                ks = slice(0, 8) if part is None else slice(2 * part, 2 * part + 2)
                act(sq[:, ks, :nt], xs[:, ks, :nt], AF.Square, [B_xs], [B_sq])

            def p1_norm(ti):
                tok0, nt, cond = tiles512[ti]
                xs, B_xs = xs2[ti % 2]
                hb_, B_hb_ = hbs[ti % 2]
                pss, B_pss = pA.next()
                norm_mod((sq, B_sq, lnb, B_ln, rstd, B_rstd, tmps), xs, B_xs, nt,
                         lambda k: gs1[:, l, k, cond:cond + 1], lambda k: mod[:, l, k, cond:cond + 1],
                         hb_, B_hb_, pss, B_pss, do_square=False)

            p1_load(0)
            p1_square(0)
            p1_norm(0)
            load_win()
            for ti, (tok0, nt, cond) in enumerate(tiles512):
                xs, B_xs = xs2[ti % 2]
                cs, B_cs = cs2[ti % 2]
                hb, B_hb = hbs[ti % 2]
                if ti + 1 < len(tiles512):
                    p1_load(ti + 1)
                if dbg and ti == 0:
                    print('  mark qk', Prog.count)
                full = not (last and cond == 1)
                ns = nt // 128
                prev = [None]

                def rope_tail():
                    if prev[0] is None:
                        return
                    (dst_ap, qb, B_qb, r1, B_r1) = prev[0]
                    prev[0] = None
                    r2, B_r2 = r2s.next()
                    pr, B_pr = pR.next()
                    qr, B_qr = qrs.next()
                    mm(pr[:, :nt], rsw_bf[:], qb[:, :nt], True, True, [B_rsw, B_qb], [B_pr])
                    tt(r2[:, :nt], pr[:, :nt], cs[:, 1, :nt], ALU.mult, [B_pr, B_cs], [B_r2])
                    tt(qr[:, :nt], r1[:, :nt], r2[:, :nt], ALU.add, [B_r1, B_r2], [B_qr])
                    P.dma(dst_ap, qr[:, :nt], reads=[B_qr])

                for qk in range(2):
                    if last and cond == 1 and qk == 0:
                        continue
                    dst = QT if qk == 0 else KT
                    for c in range(4):
                        col0 = qk * 512 + c * 128
                        ps, B_ps = pA.next()
                        for k in range(8):
                            mm(ps[:, :nt], win[:, k, col0:col0 + 128], hb[:, k, :nt], k == 0, k == 7, bw(col0, 128) + [B_hb], [B_ps])
                        if cond == 0:
                            qb, B_qb = qbs.next()
                            r1, B_r1 = r1s.next()
                            act(qb[:, :nt], ps[:, :nt], AF.Copy, [B_ps], [B_qb])
                            tt(r1[:, :nt], ps[:, :nt], cs[:, 0, :nt], ALU.mult, [B_ps, B_cs], [B_r1])
                            rope_tail()
                            prev[0] = (dst[c, :, tok0:tok0 + nt], qb, B_qb, r1, B_r1)
                        else:
                            qr, B_qr = qrs.next()
                            act(qr[:, :nt], ps[:, :nt], AF.Copy, [B_ps], [B_qr])
                            P.dma(dst[c, :, tok0:tok0 + nt], qr[:, :nt], reads=[B_qr])
                        if qk == 1 and ti + 1 < len(tiles512):
                            p1_square(ti + 1, part=c)
                if ti + 1 < len(tiles512):
                    p1_norm(ti + 1)
                rope_tail()
                for s in range(ns):
                    ps, B_ps = pA.next()
                    for k in range(8):
                        mm(ps[:, :], hb[:, k, s * 128:(s + 1) * 128], win[:, k, 1024:1536], k == 0, k == 7, bw(1024, 512) + [B_hb], [B_ps])
                    evi[0] += 1
                    cpy(vb[:, s, :, 0:128], ps[:, :].rearrange("p (h e) -> p h e", h=4), [B_ps], [B_vb],
                        eng=("act" if evi[0] % 2 else "dve"))
                P.dma(VV[tok0:tok0 + nt].rearrange("(s p) h e -> p s h e", p=128), vb[:, :ns], reads=[B_vb])
                vm_todo = []
                if full:
                    for s in range(ns):
                        ps, B_ps = pA.next()
                        for k in range(8):
                            mm(ps[:, 0:256], hb[:, k, s * 128:(s + 1) * 128], win[:, k, 1792:2048], k == 0, k == 7, bw(1792, 256) + [B_hb], [B_ps])
                        gvf, B_gvf = gvfs.next()
                        sqg, B_sqg = sqgs.next()
                        ssg, B_ssg = ssgs.next()
                        lng, B_lng = lngs.next()
                        rsg, B_rsg = rsgs.next()
                        act(gvf[:], ps[:, 0:256], AF.Copy, [B_ps], [B_gvf])
                        tt(sqg[:], gvf[:], gvf[:], ALU.mult, [B_gvf], [B_sqg])
                        red(ssg[:], sqg[:].rearrange("p (g c) -> p g c", g=4), [B_sqg], [B_ssg])
                        rsqrt_chain(rsg[:], ssg[:], 1.0 / 64, lng[:], B_lng, [B_ssg], B_rsg)
                        vn, B_vn = vns.next()
                        tt(vn[:], gvf[:].rearrange("p (g c) -> p g c", g=4), rsg[:].unsqueeze(2).to_broadcast([128, 4, 64]),
                           ALU.mult, [B_gvf, B_rsg], [B_vn])
                        vm_todo.append((s, vn, B_vn))
                if not full:
                    continue
                for c2 in range(2):
                    col0 = 1536 + c2 * 128
                    ps, B_ps = pA.next()
                    for k in range(8):
                        mm(ps[:, :nt], win[:, k, col0:col0 + 128], hb[:, k, :nt], k == 0, k == 7, bw(col0, 128) + [B_hb], [B_ps])
                    act(usb[:, c2, :nt], ps[:, :nt], AF.Copy, [B_ps], [B_usb])
                for c2 in range(2):
                    col0 = 2048 + c2 * 128
                    ps, B_ps = pA.next()
                    for k in range(8):
                        mm(ps[:, :nt], win[:, k, col0:col0 + 128], hb[:, k, :nt], k == 0, k == 7, bw(col0, 128) + [B_hb], [B_ps])
                    act(pf[:, c2, :nt], ps[:, :nt], AF.Copy, [B_ps], [B_pf])
                P.dma(PT.rearrange("c p t -> p c t")[:, :, tok0:tok0 + nt], pf[:, :, :nt], reads=[B_pf])
                for (s, vn, B_vn) in vm_todo:
                    for g in range(4):
                        gl_, c2 = g % 2, g // 2
                        mm(pVM[gl_ * 64:(gl_ + 1) * 64, c2, s * 128:(s + 1) * 128], vn[:, g, :], wsb[:, g, :], True, True,
                           [B_vn, B_wsb], [B_pVM])
                for c2 in range(2):
                    stt(mbt[:, :nt], pVM[:, c2, :nt], gvn[:, l, c2:c2 + 1], bst[:, c2, :nt], ALU.mult, ALU.add,
                        [B_pVM, B_gvn, B_bst], [B_mbt])
                    tt(mbb[:, c2, :nt], mbt[:, :nt], usb[:, c2, :nt], ALU.mult, [B_mbt, B_usb], [B_mbb])
                P.dma(MIX[4:6].rearrange("c p t -> p c t")[:, :, tok0:tok0 + nt], mbb[:, :, :nt], reads=[B_mbb])
            P.barrier()
            if dbg:
                print('ops@', l, Prog.count)

        with (ExitStack() if l * 5 + 1 <= stop_step else _Skip()) as st:
            xpA = sbt(st, "xp", [128, PADL])
            xpB = sbt(st, "xpB", [128, PADL])
            icB = sbt(st, "icB", [128, PADL])
            a2, B_a2 = sbt(st, "a2", [128, PADL])
            s4, B_s4 = sbt(st, "s4", [128, PADL])
            s8, B_s8 = sbt(st, "s8", [128, PADL])
            s16, B_s16 = sbt(st, "s16", [128, PADL])
            ic, B_ic = sbt(st, "ic", [128, PADL])
            dt_, B_dt = sbt(st, "dt", [128, PADL])
            dbf, B_dbf = sbt(st, "dbf", [128, PADL], BF16)
            wpf, B_wpf = sbt(st, "wpf", [128, 2, 128])
            wpb, B_wpb = sbt(st, "wpb", [128, 2, 128], BF16)
            mcs = Rot([sbt(st, f"mc{i}", [128, 512], BF16) for i in range(2)])
            pA = Rot([pst(st, f"pbA{i}", [128, 512]) for i in range(2)])
            memset(wpf[:], 0.0, [B_wpf])
            for g in range(4):
                gl_, c2 = g % 2, g // 2
                P.dma(wpf[gl_ * 64:(gl_ + 1) * 64, c2, gl_ * 64:(gl_ + 1) * 64], wpool_in[l, g], writes=[B_wpf])
            cpy(wpb[:], wpf[:], [B_wpf], [B_wpb])
            Lp = PADL
            icA = (ic, B_ic)
            for c2 in range(2):
                xp, B_xp = (xpA, xpB)[c2]
                ic_, B_ic_ = (icA, icB)[c2]
                memset(xp[:, 0:8], 0.0, [B_xp])
                memset(xp[:, 4104:4120], 0.0, [B_xp])
                memset(xp[:, 4376:4384], 0.0, [B_xp])
                P.dma(xp[:, 8:8 + T], PT[c2, :, 0:T], writes=[B_xp], eng=("sp" if c2 == 0 else "pool"))
                P.dma(xp[:, 4120:4120 + TC], PT[c2, :, T:TT], writes=[B_xp], eng=("sp" if c2 == 0 else "pool"))
                P.dma(ic_[:], ic_in[:, c2, :], writes=[B_ic_], eng=("sp" if c2 == 0 else "pool"))
            for c2 in range(2):
                xp, B_xp = (xpA, xpB)[c2]
                ic, B_ic = (icA, icB)[c2]
                tt(a2[:, 1:Lp], xp[:, 0:Lp - 1], xp[:, 1:Lp], ALU.add, [B_xp], [B_a2])
                if c2 == 0:
                    tt(s4[64:128, 2:Lp - 1], a2[64:128, 1:Lp - 2], a2[64:128, 3:Lp], ALU.add, [B_a2], [B_s4])
                    srcs = [(a2, B_a2), (s4, B_s4)]
                else:
                    tt(s4[:, 2:Lp - 1], a2[:, 1:Lp - 2], a2[:, 3:Lp], ALU.add, [B_a2], [B_s4])
                    tt(s8[:, 4:Lp - 3], s4[:, 2:Lp - 5], s4[:, 6:Lp - 1], ALU.add, [B_s4], [B_s8])
                    tt(s16[64:128, 8:Lp - 7], s8[64:128, 4:Lp - 11], s8[64:128, 12:Lp - 3], ALU.add, [B_s8], [B_s16])
                    srcs = [(s8, B_s8), (s16, B_s16)]
                for gl_ in range(2):
                    s_t, s_b = srcs[gl_]
                    pr_ = slice(gl_ * 64, (gl_ + 1) * 64)
                    tt(dt_[pr_, 8:Lp - 8], s_t[pr_, 8:Lp - 8], ic[pr_, 8:Lp - 8], ALU.mult, [s_b, B_ic], [B_dt])
                tt(dbf[:, 8:Lp - 8], dt_[:, 8:Lp - 8], xp[:, 8:Lp - 8], ALU.subtract, [B_dt, B_xp], [B_dbf])
                for ti, (tok0, nt, cond) in enumerate(tiles512):
                    if last and cond == 1:
                        continue
                    col = 8 + tok0 if cond == 0 else 4120 + (tok0 - T)
                    ps, B_ps = pA.next()
                    mm(ps[:, :nt], wpb[:, c2, :], dbf[:, col:col + nt], True, True, [B_wpb, B_dbf], [B_ps])
                    mc, B_mc = mcs.next()
                    act(mc[:, :nt], ps[:, :nt], AF.Identity, [B_ps, B_spool], [B_mc], scale=spool[:, l, c2:c2 + 1])
                    P.dma(MIX[6 + c2, :, tok0:tok0 + nt], mc[:, :nt], reads=[B_mc])
            P.barrier()
            if dbg:
                print('ops@', l, Prog.count)

        sawm = ExitStack()
        w1b, B_w1 = sbt(sawm, "w1b", [128, 8, 4 * D], BF16)
        w1_pieces = [(k, nq) for k in range(8) for nq in range(4)]

        wout, B_wout = sbt(sawm, "wout", [128, 8, D], BF16)
        wout_pieces = list(range(8))

        def wout_piece(stg_rot):
            pc = wout_pieces.pop(0)
            s_t, s_b = stg_rot.next()
            P.dma(s_t[:].rearrange("p (k n) -> p k n", k=8),
                  wout_in[l].rearrange("(k p) n -> p k n", p=128)[:, :, pc * 128:(pc + 1) * 128], writes=[s_b], eng="pool")
            cpy(wout[:, :, pc * 128:(pc + 1) * 128], s_t[:].rearrange("p (k n) -> p k n", k=8), [s_b], [B_wout], eng="dve")

        def w1_piece(stg_rot):
            if wout_pieces:
                wout_piece(stg_rot)
                return
            k, nq = w1_pieces.pop(0)
            s_t, s_b = stg_rot.next()
            P.dma(s_t[:], w1_in[l, k * 128:(k + 1) * 128, nq * 1024:(nq + 1) * 1024], writes=[s_b], eng="pool")
            cpy(w1b[:, k, nq * 1024:(nq + 1) * 1024], s_t[:], [s_b], [B_w1], eng="dve")

        with (ExitStack() if l * 5 + 2 <= stop_step else _Skip()) as st:
            stgA = Rot([sbt(st, f"stgA{i}", [128, 1024]) for i in range(2)])
            kt_sb, B_kt = sbt(st, "kt_sb", [128, 4, TT], BF16)
            v_sb, B_v = sbt(st, "v_sb", [128, NKT, 4, 130], BF16)
            for h in range(4):
                P.dma(kt_sb[:, h, :], KT[h], writes=[Buf()], key=B_kt, eng=("sp" if h % 2 == 0 else "pool"))
            for g4 in range(0, NKT, 6):
                n4 = min(6, NKT - g4)
                P.dma(v_sb[:, g4:g4 + n4], VV[g4 * 128:(g4 + n4) * 128].rearrange("(k p) h e -> p k h e", p=128),
                      writes=[Buf()], key=B_v, eng=("pool" if (g4 // 6) % 2 == 0 else "sp"))
            P.barrier()
            qbl = [sbt(st, f"qbl{i}", [128, 512], BF16) for i in range(2)]
            Es = [sbt(st, f"E{i}", [128, 2, 512], BF16) for i in range(3)]
            sps = [pst(st, f"sps{i}", [128, 2, 512]) for i in range(2)]
            accb = [pst(st, "accA", [128, 512]), pst(st, "accB", [128, 512]), pst(st, "accC", [128, 512])]
            pT, B_pT = pst(st, "pT", [128, 4, 128], BF16)
            accS, B_accS = sbt(st, "accS", [128, 8, 130])
            rz8, B_rz = sbt(st, "rz8", [128, 8, 1])
            nrz4, B_nrz = sbt(st, "nrz4", [128, 4, 1])
            o04, B_o0 = sbt(st, "o04", [128, 4, 128])
            t14, B_t1 = sbt(st, "t14", [128, 4, 128])
            oo4, B_oo = sbt(st, "oo4", [128, 4, 128])
            osq4, B_osq = t14, B_t1
            oss4, B_oss = sbt(st, "oss4", [128, 4])
            oln4, B_oln = sbt(st, "oln4", [128, 4])
            ors4, B_ors = sbt(st, "ors4", [128, 4, 1])
            abt, B_abt = o04, B_o0
            ab4s = [sbt(st, f"ab4{i}", [128, 4, 128], BF16) for i in range(2)]
            aTs = Rot([sbt(st, f"aT{i}", [128, 512], BF16) for i in range(2)])
            do_mod = (l + 1 < n_layers)
            if do_mod:
                scb2, B_scb2 = sbt(st, "scb2", [128, 2, D])
                bada2, B_bada2 = sbt(st, "bada2", [128, L, 48])
                P.dma(scb2[:], SCB, writes=[B_scb2])
                P.dma(bada2[:], bada_in, writes=[B_bada2])
                wrot2 = Rot([sbt(st, f"wst2{i}", [128, D]) for i in range(2)])
                prod2, B_prod2 = sbt(st, "prod2", [128, 1, D])
                redt2, B_redt2 = sbt(st, "redt2", [128, 48, 2])
            mod_j = [0]
            pending = []
            blocks = []
            for h in range(4):
                for qb_i in range(8):
                    blocks.append((h, qb_i * 512, 512, list(range(NKT))))
                if not last:
                    blocks.append((h, T, TC, [32, 33]))
            for bi, (h, q0, qn, kts) in enumerate(blocks):
                qt_, B_q = qbl[bi % 2]
                P.dma(qt_[:, :qn], QT[h, :, q0:q0 + qn], writes=[B_q])
                nqs = qn // 128

                def qk(i, kt):
                    sp, B_sp = sps[i % 2]
                    for j in range(2):
                        pr_ = slice(j * 64, (j + 1) * 64)
                        mm(sp[:, j, :qn], kt_sb[pr_, h, kt * 128:(kt + 1) * 128], qt_[pr_, :qn], True, True, [B_kt, B_q], [B_sp])
                    E, B_E = Es[i % 3]
                    act(E[:, :, :qn], sp[:, :, :qn], AF.Exp, [B_sp], [B_E], scale=0.125)

                def pv(i, kt):
                    E, B_E = Es[i % 3]
                    banks_started = set()
                    for j in range(2):
                        for qs in range(nqs):
                            idx = j * 4 + qs
                            bk, slot = idx // 3, idx % 3
                            a_t, a_b = accb[bk]
                            first_in_bank = (i == 0 and bk not in banks_started)
                            banks_started.add(bk)
                            mm(a_t[:, slot * 130:slot * 130 + 129], E[:, j, qs * 128:(qs + 1) * 128], v_sb[:, kt, h, 0:129],
                               first_in_bank, i == len(kts) - 1, [B_E, B_v], [a_b], skip=True)

                qk(0, kts[0])
                if len(kts) > 1:
                    qk(1, kts[1])
                for i, kt in enumerate(kts):
                    if i + 2 < len(kts):
                        qk(i + 2, kts[i + 2])
                    pv(i, kt)
                    while pending and pending[0][0] <= i:
                        pending.pop(0)[1]()
                while pending:
                    pending.pop(0)[1]()
                used = sorted(set((j * 4 + qs) // 3 for j in range(2) for qs in range(nqs)))
                for bk in used:
                    a_t, a_b = accb[bk]
                    nsl = 3 if bk < 2 else 2
                    cpy(accS[:, bk * 3:bk * 3 + nsl, :], a_t[:, 0:nsl * 130].rearrange("p (s c) -> p s c", c=130), [a_b], [B_accS],
                        eng="dve")
                P.op("dve", lambda e: e.reciprocal(out=rz8[:], in_=accS[:, :, 128:129]), [B_accS], [B_rz])
                ts(nrz4[:, :nqs], rz8[:, 4:4 + nqs], nlam[:, l:l + 1], None, ALU.mult, None, [B_rz, B_nlam], [B_nrz])
                tt(o04[:, :nqs], accS[:, 0:nqs, 0:128], rz8[:, 0:nqs].to_broadcast([128, nqs, 128]), ALU.mult, [B_accS, B_rz], [B_o0])
                tt(t14[:, :nqs], accS[:, 4:4 + nqs, 0:128], nrz4[:, :nqs].to_broadcast([128, nqs, 128]), ALU.mult,
                   [B_accS, B_nrz], [B_t1])
                tt(oo4[:, :nqs], o04[:, :nqs], t14[:, :nqs], ALU.add, [B_o0, B_t1], [B_oo])
                tt(osq4[:, :nqs], oo4[:, :nqs], oo4[:, :nqs], ALU.mult, [B_oo], [B_osq])
                red(oss4[:, :nqs], osq4[:, :nqs], [B_osq], [B_oss])
                ab4, B_ab = ab4s[bi % 2]
                aT, B_aT = aTs.next()

                def stage2(nqs=nqs):
                    rsqrt_chain(ors4[:, :nqs, 0], oss4[:, :nqs], 1.0 / 128, oln4[:, :nqs], B_oln, [B_oss], B_ors)

                def stage3(nqs=nqs, ab4=ab4, B_ab=B_ab):
                    tt(abt[:, :nqs], oo4[:, :nqs], ors4[:, :nqs].to_broadcast([128, nqs, 128]), ALU.mult, [B_oo, B_ors], [B_abt])
                    tt(ab4[:, :nqs], abt[:, :nqs], GS[:, l, :].unsqueeze(1).to_broadcast([128, nqs, 128]), ALU.mult,
                       [B_abt, B_GS], [B_ab])

                def stage4(nqs=nqs, ab4=ab4, B_ab=B_ab):
                    for qs in range(nqs):
                        transpose(pT[:, qs, :], ab4[:, qs, :], ident_bf[:], [B_ab, B_ident], [B_pT])

                def stage5(nqs=nqs, qn=qn, h=h, q0=q0, aT=aT, B_aT=B_aT):
                    cpy(aT[:, :qn], pT[:, 0:nqs, :].rearrange("p a b -> p (a b)"), [B_pT], [B_aT], eng="dve")
                    P.dma(MIX[h, :, q0:q0 + qn], aT[:, :qn], reads=[B_aT])

                pending = [(5, stage2), (10, stage3), (24, stage4), (29, stage5)]
                near_ctx = (qn != 512) or (bi + 1 < len(blocks) and blocks[bi + 1][2] != 512)
                if near_ctx:
                    continue
                if w1_pieces:
                    w1_piece(stgA)
                    if bi % 4 == 0 and w1_pieces:
                        w1_piece(stgA)
                if do_mod:
                    for _ in range(2 if mod_j[0] >= 2 * bi else 3):
                        if mod_j[0] < 48:
                            mod_chunk(l + 1, mod_j[0], scb2, B_scb2, wrot2, prod2, B_prod2, redt2, B_redt2)
                            mod_j[0] += 1
            while pending:
                pending.pop(0)[1]()
            while w1_pieces or wout_pieces:
                w1_piece(stgA)
            if do_mod:
                while mod_j[0] < 48:
                    mod_chunk(l + 1, mod_j[0], scb2, B_scb2, wrot2, prod2, B_prod2, redt2, B_redt2)
                    mod_j[0] += 1
                mod_finish(l + 1, redt2, B_redt2, bada2, B_bada2)
            P.barrier()
            if dbg:
                print('ops@', l, Prog.count)

        tiles256 = [(i * 256, 256, 0) for i in range(16)] + ([] if last else [(T, TC, 1)])
        with (ExitStack() if l * 5 + 3 <= stop_step else _Skip()) as st:
            w2b, B_w2 = sbt(st, "w2b", [128, 32, D], BF16)
            wstg = Rot([sbt(st, f"wstg{i}", [128, 1024]) for i in range(2)])
            xs2 = [sbt(st, f"wxs{i}", [128, 8, 256]) for i in range(2)]
            bfA = sbt(st, "bfA", [128, 8, 256], BF16)
            bfB = sbt(st, "bfB", [128, 8, 256], BF16)
            mx2 = [bfA, bfB]
            lnb, B_ln = sbt(st, "mlnb", [128, 256])
            rstd, B_rstd = sbt(st, "mrstd", [128, 256])
            tmps = Rot([sbt(st, f"mtmp{i}", [128, 256]) for i in range(2)])
            hid, B_hid = sbt(st, "hid", [128, 32, 256], BF16)
            r32s = Rot([sbt(st, f"r32{i}", [128, 256]) for i in range(2)])
            pA = Rot([pst(st, f"pmA{i}", [128, 512]) for i in range(3)])
            pB = Rot([pst(st, f"pmB{i}", [128, 512]) for i in range(2)])
            XTb = [Buf(f"XTb{i}") for i in range(len(tiles256))]
            w2_f = [0]

            def w2_piece():
                f = w2_f[0]
                w2_f[0] += 1
                s_t, s_b = wstg.next()
                P.dma(s_t[:], w2_in[l, f * 128:(f + 1) * 128, :], writes=[s_b], eng="pool")
                convert(w2b[:, f, :], s_t[:], [s_b], [B_w2], engs=("act", "dve"))

            sq, B_sq = bfA
            bfC = sbt(st, "bfC", [128, 8, 256], BF16)
            h2s = [bfB, bfC]

            def m_norm(ti):
                tok0, nt, cond = tiles256[ti]
                xn, B_xn = xs2[ti % 2]
                h2, B_h2 = h2s[ti % 2]
                mx, B_mx = bfA
                for co in range(8):
                    ps, B_ps = pA.next()
                    for k in range(8):
                        mm(ps[:, :nt], wout[:, k, co * 128:(co + 1) * 128], mx[:, k, :], k == 0, k == 7, [B_wout, B_mx], [B_ps])
                    stt(xn[:, co, :], ps[:, :nt], mod[:, l, 16 + co, cond:cond + 1], xn[:, co, :], ALU.mult, ALU.add,
                        [B_ps, B_mod, B_xn], [B_xn])
                pss, B_pss = pA.next()
                norm_mod((sq, B_sq, lnb, B_ln, rstd, B_rstd, tmps), xn, B_xn, nt,
                         lambda k: gs2[:, l, k, cond:cond + 1], lambda k: mod[:, l, 24 + k, cond:cond + 1],
                         h2, B_h2, pss, B_pss)

            def m_load(ti):
                tok0, nt, cond = tiles256[ti]
                xn, B_xn = xs2[ti % 2]
                mx, B_mx = bfA
                P.dma(xn[:], x_src(l, tok0, nt), writes=[B_xn])
                P.dma(mx[:], MIX.rearrange("c p t -> p c t")[:, :, tok0:tok0 + nt], writes=[B_mx])

            m_load(0)
            m_norm(0)
            pend = None
            for f in range(32):
                s_t, s_b = wstg.next()
                P.dma(s_t[:], w2_in[l, f * 128:(f + 1) * 128, :], writes=[s_b], eng="pool")
                if pend is not None:
                    cpy(w2b[:, pend[0], :], pend[1][:], [pend[2]], [B_w2], eng="pool")
                pend = (f, s_t, s_b)
            cpy(w2b[:, pend[0], :], pend[1][:], [pend[2]], [B_w2], eng="pool")
            w2_f[0] = 32
            for ti, (tok0, nt, cond) in enumerate(tiles256):
                xn, B_xn = xs2[ti % 2]
                h2, B_h2 = h2s[ti % 2]
                if ti + 1 < len(tiles256):
                    m_load(ti + 1)
                for f in range(32):
                    ps, B_ps = pA.next()
                    for k in range(8):
                        mm(ps[:, :nt], w1b[:, k, f * 128:(f + 1) * 128], h2[:, k, :], k == 0, k == 7, [B_w1, B_h2], [B_ps])
                    r32, B_r32 = r32s.next()
                    act(r32[:], ps[:, :nt], AF.Relu, [B_ps], [B_r32])
                    tt(hid[:, f, :], r32[:], r32[:], ALU.mult, [B_r32], [B_hid])
                if ti + 1 < len(tiles256):
                    m_norm(ti + 1)
                for co in range(8):
                    ps, B_ps = pB.next()
                    for f in range(32):
                        mm(ps[:, :nt], w2b[:, f, co * 128:(co + 1) * 128], hid[:, f, :], f == 0, f == 31, [B_w2, B_hid], [B_ps])
                    stt(xn[:, co, :], ps[:, :nt], mod[:, l, 40 + co, cond:cond + 1], xn[:, co, :], ALU.mult, ALU.add,
                        [B_ps, B_mod, B_xn], [B_xn])
                if not last:
                    P.dma(x_dst(tok0, nt), xn[:], reads=[B_xn])
                else:
                    act(sq[:], xn[:], AF.Square, [B_xn], [B_sq])
                    pss, B_pss = pA.next()
                    for k in range(8):
                        mm(pss[:, :nt], ones_bf[:], sq[:, k, :], k == 0, k == 7, [B_ones, B_sq], [B_pss])
                    rsqrt_chain(rstd[:], pss[:, :nt], 1.0 / D, lnb[:], B_ln, [B_pss], B_rstd)
                    for k in range(8):
                        stt(xn[:, k, :], xn[:, k, :], gf[:, k:k + 1], rstd[:], ALU.mult, ALU.mult, [B_xn, B_gf, B_rstd], [B_xn])
                    P.dma(yT_out.rearrange("(k p) t -> p k t", p=128)[:, :, tok0:tok0 + nt], xn[:], reads=[B_xn], is_output=True)
            P.barrier()
            if dbg:
                print('ops@', l, Prog.count)
        sawm.close()

    P.emit(top)
    top.close()
    return nc


def _const_tables():
    grid_w = 64
    n_freq = 16
    t = np.arange(T)
    row = (t // grid_w).astype(np.float32)
    col = (t % grid_w).astype(np.float32)
    inv = (np.float32(10000.0) ** (-np.arange(n_freq, dtype=np.float32) / np.float32(n_freq))).astype(np.float32)
    cosT = np.zeros((128, T), np.float32)
    sinT = np.zeros((128, T), np.float32)
    rsw = np.zeros((128, 128), np.float32)
    for p in range(128):
        axis = (p % 64) // 32
        half = (p % 32) // 16
        f = p % 16
        ang = ((row if axis == 0 else col) * inv[f]).astype(np.float32)
        cosT[p] = np.cos(ang)
        sinT[p] = np.sin(ang) * (-1.0 if half == 0 else 1.0)
        partner = p + 16 if half == 0 else p - 16
        rsw[partner, p] = 1.0
    ic = np.zeros((128, 2, PADL), np.float32)
    wins = (2, 4, 8, 16)
    for g, w in enumerate(wins):
        gl_, c2 = g % 2, g // 2
        for (tseg, base) in ((T, 8), (TC, 4120)):
            pos = np.arange(tseg)
            lo = np.clip(pos - w // 2, 0, tseg)
            hi = np.clip(pos + (w - w // 2), 0, tseg)
            ic[gl_ * 64:(gl_ + 1) * 64, c2, base:base + tseg] = (1.0 / (hi - lo).astype(np.float32))[None, :]
    return cosT, sinT, rsw, np.eye(128, dtype=np.float32), ic


_NC_CACHE = {}


def _pvec(v, nch):
    v = np.asarray(v, np.float32)
    lead = v.shape[:-1]
    return np.ascontiguousarray(np.moveaxis(v.reshape(lead + (nch, 128)), -1, 0))


def kernel(x, c, ctx, c_ctx, w_ada, b_ada, g_norm_mix, g_norm_mlp, w_in, lam_q1, lam_k1, lam_q2, lam_k2,
           g_subln, g_vnorm, w_spatial, b_spatial, w_pool, s_pool, w_out, w1, w2, g_final):
    f = lambda a: np.ascontiguousarray(np.asarray(a, np.float32))
    x, c, ctx, c_ctx = f(x), f(c), f(ctx), f(c_ctx)
    n = 8
    if "nc" not in _NC_CACHE:
        _NC_CACHE["nc"] = build_program()
    nc = _NC_CACHE["nc"]
    cosT, sinT, rsw, ident, ic = _const_tables()
    w_ada = f(w_ada)
    wadaT = np.ascontiguousarray(w_ada.reshape(L, D, 48, 128).transpose(0, 3, 2, 1))
    b_sp = f(b_spatial)
    bsT = np.zeros((L, 128, 2, 512), np.float32)
    for g in range(4):
        gl_, c2 = g % 2, g // 2
        bsT[:, gl_ * 64:(gl_ + 1) * 64, c2, :] = np.tile(b_sp[:, g, :], (1, 4))[:, None, :]
    lamv = np.stack([f(lam_q1), f(lam_k1), f(lam_q2), f(lam_k2)], axis=1)
    shared = {
        "wadaT": wadaT,
        "bada": _pvec(f(b_ada), 48),
        "gm": _pvec(f(g_norm_mix), 8),
        "gl": _pvec(f(g_norm_mlp), 8),
        "gf": _pvec(f(g_final), 8),
        "w_in": f(w_in),
        "lamv": np.ascontiguousarray(np.broadcast_to(lamv[None], (128, L, 4, 64))),
        "gsub": np.ascontiguousarray(np.broadcast_to(f(g_subln)[None], (128, L, 128))),
        "gvn": _pvec(f(g_vnorm), 2),
        "wsT": np.ascontiguousarray(f(w_spatial).transpose(0, 3, 1, 2)),
        "bsT": bsT,
        "w_pool": f(w_pool),
        "spool": _pvec(f(s_pool), 2),
        "w_out": f(w_out),
        "w1": f(w1),
        "w2": f(w2),
        "cosT": cosT, "sinT": sinT, "rsw": rsw, "ident": ident, "icT": ic,
    }
    in_maps = []
    for b in range(n):
        m = dict(shared)
        m["xT"] = np.ascontiguousarray(x[b].T)
        m["ctxT"] = np.ascontiguousarray(ctx[b].T)
        cbv = np.stack([c[b], c_ctx], axis=0)
        m["cb"] = np.ascontiguousarray(np.broadcast_to(cbv[None], (128, 2, D)))
        in_maps.append(m)
    if _NC_CACHE.get("return_maps"):
        return in_maps
    res = run_bass_kernel_spmd(nc, in_maps, core_ids=list(range(n)))
    out = np.stack([np.ascontiguousarray(res.results[b]["yT"].T) for b in range(n)], axis=0)
    return out.astype(np.float32)
```

```python
import math
from contextlib import ExitStack
import numpy as np
import concourse.bass as bass
import concourse.mybir as mybir
from concourse.bass_utils import run_bass_kernel_spmd

F32 = mybir.dt.float32
BF16 = mybir.dt.bfloat16
AF = mybir.ActivationFunctionType
ALU = mybir.AluOpType
AX = mybir.AxisListType

D = 1024
T = 4096
TC = 256
TT = T + TC
L = 4
NKT = TT // 128
EPS = 1e-6
IN_W = 2304
PADL = 4384
SEM_TICK_LIMIT = 30000


class Buf:
    __slots__ = ("name", "last_w", "readers", "excl")

    def __init__(self, name="", excl=False):
        self.name = name
        self.last_w = None
        self.readers = []
        self.excl = excl


class Op:
    __slots__ = ("eng", "fn", "deps", "needs_inc", "sem", "tick", "is_dma", "dma_key")

    def __init__(self, eng, fn, is_dma=False, dma_key=None):
        self.eng = eng
        self.fn = fn
        self.deps = []
        self.needs_inc = False
        self.sem = None
        self.tick = 0
        self.is_dma = is_dma
        self.dma_key = dma_key


class Prog:
    ENGS = ("pe", "act", "dve", "pool", "sp")

    def __init__(self, nc):
        self.nc = nc
        self.q = {e: [] for e in self.ENGS}
        self.out_dmas = []
        self.pending_dmas = []
        self.dma_slots = []
        self.dma_active = {}
        self.dma_free = []

    muted = False
    limit = 10 ** 12
    count = 0

    def _add(self, op, reads, writes):
        Prog.count += 1
        if Prog.muted or Prog.count > Prog.limit:
            return op
        ex = [b for b in reads if b.excl]
        if ex:
            reads = [b for b in reads if not b.excl]
            writes = list(writes) + ex
        deps = set()
        for b in reads:
            if b.last_w is not None:
                deps.add(b.last_w)
        for b in writes:
            if b.last_w is not None:
                deps.add(b.last_w)
            for r in b.readers:
                deps.add(r)
        deps.discard(op)
        for d in deps:
            if d.eng == "pe" and op.eng == "pe" and not d.is_dma and not op.is_dma:
                continue
            d.needs_inc = True
            op.deps.append(d)
        for b in reads:
            b.readers.append(op)
        for b in writes:
            b.last_w = op
            b.readers = []
        self.q[op.eng].append(op)
        return op

    def op(self, eng, fn, reads=(), writes=()):
        return self._add(Op(eng, fn), reads, writes)

    def dma(self, out_ap, in_ap, reads=(), writes=(), key=None, eng="sp", is_output=False):
        def fn(e):
            return e.dma_start(out=out_ap, in_=in_ap)
        if key is None:
            key = writes[0] if writes else reads[0]
        o = Op(eng, fn, is_dma=True, dma_key=key)
        self._add(o, reads, writes)
        o.needs_inc = True
        if Prog.muted or Prog.count > Prog.limit:
            return o
        k = id(key)
        if k not in self.dma_active:
            if self.dma_free:
                self.dma_active[k] = self.dma_free.pop()
            else:
                self.dma_slots.append(0)
                self.dma_active[k] = len(self.dma_slots) - 1
        slot = self.dma_active[k]
        self.dma_slots[slot] += 16
        o.sem, o.tick = slot, self.dma_slots[slot]
        self.pending_dmas.append(o)
        if is_output:
            self.out_dmas.append(o)
        return o

    def barrier(self):
        if Prog.muted:
            return
        marks = []
        for e in self.ENGS:
            last = None
            for o in reversed(self.q[e]):
                if not o.is_dma and o.fn is not None:
                    last = o
                    break
            if last is not None:
                last.needs_inc = True
                marks.append(last)
        marks += self.pending_dmas
        self.pending_dmas = []
        for k, slot in self.dma_active.items():
            if self.dma_slots[slot] < 40000:
                self.dma_free.append(slot)
        self.dma_active = {}
        for e in self.ENGS:
            o = Op(e, None)
            o.deps = list(marks)
            self.q[e].append(o)

    def emit(self, stack):
        nc = self.nc

        def new_sem(name):
            return stack.enter_context(nc.semaphore(name))

        for e in self.ENGS:
            cur, cnt, n = None, 0, 0
            for o in self.q[e]:
                if o.is_dma or not o.needs_inc:
                    continue
                if cur is None or cnt >= SEM_TICK_LIMIT:
                    cur = new_sem(f"s_{e}_{n}")
                    n += 1
                    cnt = 0
                cnt += 1
                o.sem, o.tick = cur, cnt
        slot_sems = [new_sem(f"s_dma_{i}") for i in range(len(self.dma_slots))]
        for e in self.ENGS:
            for o in self.q[e]:
                if o.is_dma:
                    o.sem = slot_sems[o.sem]
        engmap = {"pe": "tensor", "act": "scalar", "dve": "vector", "pool": "gpsimd", "sp": "sync"}
        block = stack.enter_context(nc.Block())
        for e in self.ENGS:
            def body(eng, ops=self.q[e], e=e):
                waited = {}

                def wait(d):
                    k = id(d.sem)
                    if waited.get(k, 0) >= d.tick:
                        return
                    waited[k] = d.tick
                    eng.wait_ge(d.sem, d.tick)
                for o in ops:
                    for d in o.deps:
                        wait(d)
                    if o.fn is None:
                        continue
                    ins = o.fn(eng)
                    if o.needs_inc:
                        ins.then_inc(o.sem, 16 if o.is_dma else 1)
                if e == "sp":
                    for d in self.out_dmas:
                        wait(d)
            getattr(block, engmap[e])(body)


class _Skip:
    cur = None

    def __enter__(self):
        self.st = ExitStack()
        Prog.muted = True
        return self.st

    def __exit__(self, *a):
        Prog.muted = False
        self.st.close()
        return False


class Rot:
    def __init__(self, items):
        self.items = items
        self.i = 0

    def next(self):
        it = self.items[self.i % len(self.items)]
        self.i += 1
        return it


def lam_init(l):
    return 0.8 - 0.6 * math.exp(-0.3 * l)


def build_program(n_layers=L, dbg=False, stop_step=10 ** 9):
    nc = bass.Bass("TRN2", target_bir_lowering=False)
    P = Prog(nc)
    top = ExitStack()

    def din(name, shape, dt=F32):
        return nc.dram_tensor(name, list(shape), dt, kind="ExternalInput").ap()

    def dscr(name, shape, dt):
        kind = "ExternalOutput" if dbg else "Internal"
        return nc.dram_tensor(name, list(shape), dt, kind=kind).ap()

    xT_in = din("xT", [D, T])
    ctxT_in = din("ctxT", [D, TC])
    cb_in = din("cb", [128, 2, D])
    wadaT_in = din("wadaT", [L, 128, 48, D])
    bada_in = din("bada", [128, L, 48])
    gm_in = din("gm", [128, L, 8])
    gl_in = din("gl", [128, L, 8])
    gf_in = din("gf", [128, 8])
    win_in = din("w_in", [L, D, IN_W])
    lamv_in = din("lamv", [128, L, 4, 64])
    gsub_in = din("gsub", [128, L, 128])
    gvn_in = din("gvn", [128, L, 2])
    wsT_in = din("wsT", [L, 128, 4, 128])
    bsT_in = din("bsT", [L, 128, 2, 512])
    wpool_in = din("w_pool", [L, 4, 64, 64])
    spool_in = din("spool", [128, L, 2])
    wout_in = din("w_out", [L, D, D])
    w1_in = din("w1", [L, D, 4 * D])
    w2_in = din("w2", [L, 4 * D, D])
    cosT_in = din("cosT", [128, T])
    sinT_in = din("sinT", [128, T])
    rsw_in = din("rsw", [128, 128])
    ident_in = din("ident", [128, 128])
    ic_in = din("icT", [128, 2, PADL])
    yT_out = nc.dram_tensor("yT", [D, T], F32, kind="ExternalOutput").ap()

    XT = dscr("XT", [8, 128, TT], F32)
    QT = dscr("QT", [4, 128, TT], BF16)
    KT = dscr("KT", [4, 128, TT], BF16)
    VV = dscr("VV", [TT, 4, 130], BF16)
    PT = dscr("PT", [2, 128, TT], F32)
    MIX = dscr("MIX", [8, 128, TT], BF16)
    SCB = dscr("SCB", [128, 2, D], F32)

    uid = [0]

    def sbt(st, name, shape, dt=F32):
        uid[0] += 1
        t = st.enter_context(nc.sbuf_tensor(f"sb{uid[0]}_{name}", list(shape), dt))
        return t, Buf(name)

    def pst(st, name, shape, dt=F32):
        uid[0] += 1
        t = st.enter_context(nc.psum_tensor(f"ps{uid[0]}_{name}", list(shape), dt))
        return t, Buf(name, excl=True)

    def act(out, in_, func, reads, writes, **kw):
        P.op("act", lambda e: e.activation(out=out, in_=in_, func=func, **kw), reads, writes)

    def tt(out, in0, in1, op, reads, writes, eng="dve"):
        P.op(eng, lambda e: e.tensor_tensor(out=out, in0=in0, in1=in1, op=op), reads, writes)

    def ts(out, in0, s1, s2, op0, op1, reads, writes, eng="dve"):
        if s2 is None:
            P.op(eng, lambda e: e.tensor_scalar(out=out, in0=in0, scalar1=s1, scalar2=None, op0=op0), reads, writes)
        else:
            P.op(eng, lambda e: e.tensor_scalar(out=out, in0=in0, scalar1=s1, scalar2=s2, op0=op0, op1=op1), reads, writes)

    def stt(out, in0, scalar, in1, op0, op1, reads, writes, eng="dve"):
        P.op(eng, lambda e: e.scalar_tensor_tensor(out=out, in0=in0, scalar=scalar, in1=in1, op0=op0, op1=op1), reads, writes)

    def red(out, in_, reads, writes, eng="dve"):
        P.op(eng, lambda e: e.tensor_reduce(out=out, in_=in_, axis=AX.X, op=ALU.add), reads, writes)

    def cpy(out, in_, reads, writes, eng="dve"):
        if eng == "act":
            act(out, in_, AF.Copy, reads, writes)
        else:
            P.op(eng, lambda e: e.tensor_copy(out=out, in_=in_), reads, writes)

    def memset(ap, val, writes, eng="dve"):
        P.op(eng, lambda e: e.memset(ap, val), (), writes)

    def mm(out, lhsT, rhs, start, stop, reads, writes, skip=False):
        if skip:
            P.op("pe", lambda e: e.matmul(out, lhsT=lhsT, rhs=rhs, start=start, stop=stop, skip_group_check=True), reads, writes)
        else:
            P.op("pe", lambda e: e.matmul(out, lhsT=lhsT, rhs=rhs, start=start, stop=stop), reads, writes)

    def transpose(out, in_, ident, reads, writes):
        P.op("pe", lambda e: e.transpose(out=out, in_=in_, identity=ident), reads, writes)

    def rsqrt_chain(out, in_, scale, lnbuf, lnB, reads, outB):
        act(lnbuf, in_, AF.Ln, list(reads) + [B_eps], [lnB], scale=scale, bias=epsb[0:lnbuf.shape[0], 0:1])
        act(out, lnbuf, AF.Exp, [lnB], [outB], scale=-0.5)

    cvt_i = [0]

    def convert(out, in_, reads, writes, engs=("dve", "act")):
        e = engs[cvt_i[0] % len(engs)]
        cvt_i[0] += 1
        cpy(out, in_, reads, writes, eng=e)

    ident_bf, B_ident = sbt(top, "ident_bf", [128, 128], BF16)
    ones_bf, B_ones = sbt(top, "ones_bf", [128, 128], BF16)
    rsw_bf, B_rsw = sbt(top, "rsw_bf", [128, 128], BF16)
    gm, B_gm = sbt(top, "gm", [128, L, 8])
    gl, B_gl = sbt(top, "gl", [128, L, 8])
    gf, B_gf = sbt(top, "gf", [128, 8])
    spool, B_spool = sbt(top, "spool", [128, L, 2])
    gvn, B_gvn = sbt(top, "gvn", [128, L, 2])
    GS, B_GS = sbt(top, "GS", [128, L, 128])
    lam, B_lam = sbt(top, "lam", [128, L])
    nlam, B_nlam = sbt(top, "nlam", [128, L])
    mod, B_mod = sbt(top, "mod", [128, L, 48, 2])
    gs1, B_gs1 = sbt(top, "gs1", [128, L, 8, 2])
    gs2, B_gs2 = sbt(top, "gs2", [128, L, 8, 2])
    epsb, B_eps = sbt(top, "epsb", [128, 1])

    with ExitStack() as st:
        stg, B_stg = sbt(st, "pro_stg", [128, 2, 128])
        P.dma(stg[:, 0, :], ident_in, writes=[B_stg])
        P.dma(stg[:, 1, :], rsw_in, writes=[B_stg])
        cpy(ident_bf[:], stg[:, 0, :], [B_stg], [B_ident])
        cpy(rsw_bf[:], stg[:, 1, :], [B_stg], [B_rsw])
        memset(ones_bf[:], 1.0, [B_ones])
        memset(epsb[:], EPS, [B_eps])
        P.dma(gm[:], gm_in, writes=[B_gm])
        P.dma(gl[:], gl_in, writes=[B_gl])
        P.dma(gf[:], gf_in, writes=[B_gf])
        P.dma(spool[:], spool_in, writes=[B_spool])
        P.dma(gvn[:], gvn_in, writes=[B_gvn])
        bada, B_bada = sbt(st, "bada", [128, L, 48])
        P.dma(bada[:], bada_in, writes=[B_bada])
        gsub, B_gsub = sbt(st, "gsub", [128, L, 128])
        P.dma(gsub[:], gsub_in, writes=[B_gsub])
        lamv, B_lamv = sbt(st, "lamv", [128, L, 4, 64])
        P.dma(lamv[:], lamv_in, writes=[B_lamv])
        cbt, B_cb = sbt(st, "cbt", [128, 2, D])
        P.dma(cbt[:], cb_in, writes=[B_cb])
        scb, B_scb = sbt(st, "scb", [128, 2, D])
        act(scb[:], cbt[:], AF.Silu, [B_cb], [B_scb])
        lprod, B_lprod = sbt(st, "lprod", [128, L, 2, 64])
        lsum, B_lsum = sbt(st, "lsum", [128, L, 2])
        lexp, B_lexp = sbt(st, "lexp", [128, L, 2])
        for l in range(L):
            for j in range(2):
                tt(lprod[:, l, j, :], lamv[:, l, 2 * j, :], lamv[:, l, 2 * j + 1, :], ALU.mult, [B_lamv], [B_lprod])
        red(lsum[:], lprod[:], [B_lprod], [B_lsum])
        act(lexp[:], lsum[:], AF.Exp, [B_lsum], [B_lexp])
        for l in range(L):
            tt(lam[:, l:l + 1], lexp[:, l, 0:1], lexp[:, l, 1:2], ALU.subtract, [B_lexp], [B_lam])
            ts(lam[:, l:l + 1], lam[:, l:l + 1], float(lam_init(l)), None, ALU.add, None, [B_lam], [B_lam])
            ts(GS[:, l, :], gsub[:, l, :], float(1.0 - lam_init(l)), None, ALU.mult, None, [B_gsub], [B_GS])
        ts(nlam[:], lam[:], -1.0, None, ALU.mult, None, [B_lam], [B_nlam])
        wst = [sbt(st, f"wst{i}", [128, D]) for i in range(4)]
        wrot = Rot(wst)
        prod, B_prod = sbt(st, "prod", [128, 2, D])
        P.dma(SCB, scb[:], reads=[B_scb])

        def mod_chunk(l, j, scb_, B_scb_, wrot_, prod_, B_prod_, redt_, B_redt_):
            w_t, w_b = wrot_.next()
            P.dma(w_t[:], wadaT_in[l, :, j, :], writes=[w_b], eng="pool")
            if prod_.shape[1] == 2:
                tt(prod_[:], scb_[:], w_t[:].unsqueeze(1).to_broadcast([128, 2, D]), ALU.mult, [B_scb_, w_b], [B_prod_])
                red(redt_[:, j, :], prod_[:], [B_prod_], [B_redt_])
            else:
                for c_ in range(2):
                    tt(prod_[:, 0, :], scb_[:, c_, :], w_t[:], ALU.mult, [B_scb_, w_b], [B_prod_])
                    red(redt_[:, j, c_:c_ + 1], prod_[:, 0, :], [B_prod_], [B_redt_])

        def mod_finish(l, redt_, B_redt_, bada_, B_bada_):
            tt(mod[:, l], redt_[:], bada_[:, l, :].unsqueeze(2).to_broadcast([128, 48, 2]), ALU.add,
               [B_redt_, B_bada_], [B_mod])
            stt(gs1[:, l], mod[:, l, 8:16, :], 1.0, gm[:, l, :].unsqueeze(2).to_broadcast([128, 8, 2]), ALU.add, ALU.mult,
                [B_mod, B_gm], [B_gs1])
            stt(gs2[:, l], mod[:, l, 32:40, :], 1.0, gl[:, l, :].unsqueeze(2).to_broadcast([128, 8, 2]), ALU.add, ALU.mult,
                [B_mod, B_gl], [B_gs2])

        redt0, B_redt0 = sbt(st, "redt0", [128, 48, 2])
        memset(redt0[:], 0.0, [B_redt0])
        prodB = sbt(st, "prodB", [128, 2, D])
        prods = Rot([(prod, B_prod), prodB])
        junk, B_junk = sbt(st, "junk", [128, D])
        for j in range(48):
            w_t, w_b = wrot.next()
            P.dma(w_t[:], wadaT_in[0, :, j, :], writes=[w_b], eng="pool")
            pr_, B_pr_ = prods.next()
            tt(pr_[:], scb[:], w_t[:].unsqueeze(1).to_broadcast([128, 2, D]), ALU.mult, [B_scb, w_b], [B_pr_])
            for c_ in range(2):
                act(junk[:], pr_[:, c_, :], AF.Copy, [B_pr_], [B_junk, B_redt0], accum_out=redt0[:, j, c_:c_ + 1])
        mod_finish(0, redt0, B_redt0, bada, B_bada)
        P.barrier()
        if dbg:
            print('ops@pro', Prog.count)

    def x_src(l, tok0, nt):
        if l == 0:
            if tok0 < T:
                return xT_in.rearrange("(k p) t -> p k t", p=128)[:, :, tok0:tok0 + nt]
            return ctxT_in.rearrange("(k p) t -> p k t", p=128)[:, :, tok0 - T:tok0 - T + nt]
        return XT.rearrange("k p t -> p k t")[:, :, tok0:tok0 + nt]

    def x_dst(tok0, nt):
        return XT.rearrange("k p t -> p k t")[:, :, tok0:tok0 + nt]

    def norm_mod(st_bufs, xs, B_xs, nt, gsv, shv, hb, B_hb, pss, B_pss, do_square=True):
        sq, B_sq, lnb, B_ln, rstd, B_rstd, tmps = st_bufs
        if do_square:
            act(sq[:, :, :nt], xs[:, :, :nt], AF.Square, [B_xs], [B_sq])
        for k in range(8):
            mm(pss[:, :nt], ones_bf[:], sq[:, k, :nt], k == 0, k == 7, [B_ones, B_sq], [B_pss])
        rsqrt_chain(rstd[:, :nt], pss[:, :nt], 1.0 / D, lnb[:, :nt], B_ln, [B_pss], B_rstd)
        for k in range(8):
            tm, B_tm = tmps.next()
            stt(tm[:, :nt], xs[:, k, :nt], gsv(k), rstd[:, :nt], ALU.mult, ALU.mult, [B_xs, B_rstd, B_gs1, B_gs2], [B_tm])
            act(hb[:, k, :nt], tm[:, :nt], AF.Identity, [B_tm, B_mod], [B_hb], bias=shv(k))

    tiles512 = [(i * 512, 512, 0) for i in range(8)] + [(T, TC, 1)]
    for l in range(n_layers):
        last = (l == L - 1)
        with (ExitStack() if l * 5 + 0 <= stop_step else _Skip()) as st:
            win, _ = sbt(st, "win", [128, 8, IN_W], BF16)
            Bwin = [Buf(f"win{i}") for i in range(9)]
            wstg = [sbt(st, f"wstg{i}", [128, 8, 256]) for i in range(3)]
            wsr = Rot(wstg)

            def bw(col0, width):
                return Bwin[col0 // 256:(col0 + width - 1) // 256 + 1]

            def load_win():
                for pc in range(9):
                    s_t, s_b = wsr.next()
                    P.dma(s_t[:], win_in[l].rearrange("(k p) n -> p k n", p=128)[:, :, pc * 256:(pc + 1) * 256], writes=[s_b],
                          eng="pool")
                    convert(win[:, :, pc * 256:(pc + 1) * 256], s_t[:], [s_b], [Bwin[pc]])
            wsf, B_wsf = sbt(st, "wsf", [128, 4, 128])
            wsb, B_wsb = sbt(st, "wsb", [128, 4, 128], BF16)
            P.dma(wsf[:], wsT_in[l], writes=[B_wsf])
            cpy(wsb[:], wsf[:], [B_wsf], [B_wsb])
            bst, B_bst = sbt(st, "bst", [128, 2, 512])
            P.dma(bst[:], bsT_in[l], writes=[B_bst])
            xs2 = [sbt(st, f"xs{i}", [128, 8, 512]) for i in range(2)]
            cs2 = [sbt(st, f"cos{i}", [128, 2, 512]) for i in range(2)]
            sq, B_sq = sbt(st, "sq", [128, 8, 512], BF16)
            hb, B_hb = sbt(st, "hb", [128, 8, 512], BF16)
            lnb, B_ln = sbt(st, "lnb", [128, 512])
            rstd, B_rstd = sbt(st, "rstd", [128, 512])
            tmps = Rot([sbt(st, f"tmp{i}", [128, 512]) for i in range(4)])
            qbs = Rot([sbt(st, f"qb{i}", [128, 512], BF16) for i in range(2)])
            r1s = Rot([sbt(st, f"r1{i}", [128, 512]) for i in range(2)])
            r2s = Rot([sbt(st, f"r2{i}", [128, 512]) for i in range(2)])
            qrs = Rot([sbt(st, f"qr{i}", [128, 512], BF16) for i in range(3)])
            vb, B_vb = sbt(st, "vb", [128, 4, 4, 130], BF16)
            memset(vb[:], 1.0, [B_vb])
            pf, B_pf = sbt(st, "pf", [128, 2, 512])
            gvfs = Rot([sbt(st, f"gvf{i}", [128, 256]) for i in range(2)])
            sqgs = Rot([sbt(st, f"sqg{i}", [128, 256]) for i in range(2)])
            ssgs = Rot([sbt(st, f"ssg{i}", [128, 4]) for i in range(2)])
            lngs = Rot([sbt(st, f"lng{i}", [128, 4]) for i in range(2)])
            rsgs = Rot([sbt(st, f"rsg{i}", [128, 4]) for i in range(2)])
            vns = Rot([sbt(st, f"vn{i}", [128, 4, 64], BF16) for i in range(4)])
            mbt, B_mbt = sbt(st, "mbt", [128, 512])
            mbb, B_mbb = sbt(st, "mbb", [128, 2, 512], BF16)
            pA = Rot([pst(st, f"pA{i}", [128, 512]) for i in range(4)])
            pR = Rot([pst(st, f"pR{i}", [128, 512]) for i in range(2)])
            usb, B_usb = sbt(st, "usb", [128, 2, 512])
            pVM, B_pVM = pst(st, "pVM", [128, 2, 512])
            evi = [0]
            if dbg:
                print('  mark tiles', Prog.count)
            hbB = sbt(st, "hbB", [128, 8, 512], BF16)
            hbs = [(hb, B_hb), hbB]

            def p1_load(ti):
                tok0, nt, cond = tiles512[ti]
                xs, B_xs = xs2[ti % 2]
                cs, B_cs = cs2[ti % 2]
                P.dma(xs[:, :, :nt], x_src(l, tok0, nt), writes=[B_xs])
                if cond == 0:
                    P.dma(cs[:, 0, :], cosT_in[:, tok0:tok0 + nt], writes=[B_cs])
                    P.dma(cs[:, 1, :], sinT_in[:, tok0:tok0 + nt], writes=[B_cs])

            def p1_square(ti, part=None):
                tok0, nt, cond = tiles512[ti]
                xs, B_xs = xs2[ti % 2]
                ks = slice(0, 8) if part is None else slice(2 * part, 2 * part + 2)
                act(sq[:, ks, :nt], xs[:, ks, :nt], AF.Square, [B_xs], [B_sq])

            def p1_norm(ti):
                tok0, nt, cond = tiles512[ti]
                xs, B_xs = xs2[ti % 2]
                hb_, B_hb_ = hbs[ti % 2]
                pss, B_pss = pA.next()
                norm_mod((sq, B_sq, lnb, B_ln, rstd, B_rstd, tmps), xs, B_xs, nt,
                         lambda k: gs1[:, l, k, cond:cond + 1], lambda k: mod[:, l, k, cond:cond + 1],
                         hb_, B_hb_, pss, B_pss, do_square=False)

            p1_load(0)
            p1_square(0)
            p1_norm(0)
            load_win()
            for ti, (tok0, nt, cond) in enumerate(tiles512):
                xs, B_xs = xs2[ti % 2]
                cs, B_cs = cs2[ti % 2]
                hb, B_hb = hbs[ti % 2]
                if ti + 1 < len(tiles512):
                    p1_load(ti + 1)
                if dbg and ti == 0:
                    print('  mark qk', Prog.count)
                full = not (last and cond == 1)
                ns = nt // 128
                prev = [None]

                def rope_tail():
                    if prev[0] is None:
                        return
                    (dst_ap, qb, B_qb, r1, B_r1) = prev[0]
                    prev[0] = None
                    r2, B_r2 = r2s.next()
                    pr, B_pr = pR.next()
                    qr, B_qr = qrs.next()
                    mm(pr[:, :nt], rsw_bf[:], qb[:, :nt], True, True, [B_rsw, B_qb], [B_pr])
                    tt(r2[:, :nt], pr[:, :nt], cs[:, 1, :nt], ALU.mult, [B_pr, B_cs], [B_r2])
                    tt(qr[:, :nt], r1[:, :nt], r2[:, :nt], ALU.add, [B_r1, B_r2], [B_qr])
                    P.dma(dst_ap, qr[:, :nt], reads=[B_qr])

                for qk in range(2):
                    if last and cond == 1 and qk == 0:
                        continue
                    dst = QT if qk == 0 else KT
                    for c in range(4):
                        col0 = qk * 512 + c * 128
                        ps, B_ps = pA.next()
                        for k in range(8):
                            mm(ps[:, :nt], win[:, k, col0:col0 + 128], hb[:, k, :nt], k == 0, k == 7, bw(col0, 128) + [B_hb], [B_ps])
                        if cond == 0:
                            qb, B_qb = qbs.next()
                            r1, B_r1 = r1s.next()
                            act(qb[:, :nt], ps[:, :nt], AF.Copy, [B_ps], [B_qb])
                            tt(r1[:, :nt], ps[:, :nt], cs[:, 0, :nt], ALU.mult, [B_ps, B_cs], [B_r1])
                            rope_tail()
                            prev[0] = (dst[c, :, tok0:tok0 + nt], qb, B_qb, r1, B_r1)
                        else:
                            qr, B_qr = qrs.next()
                            act(qr[:, :nt], ps[:, :nt], AF.Copy, [B_ps], [B_qr])
                            P.dma(dst[c, :, tok0:tok0 + nt], qr[:, :nt], reads=[B_qr])
                        if qk == 1 and ti + 1 < len(tiles512):
                            p1_square(ti + 1, part=c)
                if ti + 1 < len(tiles512):
                    p1_norm(ti + 1)
                rope_tail()
                for s in range(ns):
                    ps, B_ps = pA.next()
                    for k in range(8):
                        mm(ps[:, :], hb[:, k, s * 128:(s + 1) * 128], win[:, k, 1024:1536], k == 0, k == 7, bw(1024, 512) + [B_hb], [B_ps])
                    evi[0] += 1
                    cpy(vb[:, s, :, 0:128], ps[:, :].rearrange("p (h e) -> p h e", h=4), [B_ps], [B_vb],
                        eng=("act" if evi[0] % 2 else "dve"))
                P.dma(VV[tok0:tok0 + nt].rearrange("(s p) h e -> p s h e", p=128), vb[:, :ns], reads=[B_vb])
                vm_todo = []
                if full:
                    for s in range(ns):
                        ps, B_ps = pA.next()
                        for k in range(8):
                            mm(ps[:, 0:256], hb[:, k, s * 128:(s + 1) * 128], win[:, k, 1792:2048], k == 0, k == 7, bw(1792, 256) + [B_hb], [B_ps])
                        gvf, B_gvf = gvfs.next()
                        sqg, B_sqg = sqgs.next()
                        ssg, B_ssg = ssgs.next()
                        lng, B_lng = lngs.next()
                        rsg, B_rsg = rsgs.next()
                        act(gvf[:], ps[:, 0:256], AF.Copy, [B_ps], [B_gvf])
                        tt(sqg[:], gvf[:], gvf[:], ALU.mult, [B_gvf], [B_sqg])
                        red(ssg[:], sqg[:].rearrange("p (g c) -> p g c", g=4), [B_sqg], [B_ssg])
                        rsqrt_chain(rsg[:], ssg[:], 1.0 / 64, lng[:], B_lng, [B_ssg], B_rsg)
                        vn, B_vn = vns.next()
                        tt(vn[:], gvf[:].rearrange("p (g c) -> p g c", g=4), rsg[:].unsqueeze(2).to_broadcast([128, 4, 64]),
                           ALU.mult, [B_gvf, B_rsg], [B_vn])
                        vm_todo.append((s, vn, B_vn))
                if not full:
                    continue
                for c2 in range(2):
                    col0 = 1536 + c2 * 128
                    ps, B_ps = pA.next()
                    for k in range(8):
                        mm(ps[:, :nt], win[:, k, col0:col0 + 128], hb[:, k, :nt], k == 0, k == 7, bw(col0, 128) + [B_hb], [B_ps])
                    act(usb[:, c2, :nt], ps[:, :nt], AF.Copy, [B_ps], [B_usb])
                for c2 in range(2):
                    col0 = 2048 + c2 * 128
                    ps, B_ps = pA.next()
                    for k in range(8):
                        mm(ps[:, :nt], win[:, k, col0:col0 + 128], hb[:, k, :nt], k == 0, k == 7, bw(col0, 128) + [B_hb], [B_ps])
                    act(pf[:, c2, :nt], ps[:, :nt], AF.Copy, [B_ps], [B_pf])
                P.dma(PT.rearrange("c p t -> p c t")[:, :, tok0:tok0 + nt], pf[:, :, :nt], reads=[B_pf])
                for (s, vn, B_vn) in vm_todo:
                    for g in range(4):
                        gl_, c2 = g % 2, g // 2
                        mm(pVM[gl_ * 64:(gl_ + 1) * 64, c2, s * 128:(s + 1) * 128], vn[:, g, :], wsb[:, g, :], True, True,
                           [B_vn, B_wsb], [B_pVM])
                for c2 in range(2):
                    stt(mbt[:, :nt], pVM[:, c2, :nt], gvn[:, l, c2:c2 + 1], bst[:, c2, :nt], ALU.mult, ALU.add,
                        [B_pVM, B_gvn, B_bst], [B_mbt])
                    tt(mbb[:, c2, :nt], mbt[:, :nt], usb[:, c2, :nt], ALU.mult, [B_mbt, B_usb], [B_mbb])
                P.dma(MIX[4:6].rearrange("c p t -> p c t")[:, :, tok0:tok0 + nt], mbb[:, :, :nt], reads=[B_mbb])
            P.barrier()
            if dbg:
                print('ops@', l, Prog.count)

        with (ExitStack() if l * 5 + 1 <= stop_step else _Skip()) as st:
            xpA = sbt(st, "xp", [128, PADL])
            xpB = sbt(st, "xpB", [128, PADL])
            icB = sbt(st, "icB", [128, PADL])
            a2, B_a2 = sbt(st, "a2", [128, PADL])
            s4, B_s4 = sbt(st, "s4", [128, PADL])
            s8, B_s8 = sbt(st, "s8", [128, PADL])
            s16, B_s16 = sbt(st, "s16", [128, PADL])
            ic, B_ic = sbt(st, "ic", [128, PADL])
            dt_, B_dt = sbt(st, "dt", [128, PADL])
            dbf, B_dbf = sbt(st, "dbf", [128, PADL], BF16)
            wpf, B_wpf = sbt(st, "wpf", [128, 2, 128])
            wpb, B_wpb = sbt(st, "wpb", [128, 2, 128], BF16)
            mcs = Rot([sbt(st, f"mc{i}", [128, 512], BF16) for i in range(2)])
            pA = Rot([pst(st, f"pbA{i}", [128, 512]) for i in range(2)])
            memset(wpf[:], 0.0, [B_wpf])
            for g in range(4):
                gl_, c2 = g % 2, g // 2
                P.dma(wpf[gl_ * 64:(gl_ + 1) * 64, c2, gl_ * 64:(gl_ + 1) * 64], wpool_in[l, g], writes=[B_wpf])
            cpy(wpb[:], wpf[:], [B_wpf], [B_wpb])
            Lp = PADL
            icA = (ic, B_ic)
            for c2 in range(2):
                xp, B_xp = (xpA, xpB)[c2]
                ic_, B_ic_ = (icA, icB)[c2]
                memset(xp[:, 0:8], 0.0, [B_xp])
                memset(xp[:, 4104:4120], 0.0, [B_xp])
                memset(xp[:, 4376:4384], 0.0, [B_xp])
                P.dma(xp[:, 8:8 + T], PT[c2, :, 0:T], writes=[B_xp], eng=("sp" if c2 == 0 else "pool"))
                P.dma(xp[:, 4120:4120 + TC], PT[c2, :, T:TT], writes=[B_xp], eng=("sp" if c2 == 0 else "pool"))
                P.dma(ic_[:], ic_in[:, c2, :], writes=[B_ic_], eng=("sp" if c2 == 0 else "pool"))
            for c2 in range(2):
                xp, B_xp = (xpA, xpB)[c2]
                ic, B_ic = (icA, icB)[c2]
                tt(a2[:, 1:Lp], xp[:, 0:Lp - 1], xp[:, 1:Lp], ALU.add, [B_xp], [B_a2])
                if c2 == 0:
                    tt(s4[64:128, 2:Lp - 1], a2[64:128, 1:Lp - 2], a2[64:128, 3:Lp], ALU.add, [B_a2], [B_s4])
                    srcs = [(a2, B_a2), (s4, B_s4)]
                else:
                    tt(s4[:, 2:Lp - 1], a2[:, 1:Lp - 2], a2[:, 3:Lp], ALU.add, [B_a2], [B_s4])
                    tt(s8[:, 4:Lp - 3], s4[:, 2:Lp - 5], s4[:, 6:Lp - 1], ALU.add, [B_s4], [B_s8])
                    tt(s16[64:128, 8:Lp - 7], s8[64:128, 4:Lp - 11], s8[64:128, 12:Lp - 3], ALU.add, [B_s8], [B_s16])
                    srcs = [(s8, B_s8), (s16, B_s16)]
                for gl_ in range(2):
                    s_t, s_b = srcs[gl_]
                    pr_ = slice(gl_ * 64, (gl_ + 1) * 64)
                    tt(dt_[pr_, 8:Lp - 8], s_t[pr_, 8:Lp - 8], ic[pr_, 8:Lp - 8], ALU.mult, [s_b, B_ic], [B_dt])
                tt(dbf[:, 8:Lp - 8], dt_[:, 8:Lp - 8], xp[:, 8:Lp - 8], ALU.subtract, [B_dt, B_xp], [B_dbf])
                for ti, (tok0, nt, cond) in enumerate(tiles512):
                    if last and cond == 1:
                        continue
                    col = 8 + tok0 if cond == 0 else 4120 + (tok0 - T)
                    ps, B_ps = pA.next()
                    mm(ps[:, :nt], wpb[:, c2, :], dbf[:, col:col + nt], True, True, [B_wpb, B_dbf], [B_ps])
                    mc, B_mc = mcs.next()
                    act(mc[:, :nt], ps[:, :nt], AF.Identity, [B_ps, B_spool], [B_mc], scale=spool[:, l, c2:c2 + 1])
                    P.dma(MIX[6 + c2, :, tok0:tok0 + nt], mc[:, :nt], reads=[B_mc])
            P.barrier()
            if dbg:
                print('ops@', l, Prog.count)

        sawm = ExitStack()
        w1b, B_w1 = sbt(sawm, "w1b", [128, 8, 4 * D], BF16)
        w1_pieces = [(k, nq) for k in range(8) for nq in range(4)]

        wout, B_wout = sbt(sawm, "wout", [128, 8, D], BF16)
        wout_pieces = list(range(8))

        def wout_piece(stg_rot):
            pc = wout_pieces.pop(0)
            s_t, s_b = stg_rot.next()
            P.dma(s_t[:].rearrange("p (k n) -> p k n", k=8),
                  wout_in[l].rearrange("(k p) n -> p k n", p=128)[:, :, pc * 128:(pc + 1) * 128], writes=[s_b], eng="pool")
            cpy(wout[:, :, pc * 128:(pc + 1) * 128], s_t[:].rearrange("p (k n) -> p k n", k=8), [s_b], [B_wout], eng="dve")

        def w1_piece(stg_rot):
            if wout_pieces:
                wout_piece(stg_rot)
                return
            k, nq = w1_pieces.pop(0)
            s_t, s_b = stg_rot.next()
            P.dma(s_t[:], w1_in[l, k * 128:(k + 1) * 128, nq * 1024:(nq + 1) * 1024], writes=[s_b], eng="pool")
            cpy(w1b[:, k, nq * 1024:(nq + 1) * 1024], s_t[:], [s_b], [B_w1], eng="dve")

        with (ExitStack() if l * 5 + 2 <= stop_step else _Skip()) as st:
            stgA = Rot([sbt(st, f"stgA{i}", [128, 1024]) for i in range(2)])
            kt_sb, B_kt = sbt(st, "kt_sb", [128, 4, TT], BF16)
            v_sb, B_v = sbt(st, "v_sb", [128, NKT, 4, 130], BF16)
            for h in range(4):
                P.dma(kt_sb[:, h, :], KT[h], writes=[Buf()], key=B_kt, eng=("sp" if h % 2 == 0 else "pool"))
            for g4 in range(0, NKT, 6):
                n4 = min(6, NKT - g4)
                P.dma(v_sb[:, g4:g4 + n4], VV[g4 * 128:(g4 + n4) * 128].rearrange("(k p) h e -> p k h e", p=128),
                      writes=[Buf()], key=B_v, eng=("pool" if (g4 // 6) % 2 == 0 else "sp"))
            P.barrier()
            qbl = [sbt(st, f"qbl{i}", [128, 512], BF16) for i in range(2)]
            Es = [sbt(st, f"E{i}", [128, 2, 512], BF16) for i in range(3)]
            sps = [pst(st, f"sps{i}", [128, 2, 512]) for i in range(2)]
            accb = [pst(st, "accA", [128, 512]), pst(st, "accB", [128, 512]), pst(st, "accC", [128, 512])]
            pT, B_pT = pst(st, "pT", [128, 4, 128], BF16)
            accS, B_accS = sbt(st, "accS", [128, 8, 130])
            rz8, B_rz = sbt(st, "rz8", [128, 8, 1])
            nrz4, B_nrz = sbt(st, "nrz4", [128, 4, 1])
            o04, B_o0 = sbt(st, "o04", [128, 4, 128])
            t14, B_t1 = sbt(st, "t14", [128, 4, 128])
            oo4, B_oo = sbt(st, "oo4", [128, 4, 128])
            osq4, B_osq = t14, B_t1
            oss4, B_oss = sbt(st, "oss4", [128, 4])
            oln4, B_oln = sbt(st, "oln4", [128, 4])
            ors4, B_ors = sbt(st, "ors4", [128, 4, 1])
            abt, B_abt = o04, B_o0
            ab4s = [sbt(st, f"ab4{i}", [128, 4, 128], BF16) for i in range(2)]
            aTs = Rot([sbt(st, f"aT{i}", [128, 512], BF16) for i in range(2)])
            do_mod = (l + 1 < n_layers)
            if do_mod:
                scb2, B_scb2 = sbt(st, "scb2", [128, 2, D])
                bada2, B_bada2 = sbt(st, "bada2", [128, L, 48])
                P.dma(scb2[:], SCB, writes=[B_scb2])
                P.dma(bada2[:], bada_in, writes=[B_bada2])
                wrot2 = Rot([sbt(st, f"wst2{i}", [128, D]) for i in range(2)])
                prod2, B_prod2 = sbt(st, "prod2", [128, 1, D])
                redt2, B_redt2 = sbt(st, "redt2", [128, 48, 2])
            mod_j = [0]
            pending = []
            blocks = []
            for h in range(4):
                for qb_i in range(8):
                    blocks.append((h, qb_i * 512, 512, list(range(NKT))))
                if not last:
                    blocks.append((h, T, TC, [32, 33]))
            for bi, (h, q0, qn, kts) in enumerate(blocks):
                qt_, B_q = qbl[bi % 2]
                P.dma(qt_[:, :qn], QT[h, :, q0:q0 + qn], writes=[B_q])
                nqs = qn // 128

                def qk(i, kt):
                    sp, B_sp = sps[i % 2]
                    for j in range(2):
                        pr_ = slice(j * 64, (j + 1) * 64)
                        mm(sp[:, j, :qn], kt_sb[pr_, h, kt * 128:(kt + 1) * 128], qt_[pr_, :qn], True, True, [B_kt, B_q], [B_sp])
                    E, B_E = Es[i % 3]
                    act(E[:, :, :qn], sp[:, :, :qn], AF.Exp, [B_sp], [B_E], scale=0.125)

                def pv(i, kt):
                    E, B_E = Es[i % 3]
                    banks_started = set()
                    for j in range(2):
                        for qs in range(nqs):
                            idx = j * 4 + qs
                            bk, slot = idx // 3, idx % 3
                            a_t, a_b = accb[bk]
                            first_in_bank = (i == 0 and bk not in banks_started)
                            banks_started.add(bk)
                            mm(a_t[:, slot * 130:slot * 130 + 129], E[:, j, qs * 128:(qs + 1) * 128], v_sb[:, kt, h, 0:129],
                               first_in_bank, i == len(kts) - 1, [B_E, B_v], [a_b], skip=True)

                qk(0, kts[0])
                if len(kts) > 1:
                    qk(1, kts[1])
                for i, kt in enumerate(kts):
                    if i + 2 < len(kts):
                        qk(i + 2, kts[i + 2])
                    pv(i, kt)
                    while pending and pending[0][0] <= i:
                        pending.pop(0)[1]()
                while pending:
                    pending.pop(0)[1]()
                used = sorted(set((j * 4 + qs) // 3 for j in range(2) for qs in range(nqs)))
                for bk in used:
                    a_t, a_b = accb[bk]
                    nsl = 3 if bk < 2 else 2
                    cpy(accS[:, bk * 3:bk * 3 + nsl, :], a_t[:, 0:nsl * 130].rearrange("p (s c) -> p s c", c=130), [a_b], [B_accS],
                        eng="dve")
                P.op("dve", lambda e: e.reciprocal(out=rz8[:], in_=accS[:, :, 128:129]), [B_accS], [B_rz])
                ts(nrz4[:, :nqs], rz8[:, 4:4 + nqs], nlam[:, l:l + 1], None, ALU.mult, None, [B_rz, B_nlam], [B_nrz])
                tt(o04[:, :nqs], accS[:, 0:nqs, 0:128], rz8[:, 0:nqs].to_broadcast([128, nqs, 128]), ALU.mult, [B_accS, B_rz], [B_o0])
                tt(t14[:, :nqs], accS[:, 4:4 + nqs, 0:128], nrz4[:, :nqs].to_broadcast([128, nqs, 128]), ALU.mult,
                   [B_accS, B_nrz], [B_t1])
                tt(oo4[:, :nqs], o04[:, :nqs], t14[:, :nqs], ALU.add, [B_o0, B_t1], [B_oo])
                tt(osq4[:, :nqs], oo4[:, :nqs], oo4[:, :nqs], ALU.mult, [B_oo], [B_osq])
                red(oss4[:, :nqs], osq4[:, :nqs], [B_osq], [B_oss])
                ab4, B_ab = ab4s[bi % 2]
                aT, B_aT = aTs.next()

                def stage2(nqs=nqs):
                    rsqrt_chain(ors4[:, :nqs, 0], oss4[:, :nqs], 1.0 / 128, oln4[:, :nqs], B_oln, [B_oss], B_ors)

                def stage3(nqs=nqs, ab4=ab4, B_ab=B_ab):
                    tt(abt[:, :nqs], oo4[:, :nqs], ors4[:, :nqs].to_broadcast([128, nqs, 128]), ALU.mult, [B_oo, B_ors], [B_abt])
                    tt(ab4[:, :nqs], abt[:, :nqs], GS[:, l, :].unsqueeze(1).to_broadcast([128, nqs, 128]), ALU.mult,
                       [B_abt, B_GS], [B_ab])

                def stage4(nqs=nqs, ab4=ab4, B_ab=B_ab):
                    for qs in range(nqs):
                        transpose(pT[:, qs, :], ab4[:, qs, :], ident_bf[:], [B_ab, B_ident], [B_pT])

                def stage5(nqs=nqs, qn=qn, h=h, q0=q0, aT=aT, B_aT=B_aT):
                    cpy(aT[:, :qn], pT[:, 0:nqs, :].rearrange("p a b -> p (a b)"), [B_pT], [B_aT], eng="dve")
                    P.dma(MIX[h, :, q0:q0 + qn], aT[:, :qn], reads=[B_aT])

                pending = [(5, stage2), (10, stage3), (24, stage4), (29, stage5)]
                near_ctx = (qn != 512) or (bi + 1 < len(blocks) and blocks[bi + 1][2] != 512)
                if near_ctx:
                    continue
                if w1_pieces:
                    w1_piece(stgA)
                    if bi % 4 == 0 and w1_pieces:
                        w1_piece(stgA)
                if do_mod:
                    for _ in range(2 if mod_j[0] >= 2 * bi else 3):
                        if mod_j[0] < 48:
                            mod_chunk(l + 1, mod_j[0], scb2, B_scb2, wrot2, prod2, B_prod2, redt2, B_redt2)
                            mod_j[0] += 1
            while pending:
                pending.pop(0)[1]()
            while w1_pieces or wout_pieces:
                w1_piece(stgA)
            if do_mod:
                while mod_j[0] < 48:
                    mod_chunk(l + 1, mod_j[0], scb2, B_scb2, wrot2, prod2, B_prod2, redt2, B_redt2)
                    mod_j[0] += 1
                mod_finish(l + 1, redt2, B_redt2, bada2, B_bada2)
            P.barrier()
            if dbg:
                print('ops@', l, Prog.count)

        tiles256 = [(i * 256, 256, 0) for i in range(16)] + ([] if last else [(T, TC, 1)])
        with (ExitStack() if l * 5 + 3 <= stop_step else _Skip()) as st:
            w2b, B_w2 = sbt(st, "w2b", [128, 32, D], BF16)
            wstg = Rot([sbt(st, f"wstg{i}", [128, 1024]) for i in range(2)])
            xs2 = [sbt(st, f"wxs{i}", [128, 8, 256]) for i in range(2)]
            bfA = sbt(st, "bfA", [128, 8, 256], BF16)
            bfB = sbt(st, "bfB", [128, 8, 256], BF16)
            mx2 = [bfA, bfB]
            lnb, B_ln = sbt(st, "mlnb", [128, 256])
            rstd, B_rstd = sbt(st, "mrstd", [128, 256])
            tmps = Rot([sbt(st, f"mtmp{i}", [128, 256]) for i in range(2)])
            hid, B_hid = sbt(st, "hid", [128, 32, 256], BF16)
            r32s = Rot([sbt(st, f"r32{i}", [128, 256]) for i in range(2)])
            pA = Rot([pst(st, f"pmA{i}", [128, 512]) for i in range(3)])
            pB = Rot([pst(st, f"pmB{i}", [128, 512]) for i in range(2)])
            XTb = [Buf(f"XTb{i}") for i in range(len(tiles256))]
            w2_f = [0]

            def w2_piece():
                f = w2_f[0]
                w2_f[0] += 1
                s_t, s_b = wstg.next()
                P.dma(s_t[:], w2_in[l, f * 128:(f + 1) * 128, :], writes=[s_b], eng="pool")
                convert(w2b[:, f, :], s_t[:], [s_b], [B_w2], engs=("act", "dve"))

            sq, B_sq = bfA
            bfC = sbt(st, "bfC", [128, 8, 256], BF16)
            h2s = [bfB, bfC]

            def m_norm(ti):
                tok0, nt, cond = tiles256[ti]
                xn, B_xn = xs2[ti % 2]
                h2, B_h2 = h2s[ti % 2]
                mx, B_mx = bfA
                for co in range(8):
                    ps, B_ps = pA.next()
                    for k in range(8):
                        mm(ps[:, :nt], wout[:, k, co * 128:(co + 1) * 128], mx[:, k, :], k == 0, k == 7, [B_wout, B_mx], [B_ps])
                    stt(xn[:, co, :], ps[:, :nt], mod[:, l, 16 + co, cond:cond + 1], xn[:, co, :], ALU.mult, ALU.add,
                        [B_ps, B_mod, B_xn], [B_xn])
                pss, B_pss = pA.next()
                norm_mod((sq, B_sq, lnb, B_ln, rstd, B_rstd, tmps), xn, B_xn, nt,
                         lambda k: gs2[:, l, k, cond:cond + 1], lambda k: mod[:, l, 24 + k, cond:cond + 1],
                         h2, B_h2, pss, B_pss)

            def m_load(ti):
                tok0, nt, cond = tiles256[ti]
                xn, B_xn = xs2[ti % 2]
                mx, B_mx = bfA
                P.dma(xn[:], x_src(l, tok0, nt), writes=[B_xn])
                P.dma(mx[:], MIX.rearrange("c p t -> p c t")[:, :, tok0:tok0 + nt], writes=[B_mx])

            m_load(0)
            m_norm(0)
            m_load(1)
            m_norm(1)
            for ti, (tok0, nt, cond) in enumerate(tiles256):
                xn, B_xn = xs2[ti % 2]
                h2, B_h2 = h2s[ti % 2]
                if ti >= 1 and ti + 1 < len(tiles256):
                    m_load(ti + 1)
                for f in range(32):
                    ps, B_ps = pA.next()
                    for k in range(8):
                        mm(ps[:, :nt], w1b[:, k, f * 128:(f + 1) * 128], h2[:, k, :], k == 0, k == 7, [B_w1, B_h2], [B_ps])
                    r32, B_r32 = r32s.next()
                    act(r32[:], ps[:, :nt], AF.Relu, [B_ps], [B_r32])
                    tt(hid[:, f, :], r32[:], r32[:], ALU.mult, [B_r32], [B_hid])
                    if w2_f[0] < 32:
                        w2_piece()
                if ti >= 1 and ti + 1 < len(tiles256):
                    m_norm(ti + 1)
                for co in range(8):
                    ps, B_ps = pB.next()
                    for f in range(32):
                        mm(ps[:, :nt], w2b[:, f, co * 128:(co + 1) * 128], hid[:, f, :], f == 0, f == 31, [B_w2, B_hid], [B_ps])
                    stt(xn[:, co, :], ps[:, :nt], mod[:, l, 40 + co, cond:cond + 1], xn[:, co, :], ALU.mult, ALU.add,
                        [B_ps, B_mod, B_xn], [B_xn])
                if not last:
                    P.dma(x_dst(tok0, nt), xn[:], reads=[B_xn])
                else:
                    act(sq[:], xn[:], AF.Square, [B_xn], [B_sq])
                    pss, B_pss = pA.next()
                    for k in range(8):
                        mm(pss[:, :nt], ones_bf[:], sq[:, k, :], k == 0, k == 7, [B_ones, B_sq], [B_pss])
                    rsqrt_chain(rstd[:], pss[:, :nt], 1.0 / D, lnb[:], B_ln, [B_pss], B_rstd)
                    for k in range(8):
                        stt(xn[:, k, :], xn[:, k, :], gf[:, k:k + 1], rstd[:], ALU.mult, ALU.mult, [B_xn, B_gf, B_rstd], [B_xn])
                    P.dma(yT_out.rearrange("(k p) t -> p k t", p=128)[:, :, tok0:tok0 + nt], xn[:], reads=[B_xn], is_output=True)
            P.barrier()
            if dbg:
                print('ops@', l, Prog.count)
        sawm.close()

    P.emit(top)
    top.close()
    return nc


def _const_tables():
    grid_w = 64
    n_freq = 16
    t = np.arange(T)
    row = (t // grid_w).astype(np.float32)
    col = (t % grid_w).astype(np.float32)
    inv = (np.float32(10000.0) ** (-np.arange(n_freq, dtype=np.float32) / np.float32(n_freq))).astype(np.float32)
    cosT = np.zeros((128, T), np.float32)
    sinT = np.zeros((128, T), np.float32)
    rsw = np.zeros((128, 128), np.float32)
    for p in range(128):
        axis = (p % 64) // 32
        half = (p % 32) // 16
        f = p % 16
        ang = ((row if axis == 0 else col) * inv[f]).astype(np.float32)
        cosT[p] = np.cos(ang)
        sinT[p] = np.sin(ang) * (-1.0 if half == 0 else 1.0)
        partner = p + 16 if half == 0 else p - 16
        rsw[partner, p] = 1.0
    ic = np.zeros((128, 2, PADL), np.float32)
    wins = (2, 4, 8, 16)
    for g, w in enumerate(wins):
        gl_, c2 = g % 2, g // 2
        for (tseg, base) in ((T, 8), (TC, 4120)):
            pos = np.arange(tseg)
            lo = np.clip(pos - w // 2, 0, tseg)
            hi = np.clip(pos + (w - w // 2), 0, tseg)
            ic[gl_ * 64:(gl_ + 1) * 64, c2, base:base + tseg] = (1.0 / (hi - lo).astype(np.float32))[None, :]
    return cosT, sinT, rsw, np.eye(128, dtype=np.float32), ic


_NC_CACHE = {}


def _pvec(v, nch):
    v = np.asarray(v, np.float32)
    lead = v.shape[:-1]
    return np.ascontiguousarray(np.moveaxis(v.reshape(lead + (nch, 128)), -1, 0))


def kernel(x, c, ctx, c_ctx, w_ada, b_ada, g_norm_mix, g_norm_mlp, w_in, lam_q1, lam_k1, lam_q2, lam_k2,
           g_subln, g_vnorm, w_spatial, b_spatial, w_pool, s_pool, w_out, w1, w2, g_final):
    f = lambda a: np.ascontiguousarray(np.asarray(a, np.float32))
    x, c, ctx, c_ctx = f(x), f(c), f(ctx), f(c_ctx)
    n = 8
    if "nc" not in _NC_CACHE:
        _NC_CACHE["nc"] = build_program()
    nc = _NC_CACHE["nc"]
    cosT, sinT, rsw, ident, ic = _const_tables()
    w_ada = f(w_ada)
    wadaT = np.ascontiguousarray(w_ada.reshape(L, D, 48, 128).transpose(0, 3, 2, 1))
    b_sp = f(b_spatial)
    bsT = np.zeros((L, 128, 2, 512), np.float32)
    for g in range(4):
        gl_, c2 = g % 2, g // 2
        bsT[:, gl_ * 64:(gl_ + 1) * 64, c2, :] = np.tile(b_sp[:, g, :], (1, 4))[:, None, :]
    lamv = np.stack([f(lam_q1), f(lam_k1), f(lam_q2), f(lam_k2)], axis=1)
    shared = {
        "wadaT": wadaT,
        "bada": _pvec(f(b_ada), 48),
        "gm": _pvec(f(g_norm_mix), 8),
        "gl": _pvec(f(g_norm_mlp), 8),
        "gf": _pvec(f(g_final), 8),
        "w_in": f(w_in),
        "lamv": np.ascontiguousarray(np.broadcast_to(lamv[None], (128, L, 4, 64))),
        "gsub": np.ascontiguousarray(np.broadcast_to(f(g_subln)[None], (128, L, 128))),
        "gvn": _pvec(f(g_vnorm), 2),
        "wsT": np.ascontiguousarray(f(w_spatial).transpose(0, 3, 1, 2)),
        "bsT": bsT,
        "w_pool": f(w_pool),
        "spool": _pvec(f(s_pool), 2),
        "w_out": f(w_out),
        "w1": f(w1),
        "w2": f(w2),
        "cosT": cosT, "sinT": sinT, "rsw": rsw, "ident": ident, "icT": ic,
    }
    in_maps = []
    for b in range(n):
        m = dict(shared)
        m["xT"] = np.ascontiguousarray(x[b].T)
        m["ctxT"] = np.ascontiguousarray(ctx[b].T)
        cbv = np.stack([c[b], c_ctx], axis=0)
        m["cb"] = np.ascontiguousarray(np.broadcast_to(cbv[None], (128, 2, D)))
        in_maps.append(m)
    if _NC_CACHE.get("return_maps"):
        return in_maps
    res = run_bass_kernel_spmd(nc, in_maps, core_ids=list(range(n)))
    out = np.stack([np.ascontiguousarray(res.results[b]["yT"].T) for b in range(n)], axis=0)
    return out.astype(np.float32)
```
